# Optimizing a Trainium2 kernel written in Bass

```python
import functools
import jax, jax.numpy as jnp
from jax import lax
import numpy as np

D_MODEL = 1024
BATCH = 16
SEQ = 2048
DEPTH = 1
DEC_BATCH = 128
DEC_SEQ = 4
PAST_LEN = 16384
PAGE_SIZE = 128

HEAD_DIM = 64
ATTN_WIDTH = D_MODEL // 2
N_Q_HEADS = ATTN_WIDTH // HEAD_DIM
N_KV_HEADS = N_Q_HEADS // 4
KV_WIDTH = N_KV_HEADS * HEAD_DIM
GQA_GROUP = N_Q_HEADS // N_KV_HEADS
RWKV_WIDTH = D_MODEL - ATTN_WIDTH
N_RWKV_HEADS = RWKV_WIDTH // HEAD_DIM
WINDOW = 128
BLOCK = WINDOW
SCALE = HEAD_DIM ** -0.5
DECAY_LORA = 64
AAA_LORA = 64
GATE_LORA = 128
D_FF = 4 * D_MODEL
RMS_EPS = 1e-6
GN_EPS = 64e-5
Q_OFF = 0
K_OFF = Q_OFF + ATTN_WIDTH
V_OFF = K_OFF + KV_WIDTH
RWKV_OFF = V_OFF + KV_WIDTH
RWKV_PROJ = 3 * RWKV_WIDTH + DECAY_LORA + AAA_LORA + GATE_LORA
RWKV_SPLITS = (RWKV_WIDTH, RWKV_WIDTH + DECAY_LORA, 2 * RWKV_WIDTH + DECAY_LORA,
               3 * RWKV_WIDTH + DECAY_LORA, 3 * RWKV_WIDTH + DECAY_LORA + AAA_LORA)
IN_WIDTH = RWKV_OFF + RWKV_PROJ

kernel_name = 'hymba_swa_rwkv7_step'


def _rms_norm(x, g):
    x32 = x.astype(jnp.float32)
    y = x32 * lax.rsqrt(jnp.mean(x32 * x32, axis=-1, keepdims=True) + RMS_EPS)
    return (y * g.astype(jnp.float32)).astype(x.dtype)


def _sink_softmax(scores, mask, sinks):
    s = jnp.where(mask, scores.astype(jnp.float32), -jnp.inf)
    sink = sinks.astype(jnp.float32)[..., None, None]
    m = jnp.maximum(jnp.max(s, axis=-1, keepdims=True), sink)
    p = jnp.exp(s - m)
    return p / (jnp.sum(p, axis=-1, keepdims=True) + jnp.exp(sink - m))


def _swa_prompt(q, k, v, sinks):
    B, S = q.shape[0], q.shape[1]
    nb = S // BLOCK
    qb = q.reshape(B, nb, BLOCK, N_KV_HEADS, GQA_GROUP, HEAD_DIM)
    kb = k.reshape(B, nb, BLOCK, N_KV_HEADS, HEAD_DIM)
    vb = v.reshape(B, nb, BLOCK, N_KV_HEADS, HEAD_DIM)

    def with_prev(t):
        prev = jnp.concatenate([jnp.zeros_like(t[:, :1]), t[:, :-1]], axis=1)
        return jnp.concatenate([prev, t], axis=2)

    kc, vc = with_prev(kb), with_prev(vb)
    blk = jnp.arange(nb)[:, None] * BLOCK
    qpos = blk + jnp.arange(BLOCK)[None, :]
    kpos = blk + jnp.arange(2 * BLOCK)[None, :] - BLOCK
    rel = qpos[:, :, None] - kpos[:, None, :]
    mask = (rel >= 0) & (rel < WINDOW) & (kpos[:, None, :] >= 0)
    scores = jnp.einsum('bnqhgd,bnkhd->bnhgqk', qb, kc) * SCALE
    probs = _sink_softmax(scores, mask[None, :, None, None], sinks.reshape(N_KV_HEADS, GQA_GROUP))
    out = jnp.einsum('bnhgqk,bnkhd->bnqhgd', probs.astype(v.dtype), vc)
    return out.reshape(B, S, ATTN_WIDTH), k[:, S - WINDOW:], v[:, S - WINDOW:]


def _swa_cached(cache_k, cache_v, q, k, v, sinks):
    B, T = q.shape[0], q.shape[1]
    wc = cache_k.shape[1]
    kall = jnp.concatenate([cache_k.astype(k.dtype), k], axis=1)
    vall = jnp.concatenate([cache_v.astype(v.dtype), v], axis=1)
    rel = jnp.arange(T)[:, None] - (jnp.arange(wc + T)[None, :] - wc)
    mask = (rel >= 0) & (rel < WINDOW)
    qg = q.reshape(B, T, N_KV_HEADS, GQA_GROUP, HEAD_DIM)
    scores = jnp.einsum('bthgd,bkhd->bhgtk', qg, kall) * SCALE
    probs = _sink_softmax(scores, mask, sinks.reshape(N_KV_HEADS, GQA_GROUP))
    out = jnp.einsum('bhgtk,bkhd->bthgd', probs.astype(v.dtype), vall)
    return out.reshape(B, T, ATTN_WIDTH), kall[:, T:], vall[:, T:]


def _wkv7_scan(r, decay, k, v, a_vec, b_vec, s0):
    def step(s, inp):
        r_t, d_t, k_t, v_t, a_t, b_t = inp
        sa = jnp.einsum('bhij,bhj->bhi', s, a_t)
        s = s * d_t[:, :, None, :] + sa[..., None] * b_t[:, :, None, :] + v_t[..., None] * k_t[:, :, None, :]
        return s, jnp.einsum('bhij,bhj->bhi', s, r_t)

    xs = tuple(jnp.swapaxes(t, 0, 1) for t in (r, decay, k, v, a_vec, b_vec))
    s_final, ys = lax.scan(step, s0.astype(jnp.float32), xs)
    return jnp.swapaxes(ys, 0, 1), s_final


def _rwkv7_mix(p, p_prev, s0, lp):
    B, T = p.shape[0], p.shape[1]
    f = lambda t: t.astype(jnp.float32)
    p = f(p)
    p_shift = jnp.concatenate([f(p_prev)[:, None], p[:, :-1]], axis=1)
    pm = p + (p_shift - p) * f(lp['rwkv_mu'])
    xr, xw, xk, xv, xa, xg = jnp.split(pm, RWKV_SPLITS, axis=-1)
    w_log = -jax.nn.softplus(-(f(lp['w_decay_0']) + jnp.tanh(xw) @ f(lp['w_decay_up']))) - 0.5
    decay = jnp.exp(-jnp.exp(w_log))
    a = jax.nn.sigmoid(f(lp['a_0']) + xa @ f(lp['a_up']))
    g = jax.nn.sigmoid(xg) @ f(lp['g_up'])
    hs = lambda t: t.reshape(B, T, N_RWKV_HEADS, HEAD_DIM)
    kk = hs(xk * f(lp['k_k']))
    kk = kk / jnp.maximum(jnp.sqrt(jnp.sum(kk * kk, axis=-1, keepdims=True)), 1e-12)
    a_h = hs(a)
    k_h = hs(xk * (1.0 + (a - 1.0) * f(lp['k_a'])))
    r_h, v_h = hs(xr), hs(xv)
    y, s = _wkv7_scan(r_h, hs(decay), k_h, v_h, -kk, kk * a_h, s0)
    mu = jnp.mean(y, axis=-1, keepdims=True)
    var = jnp.mean(jnp.square(y - mu), axis=-1, keepdims=True)
    yn = ((y - mu) * lax.rsqrt(var + GN_EPS)).reshape(B, T, RWKV_WIDTH) * f(lp['ln_x_g']) + f(lp['ln_x_b'])
    bonus = jnp.sum(r_h * k_h * f(lp['r_k']), axis=-1, keepdims=True) * v_h
    out = (yn + bonus.reshape(B, T, RWKV_WIDTH)) * g
    return out, s


def _layer(x, attn_fn, shift_prev, s0, lp):
    B, T = x.shape[0], x.shape[1]
    h = _rms_norm(x, lp['norm1_g'])
    proj = h @ lp['w_in']
    q = proj[..., Q_OFF:K_OFF].reshape(B, T, N_Q_HEADS, HEAD_DIM)
    k = proj[..., K_OFF:V_OFF].reshape(B, T, N_KV_HEADS, HEAD_DIM)
    v = proj[..., V_OFF:RWKV_OFF].reshape(B, T, N_KV_HEADS, HEAD_DIM)
    q = _rms_norm(q, lp['q_norm_g'])
    k = _rms_norm(k, lp['k_norm_g'])
    attn_out, new_k, new_v = attn_fn(q, k, v, lp['attn_sinks'])
    p_prev = shift_prev.astype(h.dtype) @ lp['w_in'][:, RWKV_OFF:]
    rwkv_out, s_new = _rwkv7_mix(proj[..., RWKV_OFF:], p_prev, s0, lp)
    mix = jnp.concatenate([attn_out, rwkv_out.astype(x.dtype)], axis=-1)
    x = x + mix @ lp['w_out']
    h2 = _rms_norm(x, lp['norm2_g'])
    x = x + jnp.square(jax.nn.relu(h2 @ lp['w_ff_up'])) @ lp['w_ff_down']
    return x, new_k, new_v, s_new, h[:, -1]


def setup_inputs(seed: int = 0) -> dict:
    key = jax.random.key(seed)
    ks = jax.random.split(key, 32)
    f32 = jnp.float32
    nrm = lambda i, shape, s: jax.random.normal(ks[i], shape, f32) * s
    cache_rows = min(WINDOW, PAST_LEN)
    L = DEPTH
    return {
        'x_prompt': nrm(0, (BATCH, SEQ, D_MODEL), 1.0),
        'x_sample': nrm(1, (DEC_BATCH, DEC_SEQ, D_MODEL), 1.0),
        'cache_k': nrm(2, (L, DEC_BATCH, cache_rows, N_KV_HEADS, HEAD_DIM), 1.0),
        'cache_v': nrm(3, (L, DEC_BATCH, cache_rows, N_KV_HEADS, HEAD_DIM), 1.0),
        'state_wkv': nrm(4, (L, DEC_BATCH, N_RWKV_HEADS, HEAD_DIM, HEAD_DIM), 0.1),
        'state_shift': nrm(5, (L, DEC_BATCH, D_MODEL), 1.0),
        'norm1_g': 1.0 + nrm(6, (L, D_MODEL), 0.02),
        'w_in': nrm(7, (L, D_MODEL, IN_WIDTH), D_MODEL ** -0.5),
        'q_norm_g': 1.0 + nrm(8, (L, HEAD_DIM), 0.02),
        'k_norm_g': 1.0 + nrm(9, (L, HEAD_DIM), 0.02),
        'attn_sinks': nrm(10, (L, N_Q_HEADS), 0.5),
        'rwkv_mu': jax.random.uniform(ks[11], (L, RWKV_PROJ), f32),
        'w_decay_0': jax.random.uniform(ks[12], (L, RWKV_WIDTH), f32, -4.0, 1.0),
        'w_decay_up': nrm(13, (L, DECAY_LORA, RWKV_WIDTH), 0.1),
        'a_0': nrm(14, (L, RWKV_WIDTH), 0.1),
        'a_up': nrm(15, (L, AAA_LORA, RWKV_WIDTH), 0.5 * AAA_LORA ** -0.5),
        'g_up': nrm(16, (L, GATE_LORA, RWKV_WIDTH), GATE_LORA ** -0.5),
        'k_k': 0.85 + nrm(17, (L, RWKV_WIDTH), 0.02),
        'k_a': 1.0 + nrm(18, (L, RWKV_WIDTH), 0.02),
        'r_k': nrm(19, (L, N_RWKV_HEADS, HEAD_DIM), 0.1),
        'ln_x_g': 1.0 + nrm(20, (L, RWKV_WIDTH), 0.02),
        'ln_x_b': nrm(21, (L, RWKV_WIDTH), 0.02),
        'w_out': nrm(22, (L, D_MODEL, D_MODEL), D_MODEL ** -0.5),
        'norm2_g': 1.0 + nrm(23, (L, D_MODEL), 0.02),
        'w_ff_up': nrm(24, (L, D_MODEL, D_FF), D_MODEL ** -0.5),
        'w_ff_down': nrm(25, (L, D_FF, D_MODEL), D_FF ** -0.5),
    }


def reference(x_prompt, x_sample, cache_k, cache_v, state_wkv, state_shift, norm1_g, w_in, q_norm_g,
              k_norm_g, attn_sinks, rwkv_mu, w_decay_0, w_decay_up, a_0, a_up, g_up, k_k, k_a, r_k,
              ln_x_g, ln_x_b, w_out, norm2_g, w_ff_up, w_ff_down):
    yp, ys = x_prompt, x_sample
    pk, pv, pw, psh, sk, sv, sw, ssh = [], [], [], [], [], [], [], []
    for l in range(DEPTH):
        lp = {'norm1_g': norm1_g[l], 'w_in': w_in[l], 'q_norm_g': q_norm_g[l], 'k_norm_g': k_norm_g[l],
              'attn_sinks': attn_sinks[l], 'rwkv_mu': rwkv_mu[l], 'w_decay_0': w_decay_0[l],
              'w_decay_up': w_decay_up[l], 'a_0': a_0[l], 'a_up': a_up[l], 'g_up': g_up[l],
              'k_k': k_k[l], 'k_a': k_a[l], 'r_k': r_k[l], 'ln_x_g': ln_x_g[l], 'ln_x_b': ln_x_b[l],
              'w_out': w_out[l], 'norm2_g': norm2_g[l], 'w_ff_up': w_ff_up[l], 'w_ff_down': w_ff_down[l]}
        bp = yp.shape[0]
        zero_shift = jnp.zeros((bp, D_MODEL), yp.dtype)
        zero_wkv = jnp.zeros((bp, N_RWKV_HEADS, HEAD_DIM, HEAD_DIM), jnp.float32)
        yp, k1, v1, w1, s1 = _layer(yp, _swa_prompt, zero_shift, zero_wkv, lp)
        ys, k2, v2, w2, s2 = _layer(ys, functools.partial(_swa_cached, cache_k[l], cache_v[l]),
                                    state_shift[l], state_wkv[l], lp)
        pk.append(k1); pv.append(v1); pw.append(w1); psh.append(s1)
        sk.append(k2); sv.append(v2); sw.append(w2); ssh.append(s2)
    return (yp, ys, jnp.stack(pk), jnp.stack(pv), jnp.stack(pw), jnp.stack(psh),
            jnp.stack(sk), jnp.stack(sv), jnp.stack(sw), jnp.stack(ssh))
```

```python
import numpy as np
import concourse.bass as bass
import concourse.mybir as mybir
from concourse.bass_utils import run_bass_kernel_spmd

F32 = mybir.dt.float32
F32R = mybir.dt.float32r
BF16 = mybir.dt.bfloat16
AF = mybir.ActivationFunctionType
ALU = mybir.AluOpType
AX = mybir.AxisListType

D = 1024
HD = 64
NQ = 8
NKV = 2
WINDOW = 128
RW = 512
NH = 8
DFF = 4096
INW = 2560
RMS_EPS = 1e-6
GN_EPS = 64e-5
NEG = -30000.0
ENGS = ("pe", "act", "dve", "pool", "sp")
BLK = {"pe": "tensor", "act": "scalar", "dve": "vector", "pool": "gpsimd", "sp": "sync"}
SAME_ENG_SYNC = True


class Op:
    __slots__ = ("eng", "fn", "deps", "ddeps", "sig", "val", "sem", "is_dma", "semname", "idx")


class Sched:
    def __init__(self, nc):
        self.nc = nc
        self.streams = {e: [] for e in ENGS}
        self.lastw = {}
        self.readers = {}
        self.pending = {e: ([], {}) for e in ENGS}
        self.dma_count = {}
        self.ncomp = {}

    def add(self, eng, fn, r=(), w=(), dma=None, out=False):
        op = Op()
        op.eng, op.fn, op.sig, op.val, op.sem = eng, fn, False, 0, None
        op.is_dma = dma is not None
        op.semname = dma
        op.idx = self.ncomp.get(eng, 0)
        if not op.is_dma:
            self.ncomp[eng] = op.idx + 1
        deps = [(d, False) for d in self.pending[eng][0]]
        ddeps = dict(self.pending[eng][1])
        self.pending[eng] = ([], {})
        for k in r:
            lw = self.lastw.get(k)
            if lw is not None:
                deps.append((lw, True))
        for k in w:
            lw = self.lastw.get(k)
            if lw is not None:
                deps.append((lw, False))
            deps.extend((x, False) for x in self.readers.get(k, {}).values())
        keep = {}
        for d, raw in deps:
            if d is op:
                continue
            if d.is_dma:
                ddeps[d.semname] = max(ddeps.get(d.semname, 0), 16 * self.dma_count[d.semname])
                continue
            if d.eng == eng and not op.is_dma:
                if eng == "pe" or not SAME_ENG_SYNC or not raw or d.idx != op.idx - 1:
                    continue
            keep[id(d)] = d
        op.deps = list(keep.values())
        op.ddeps = ddeps
        for d in op.deps:
            d.sig = True
        if op.is_dma:
            op.sig = True
            self.dma_count[dma] = self.dma_count.get(dma, 0) + 1
            op.val = 16 * self.dma_count[dma]
        for k in r:
            self.readers.setdefault(k, {})[(eng, dma)] = op
        for k in w:
            self.lastw[k] = op
            self.readers[k] = {}
        self.streams[eng].append(op)
        return op

    def barrier(self):
        lasts = []
        for e in ENGS:
            comp = [o for o in self.streams[e] if not o.is_dma]
            if comp:
                comp[-1].sig = True
                lasts.append(comp[-1])
        dd = {k: 16 * v for k, v in self.dma_count.items()}
        for e in ENGS:
            self.pending[e] = ([o for o in lasts if o.eng != e], dict(dd))

    def emit(self, es):
        nc = self.nc
        eng_sem = {e: es.enter_context(nc.semaphore("s_" + e)) for e in ENGS}
        dma_sems = {k: es.enter_context(nc.semaphore("d_" + k)) for k in self.dma_count}
        for e in ENGS:
            cnt = 0
            for op in self.streams[e]:
                if op.is_dma:
                    op.sem = dma_sems[op.semname]
                elif op.sig:
                    cnt += 1
                    op.sem, op.val = eng_sem[e], cnt
        fin = Op()
        fin.eng, fin.fn, fin.sig, fin.is_dma, fin.deps = "sp", None, False, False, []
        fin.ddeps = {k: 16 * v for k, v in self.dma_count.items()}
        self.streams["sp"].append(fin)
        self.n_sems = len(dma_sems) + len(ENGS)
        block = es.enter_context(nc.Block())
        for e in ENGS:
            ops = self.streams[e]

            def body(engine, ops=ops):
                seen = {}
                for op in ops:
                    waits = {}
                    for d in op.deps:
                        key = "e_" + d.eng
                        if key not in waits or waits[key][1] < d.val:
                            waits[key] = (d.sem, d.val)
                    for name, val in op.ddeps.items():
                        waits["d_" + name] = (dma_sems[name], val)
                    for key, (sem, val) in waits.items():
                        if seen.get(key, 0) < val:
                            engine.wait_ge(sem, val)
                            seen[key] = val
                    if op.fn is None:
                        continue
                    ins = op.fn(engine)
                    if op.sig:
                        ins.then_inc(op.sem, 16 if op.is_dma else 1)

            getattr(block, BLK[e])(body)


class Arena:
    def __init__(self, nc, base, limit):
        self.nc, self.off, self.limit, self.n = nc, base, limit, 0
        self.peak = base

    def alloc(self, name, shape, dtype):
        esz = {F32: 4, F32R: 4, BF16: 2}[dtype]
        size = esz * int(np.prod(shape[1:]))
        size = (size + 31) // 32 * 32
        if self.off + size > self.limit:
            raise RuntimeError(f"SBUF arena overflow allocating {name} {shape}: off={self.off} size={size} limit={self.limit}")
        self.n += 1
        t = self.nc.alloc_sbuf_tensor_at(f"{name}", list(shape), dtype, offset=self.off)
        self.off += size
        self.peak = max(self.peak, self.off)
        return t

    def mark(self):
        return self.off

    def reset(self, m):
        self.off = m


def r32(ap):
    return ap.bitcast(F32R)


class K:
    def __init__(self, NSEQ, S, NB, stop_after=None, debug=False):
        self.NSEQ, self.S, self.NB = NSEQ, S, NB
        self.NSAMP = NB * 4
        self.TS = NB * 5
        self.NTOK = NSEQ * S + self.NSAMP
        self.T = 256
        self.stop_after = stop_after
        self.debug = debug
        nc = self.nc = bass.Bass("TRN2", target_bir_lowering=False)
        self.s = Sched(nc)
        self.din = {}
        self.dout = {}

    def declare_dram(self):
        nc, NSEQ, S, NB = self.nc, self.NSEQ, self.S, self.NB

        def i(name, shape):
            self.din[name] = nc.dram_tensor(name, list(shape), F32, kind="ExternalInput").ap()

        def o(name, shape):
            self.dout[name] = nc.dram_tensor(name, list(shape), F32, kind="ExternalOutput").ap()

        i("xp", [NSEQ * S, D])
        i("xs", [self.TS, D])
        i("ck", [NB, 128, 128])
        i("cv", [NB, 128, 128])
        i("swkv", [NB * 8, 4096])
        i("w_in", [D, INW])
        i("mu", [14 * 128])
        i("w_out", [D, D])
        i("w_up", [D, DFF])
        i("w_dn", [DFF, D])
        i("g1", [D])
        i("g2", [D])
        i("qg", [HD])
        i("kg", [HD])
        i("sinks", [NQ])
        i("w0", [RW])
        i("wdu", [64, RW])
        i("a0", [RW])
        i("aup", [64, RW])
        i("gup", [128, RW])
        i("kk", [RW])
        i("ka", [RW])
        i("rk", [RW])
        i("lng", [RW])
        i("lnb", [RW])
        o("yp", [NSEQ * S, D])
        o("ys", [self.NSAMP, D])
        o("nkp", [NSEQ, 128, 128])
        o("nvp", [NSEQ, 128, 128])
        o("nwp", [NSEQ * 8 * 64, 64])
        o("nsp", [NSEQ, D])
        o("nks", [NB, 128, 128])
        o("nvs", [NB, 128, 128])
        o("nws", [NB * 8, 4096])
        o("nss", [NB, D])
        if self.debug == "phaseB":
            self.mix_scr = nc.dram_tensor("mix_in", [8, 128, self.NTOK], BF16, kind="ExternalInput").ap()
        else:
            self.mix_scr = nc.dram_tensor("mix_scr", [8, 128, self.NTOK], BF16, kind="Internal").ap()
        self.rw_scr = nc.dram_tensor("rw_scr", [6, self.NSAMP, RW], F32, kind="Internal").ap()
        self.w_up_bf = nc.dram_tensor("w_up_bf", [D, DFF], BF16, kind="Internal").ap()
        self.w_dn_bf = nc.dram_tensor("w_dn_bf", [DFF, D], BF16, kind="Internal").ap()
        self.y_scr = nc.dram_tensor("y_scr", [self.NSAMP, RW], F32, kind="Internal").ap()

    def _op(self, eng, meth, R, W, kw):
        return self.s.add(eng, lambda e: getattr(e, meth)(**kw), R, W)

    def pe(self, meth, R=(), W=(), **kw):
        return self._op("pe", meth, R, W, kw)

    def act(self, meth, R=(), W=(), **kw):
        return self._op("act", meth, R, W, kw)

    def dve(self, meth, R=(), W=(), **kw):
        return self._op("dve", meth, R, W, kw)

    def pool(self, meth, R=(), W=(), **kw):
        return self._op("pool", meth, R, W, kw)

    def dbg(self, name, ap, R):
        if not self.debug:
            return
        self.ndbg = getattr(self, "ndbg", 0) + 1
        d = self.nc.dram_tensor(name, list(ap.shape), ap.dtype, kind="ExternalOutput").ap()
        self.dout[name] = d
        self.dma("sp", f"dbg{self.ndbg}", d, ap, R=R, W=[("dbg", name)])

    def dma(self, q, sem, out, in_, R=(), W=(), **kw):
        return self.s.add(q, lambda e: e.dma_start(out=out, in_=in_, **kw), R, W, dma=sem)

    def setup(self):
        nc = self.nc
        base = 16512
        self.ar = Arena(nc, base, 229376)
        A = self.ar.alloc
        self.pb = [nc.alloc_psum_tensor(f"pb{i}", [128, 512], F32) for i in range(8)]
        self.ident_f = A("ident_f", [128, 128], F32)
        self.ident_bf = A("ident_bf", [128, 128], BF16)
        self.onesblk = A("onesblk", [128, 128], F32)
        self.maskZ = A("maskZ", [128, 128], F32)
        self.maskGT = A("maskGT", [64, 64], F32)
        self.amask = A("amask", [128, 2, 512], BF16)
        self.scanmask = A("scanmask", [128, 256], F32)
        self.gbc1 = A("gbc1", [128, D], F32)
        self.cols = A("cols", [128, 96], F32)
        self.esink = A("esink", [128, 8], F32)
        self.w_out_sb = A("w_out_sb", [128, 8, D], BF16)
        obf = A("onesblk_f", [128, 128], F32)
        self.shared_base = self.ar.mark()
        P, DV = self.pool, self.dve
        idf, idb, ob, mz, mg = self.ident_f, self.ident_bf, self.onesblk, self.maskZ, self.maskGT

        P("memset", W=["ident_f"], ap=idf[:], constant=0.0)
        P("affine_select", W=["ident_f"], out=idf[:], in_=idf[:], pattern=[[-1, 128]], compare_op=ALU.not_equal,
          fill=1.0, base=0, channel_multiplier=1)
        DV("tensor_copy", R=["ident_f"], W=["ident_bf"], out=idb[:], in_=idf[:])
        P("memset", W=["onesblk_f"], ap=obf[:], constant=0.0)
        P("memset", W=["onesblk_f"], ap=obf[0:64, 0:64], constant=1.0)
        P("memset", W=["onesblk_f"], ap=obf[64:128, 64:128], constant=1.0)
        DV("tensor_copy", R=["onesblk_f"], W=["onesblk"], out=r32(ob[:]), in_=obf[:])
        P("memset", W=["maskZ"], ap=mz[:], constant=1.0)
        for (p0, c0, strict) in ((0, 0, True), (64, 0, True), (0, 64, False), (64, 64, False)):
            P("affine_select", W=["maskZ"], out=mz[p0:p0 + 64, c0:c0 + 64], in_=mz[p0:p0 + 64, c0:c0 + 64],
              pattern=[[1, 64]], compare_op=ALU.is_ge, fill=0.0, base=(-1 if strict else 0), channel_multiplier=-1)
        P("memset", W=["maskGT"], ap=mg[:], constant=1.0)
        P("affine_select", W=["maskGT"], out=mg[:], in_=mg[:], pattern=[[-1, 64]], compare_op=ALU.is_ge, fill=0.0,
          base=-1, channel_multiplier=1)
        am = self.amask
        P("memset", W=["amask"], ap=am[:], constant=0.0)
        for j in range(4):
            P("affine_select", W=["amask"], out=am[:, 0, j * 128:(j + 1) * 128], in_=am[:, 0, j * 128:(j + 1) * 128],
              pattern=[[-1, 128]], compare_op=ALU.is_ge, fill=NEG, base=-1, channel_multiplier=1)
            P("affine_select", W=["amask"], out=am[:, 1, j * 128:(j + 1) * 128], in_=am[:, 1, j * 128:(j + 1) * 128],
              pattern=[[1, 128]], compare_op=ALU.is_ge, fill=NEG, base=0, channel_multiplier=-1)
        sm = self.scanmask
        P("memset", W=["scanmask"], ap=sm[:], constant=1.0)
        P("memset", W=["scanmask"], ap=sm[:].rearrange("p (c t) -> p c t", t=64)[:, :, 0:1], constant=0.0)

    def load_small(self):
        nc, din = self.nc, self.din
        cols = self.cols
        self.col = {}
        n = [0]

        def colvec(name, src_ap, ncol):
            c0 = n[0]
            n[0] += ncol
            self.col[name] = c0
            self.dma("sp", "small", cols[:, c0:c0 + ncol], src_ap, W=["cols"], allow_slow_non_contiguous=True)

        colvec("mu", din["mu"].rearrange("(j p) -> p j", p=128), 14)
        for nm in ("w0", "a0", "kk", "ka", "rk", "lng", "lnb"):
            colvec(nm, din[nm].rearrange("(j p) -> p j", p=128), 4)
        colvec("g2", din["g2"].rearrange("(j p) -> p j", p=128), 8)
        for nm in ("qg", "kg"):
            c0 = n[0]
            n[0] += 1
            self.col[nm] = c0
            for h in range(2):
                self.dma("sp", "small", cols[h * 64:(h + 1) * 64, c0:c0 + 1], din[nm].rearrange("(p o) -> p o", o=1),
                         W=["cols"], allow_slow_non_contiguous=True)
        self.dma("sp", "small", self.gbc1[:], din["g1"].rearrange("(o d) -> o d", o=1).partition_broadcast(128), W=["gbc1"])
        self.dma("sp", "small", self.esink[:], din["sinks"].rearrange("(o d) -> o d", o=1).partition_broadcast(128), W=["esink"])
        c = self.col
        c["qgs"] = n[0]; n[0] += 1
        c["omka"] = n[0]; n[0] += 4
        c["nw0"] = n[0]; n[0] += 4
        c["na0"] = n[0]; n[0] += 4
        cq, cqs, cka, comka = c["qg"], c["qgs"], c["ka"], c["omka"]
        self.dve("tensor_scalar", R=["cols"], W=["cols2"], out=cols[:, cqs:cqs + 1], in0=cols[:, cq:cq + 1], scalar1=0.125,
                 scalar2=None, op0=ALU.mult)
        self.dve("tensor_scalar", R=["cols"], W=["cols2"], out=cols[:, comka:comka + 4], in0=cols[:, cka:cka + 4], scalar1=-1.0,
                 scalar2=1.0, op0=ALU.mult, op1=ALU.add)
        for src, dst in (("w0", "nw0"), ("a0", "na0")):
            self.dve("tensor_scalar", R=["cols"], W=["cols2"], out=cols[:, c[dst]:c[dst] + 4], in0=cols[:, c[src]:c[src] + 4], scalar1=-1.0,
                     scalar2=None, op0=ALU.mult)
        self.act("activation", W=["esink"], out=self.esink[:], in_=self.esink[:], func=AF.Exp)

    def load_weight_bf16(self, dst, src, nkt, ncols, name, key):
        v = src.rearrange("(kt p) n -> p kt n", p=128)
        step = 1024
        for kt in range(nkt):
            for c0 in range(0, ncols, step):
                c1 = min(ncols, c0 + step)
                self.dma("pool", f"w_{name}", dst[:, kt, c0:c1], v[:, kt, c0:c1], W=[key])

    def bank(self, lo, hi, tag):
        c = self._bankctr.get(tag, 0)
        self._bankctr[tag] = c + 1
        return lo + c % (hi - lo)

    def phase_a(self):
        ar = self.ar
        ar.reset(self.shared_base)
        A = ar.alloc
        T = self.T
        self._bankctr = {}
        din = self.din
        self.w_in_sb = A("w_in_sb", [128, 8, INW], BF16)
        vw = din["w_in"].rearrange("(kt p) n -> p kt n", p=128)
        for ci, (c0, c1) in enumerate(((0, 1024), (1024, 2048), (2048, INW))):
            for kt in range(8):
                self.dma("pool", f"w_in{ci}", self.w_in_sb[:, kt, c0:c1], vw[:, kt, c0:c1], W=[("w_in", ci)])
        self.wcast_jobs = []
        for r0 in range(0, D, 128):
            for c0 in range(0, DFF, 1024):
                self.wcast_jobs.append((self.w_up_bf[r0:r0 + 128, c0:c0 + 1024], din["w_up"][r0:r0 + 128, c0:c0 + 1024], "w_up_bf"))
        for r0 in range(0, DFF, 128):
            self.wcast_jobs.append((self.w_dn_bf[r0:r0 + 128, :], din["w_dn"][r0:r0 + 128, :], "w_dn_bf"))
        self.wl = A("wl", [128, RW], F32)
        self.gup = A("gup", [128, RW], F32)
        hl = A("hl", [128, D], F32)
        self.dma("sp", "wl", hl[0:64, 0:RW], din["wdu"], W=["hl"])
        self.dma("sp", "wl", hl[64:128, 0:RW], din["aup"], W=["hl"])
        self.dma("sp", "wl", hl[:, RW:2 * RW], din["gup"], W=["hl"])
        self.dve("tensor_copy", R=["hl"], W=["wl"], out=r32(self.wl[:]), in_=hl[:, 0:RW])
        self.dve("tensor_copy", R=["hl"], W=["gup"], out=r32(self.gup[:]), in_=hl[:, RW:2 * RW])
        B = self.B = {}
        B["xin"] = [A("xin0", [128, T // 128, D], F32)]
        B["xin"].append(B["xin"][0])
        B["xn"] = [A(f"xn{i}", [128, D], BF16) for i in range(2)]
        B["hl"] = hl
        B["st"] = A("sta", [128, 8], F32)
        B["hT"] = A("hT", [128, 8, T], BF16)
        B["raw"] = [A("raw0", [128, T], F32)] * 2
        B["sq"] = [A("sq0", [128, T], F32)] * 2
        B["lnv"] = [A("lnv0", [128, T], F32)] * 2
        B["vf"] = A("vf", [128, 128], F32)
        B["kf"] = A("kf", [128, 128], F32)
        B["kft"] = A("kft", [128, 128], F32)
        B["pT"] = [A(f"pT{i}", [128, 512], BF16) for i in range(4)]
        B["attn"] = [A("attn0", [128, 512], BF16)] * 2
        B["rden"] = A("rden", [128, 8], F32)
        B["pcur"] = [A(f"pcur{i}", [128, T + 1], F32) for i in range(2)]
        B["diff"] = [A(f"diff{i}", [128, T], F32) for i in range(2)]
        for nm in ("tw", "sgg", "sg", "aa", "Lc", "eL", "eN", "rn", "kkn", "fac", "kh", "t2", "rk"):
            B[nm] = A(nm, [128, T], F32)
        B["xar"] = B["tw"]
        B["gst"] = A("gst", [64, 96], F32)
        B["yg"] = A("yg", [128, T], F32)
        B["yb"] = A("yb", [128, T], F32)
        self.prompt_only_base = ar.mark()
        B["qT"] = A("qT", [128, 4, T], BF16)
        B["kT"] = A("kT", [128, 512], BF16)
        B["vtok"] = A("vtok", [128, 4, 2, 65], BF16)
        B["carry"] = A("carry", [128, 16], F32)
        B["pm"] = A("pm", [128, 14, T], F32)
        B["mixT"] = A("mixT", [128, 8, T], BF16)
        B["bonus"] = A("bonus", [128, 4, T], F32)
        B["AR"] = A("AR", [128, 4, T // 64, 2, 64], F32)
        B["BK"] = A("BK", [128, 4, T // 64, 2, 64], F32)
        B["eLC"] = A("eLC", [128, 4, T // 64], F32)
        B["Z"] = [A(f"Z{i}", [128, 8, 128], F32) for i in range(2)]
        for nm in ("G0", "Ga", "Gb", "GTa", "GTb", "Sa", "Sb", "Xsb"):
            B[nm] = A(nm, [128, 8, 64], F32)
        B["BKtok"] = [A(f"BKtok{i}", [128, 8, 64], F32) for i in range(2)]
        B["UV"] = [A(f"UV{i}", [128, 8, 64], F32) for i in range(2)]
        B["HS"] = [A(f"HS{i}", [128, 4, 64], F32) for i in range(2)]
        B["Hd"] = A("Hd", [128, 4, 64], F32)
        B["Htmp"] = A("Htmp", [128, 4, 64], F32)
        B["ysb"] = A("ysb", [64, T // 64, 8, 64], F32)
        B["stout"] = A("stout", [64, 8, 64], F32)
        self.a_mark = ar.mark()

        tiles = [dict(kind="p", seq=q, t0=t0, tok0=q * self.S + t0, T=T, nsub=T // 128, rows=128, first=(t0 == 0), last=(t0 + T == self.S))
                 for q in range(self.NSEQ) for t0 in range(0, self.S, T)]
        self.pool("memset", W=["vtok"], ap=B["vtok"][:, :, :, 64:65], constant=1.0)
        if self.stop_after in ("proj", "elem"):
            for tl in tiles:
                self.tile_load(tl)
                self.norm_transpose(tl, 0)
                self.tile_a(tl, None)
        else:
            self.tile_load(tiles[0])
            self.norm_transpose(tiles[0], 0)
            per = -(-len(self.wcast_jobs) // max(1, len(tiles) - 2))
            for i, tl in enumerate(tiles):
                self.tile_a(tl, tiles[i + 1] if i + 1 < len(tiles) else None)
                for _ in range(per):
                    if self.wcast_jobs:
                        o, i_, k_ = self.wcast_jobs.pop(0)
                        self.dma("pool", "wcast", o, i_, W=[k_])
            while self.wcast_jobs:
                o, i_, k_ = self.wcast_jobs.pop(0)
                self.dma("pool", "wcast", o, i_, W=[k_])
        if self.stop_after is not None:
            return
        self.sample_tile()

    def sample_tile(self):
        s_ = self.s
        s_.barrier()
        ar, B, pb, din, dout = self.ar, self.B, self.pb, self.din, self.dout
        ar.reset(self.prompt_only_base)
        A = ar.alloc
        NB, NS, TS = self.NB, self.NSAMP, self.TS
        SB = self.SB = {}
        SB["qT_s"] = A("qT_s", [128, 4, TS], BF16)
        SB["kT_s"] = A("kT_s", [128, TS], BF16)
        SB["vtok_s"] = A("vtok_s", [128, 2, 65], BF16)
        SB["pm_s"] = A("pm_s", [128, 14, TS], F32)
        SB["bonus_s"] = A("bonus_s", [128, 4, NS], F32)
        SB["q6"] = A("q6", [128, 6, 4, NS], F32)
        SB["mixT_s"] = A("mixT_s", [128, 8, NS], BF16)
        SB["maskc"] = A("maskc", [128, NB, 4, NS], BF16)
        SB["maskn"] = A("maskn", [128, 4, NS], BF16)
        SB["ckf"] = A("ckf", [128, NB, 128], F32)
        SB["ckb"] = A("ckb", [128, NB, 128], BF16)
        SB["ckT"] = A("ckT", [128, NB, 128], BF16)
        SB["cvb"] = A("cvb", [128, NB, 2, 65], BF16)
        SB["q6tok"] = A("q6tok", [64, 6, RW], F32)
        SB["rkv"] = A("rkv", [128, 6, 4, 64], F32)
        SB["Sst"] = A("Sst", [128, 64, 64], F32)
        wv = self.w_in_sb[:].rearrange("p k n -> p (k n)").bitcast(F32)
        SB["tA"] = wv[:, 0:4096].rearrange("p (i j) -> p i j", i=64)
        SB["tB"] = wv[:, 4096:8192].rearrange("p (i j) -> p i j", i=64)
        SB["sa"] = A("sa", [128, 64], F32)
        SB["ysr"] = A("ysr", [128, 4, 64], F32)
        SB["ytok"] = A("ytok", [64, 1, 8, 64], F32)
        SB["ysq_s"] = A("ysq_s", [64, 1, 8, 64], F32)
        tl = dict(kind="s", T=TS, nsub=1, rows=TS, nreal=NS, first=True, last=True, t0=0)
        x_ = B["xin"][0]
        self.dma("sp", "xs_in", x_[0:TS, 0, :], din["xs"][:, :], W=[("xin", 0)])
        self.dma("sp", "ck_in", SB["ckf"][:], din["ck"].rearrange("b k d -> k b d"), W=["ckf"])
        self.dma("sp", "swkv_in", SB["Sst"][:].rearrange("p i j -> p (i j)"), din["swkv"][:, :], W=[("Sst", 0), ("Sst", 1)])
        self.dma("sp", "nks_c", dout["nks"][:, 0:124, :], din["ck"][:, 4:128, :], W=["nks_c"])
        self.dma("sp", "nvs_c", dout["nvs"][:, 0:124, :], din["cv"][:, 4:128, :], W=["nvs_c"])
        mc, mn = SB["maskc"], SB["maskn"]
        mcf = mc[:].rearrange("p b j q -> p (b j q)")
        self.pool("memset", W=["maskc"], ap=mcf, constant=0.0)
        self.pool("affine_select", W=["maskc"], out=mcf, in_=mcf, pattern=[[1, NB], [0, 4], [-1, NB], [0, 4]],
                  compare_op=ALU.is_equal, fill=NEG, base=0, channel_multiplier=0)
        self.pool("affine_select", W=["maskc"], out=mcf, in_=mcf, pattern=[[0, NB], [0, 4], [0, NB], [-1, 4]],
                  compare_op=ALU.is_ge, fill=NEG, base=-1, channel_multiplier=1)
        mnf = mn[0:NS].rearrange("p j q -> p (j q)")
        self.pool("memset", W=["maskn"], ap=mn[:].rearrange("p j q -> p (j q)"), constant=0.0)
        self.pool("affine_select", W=["maskn"], out=mnf, in_=mnf, pattern=[[0, 4], [-4, NB], [0, 4]], compare_op=ALU.is_ge, fill=NEG,
                  base=0, channel_multiplier=1)
        self.pool("affine_select", W=["maskn"], out=mnf, in_=mnf, pattern=[[0, 4], [4, NB], [1, 4]], compare_op=ALU.is_ge, fill=NEG,
                  base=0, channel_multiplier=-1)
        self.pool("memset", W=["vtok_s"], ap=SB["vtok_s"][:, :, 64:65], constant=1.0)
        self.norm_transpose(tl, 0)
        self.project(tl)
        for t in range(4):
            self.dma("pool", "nks_n", dout["nks"][:, 124 + t, :], B["kft"][t:NS:4, :], R=["kft"], W=[("nks_n", t)])
            self.dma("pool", "nvs_n", dout["nvs"][:, 124 + t, :], B["vf"][t:NS:4, :], R=["vf"], W=[("nvs_n", t)])
        self.pool("tensor_copy", R=["ckf"], W=["ckb"], out=SB["ckb"][:], in_=SB["ckf"][:])
        for g in range(NB // 8):
            b = self.bank(2, 4, "tr")
            pbt = pb[b][:].bitcast(BF16).rearrange("p (k t) -> p k t", k=8)
            for i in range(8):
                self.pe("transpose", R=["ckb", "ident_bf"], W=[("pb", b)], out=pbt[:, i, :], in_=SB["ckb"][:, g * 8 + i, :],
                        identity=self.ident_bf[:])
            self.act("activation", W=[("pb", b), "ckT"], out=SB["ckT"][:, g * 8:(g + 1) * 8, :], in_=pbt[:, :, :], func=AF.Copy)
        self.dma("sp", "cv_in", SB["ckf"][:], din["cv"].rearrange("b k d -> k b d"), R=["ckb"], W=["ckf"])
        self.pool("memset", W=["cvb"], ap=SB["cvb"][:, :, :, 64:65], constant=1.0)
        self.pool("tensor_copy", R=["ckf"], W=["cvb"], out=SB["cvb"][:, :, :, 0:64], in_=SB["ckf"][:].rearrange("p b (k d) -> p b k d", k=2))
        if self.stop_after == "sproj":
            self.dbg("d_qTs", SB["qT_s"][:], ["qT"])
            self.dbg("d_pms", SB["pm_s"][:], [("pm", i) for i in range(14)])
            return
        qT_s, kT_s = SB["qT_s"], SB["kT_s"]
        at, ka = B["attn"][0], ("attn", 0)
        for kvh in range(2):
            ps_ = slice(kvh * 64, (kvh + 1) * 64)
            bpv = self.bank(6, 8, "pv")
            pv = pb[bpv][0:NS, 0:260].rearrange("p (j d) -> p j d", j=4)
            first = [True]

            def pv_mm(pt_ap, rhs_ap, rkeys, last):
                for j in range(4):
                    self.pe("matmul", R=rkeys, W=[("pb", bpv)], out=pv[:, j, :], lhsT=pt_ap(j), rhs=rhs_ap, start=first[0],
                            stop=last and j == 3, skip_group_check=True)
                    first[0] = False
            for g in range(NB // 2):
                b = self.bank(4, 6, "sc")
                ip = self._bankctr.get("spt", 0)
                self._bankctr["spt"] = ip + 1
                pt = B["pT"][ip % 4]
                kpt = ("pT", ip % 4)
                for hf in range(2):
                    bb = g * 2 + hf
                    o = pb[b][:, hf * 256:(hf + 1) * 256]
                    self.pe("matmul", R=["ident_bf", "maskc"], W=[("pb", b)], out=o, lhsT=self.ident_bf[:],
                            rhs=mc[:, bb, :, :].rearrange("p j q -> p (j q)"), start=True, stop=False)
                    self.pe("matmul", R=["ckT", "qT"], W=[("pb", b)], out=o.rearrange("p (j q) -> p j q", j=4), lhsT=SB["ckT"][ps_, bb, :],
                            rhs=qT_s[ps_, :, 0:NS], start=False, stop=True)
                self.act("activation", W=[("pb", b), kpt], out=pt[:, :], in_=pb[b][:, :], func=AF.Exp)
                for hf in range(2):
                    bb = g * 2 + hf
                    pv_mm(lambda j, hf=hf, pt=pt: pt[:, hf * 256 + j * NS:hf * 256 + (j + 1) * NS], SB["cvb"][:, bb, kvh, :], [kpt, "cvb"], False)
            b = self.bank(4, 6, "sc")
            ip = self._bankctr.get("spt", 0)
            self._bankctr["spt"] = ip + 1
            pt = B["pT"][ip % 4]
            kpt = ("pT", ip % 4)
            o = pb[b][0:NS, 0:256]
            self.pe("matmul", R=["ident_bf", "maskn"], W=[("pb", b)], out=o, lhsT=self.ident_bf[:, 0:NS],
                    rhs=mn[:].rearrange("p j q -> p (j q)"), start=True, stop=False)
            self.pe("matmul", R=["kT", "qT"], W=[("pb", b)], out=o.rearrange("p (j q) -> p j q", j=4), lhsT=kT_s[ps_, 0:NS],
                    rhs=qT_s[ps_, :, 0:NS], start=False, stop=True)
            self.act("activation", W=[("pb", b), kpt], out=pt[0:NS, 0:256], in_=o, func=AF.Exp)
            pv_mm(lambda j, pt=pt: pt[0:NS, j * NS:(j + 1) * NS], SB["vtok_s"][0:NS, kvh, :], [kpt, "vtok_s"], True)
            rden = B["rden"]
            self.dve("tensor_tensor", R=["esink"], W=[("pb", bpv), "rden"], out=rden[0:NS, 0:4], in0=pv[:, :, 64],
                     in1=self.esink[0:NS, kvh * 4:(kvh + 1) * 4], op=ALU.add)
            self.dve("reciprocal", R=["rden"], W=["rden"], out=rden[0:NS, 4:8], in_=rden[0:NS, 0:4])
            self.dve("tensor_tensor", R=["rden"], W=[("pb", bpv), ka],
                     out=at[0:NS, kvh * 256:(kvh + 1) * 256].rearrange("p (j d) -> p j d", j=4), in0=pv[:, :, 0:64],
                     in1=rden[0:NS, 4:8].unsqueeze(2).to_broadcast([NS, 4, 64]), op=ALU.mult)
        self.attn_to_mix(at, ka, SB["mixT_s"], 0, NS)
        for _ in self.rwkv_elem(tl, sample=True):
            pass
        q6, q6tok, rkv = SB["q6"], SB["q6tok"], SB["rkv"]
        idf = self.ident_f
        for sl in range(6):
            b = self.bank(0, 6, "ch")
            for j in range(4):
                self.pe("transpose", R=[("q6", j), "ident_f"], W=[("pb", b)], out=pb[b][0:NS, j * 128:(j + 1) * 128], in_=q6[:, sl, j, 0:NS],
                        identity=idf[:])
            self.act("activation", W=[("pb", b), "q6tok"], out=q6tok[0:NS, sl, :], in_=pb[b][0:NS, :], func=AF.Copy)
        scr = self.rw_scr
        scr5 = scr.rearrange("s n f -> s (n f)").rearrange("s (b h t c) -> s b h t c", b=NB, h=8, t=4)
        for t in range(4):
            for sl in range(6):
                self.dma("sp", "rw_w", scr5[sl, :, :, t, :], q6tok[t:NS:4, sl, :].rearrange("p (h c) -> p h c", h=8), R=["q6tok"], W=["rwscr"])
        self.dma("sp", "rw_r", rkv[:], scr.rearrange("s n f -> s (n f)").rearrange("s (p tc) -> p s tc", p=128).rearrange("p s (t c) -> p s t c", t=4),
                 R=["rwscr"], W=["rkv"])
        S3, tA, tB, sa, ysr = SB["Sst"], SB["tA"], SB["tB"], SB["sa"], SB["ysr"]
        RH = 30
        halves = ((self.dve, slice(0, RH), 0), (self.pool, slice(RH, 64), 1))
        WIN = [("w_in", 0), ("w_in", 1), ("w_in", 2)]
        for t in range(4):
            def bi(sl, rs_):
                n_ = rs_.stop - rs_.start
                return rkv[:, sl, t, :].unsqueeze(1).to_broadcast([128, n_, 64])
            for eng, rs_, hh in halves:
                eng("tensor_tensor", R=[("Sst", hh), "rkv"], W=[("tA", hh)] + WIN, out=tA[:, rs_, :], in0=S3[:, rs_, :], in1=bi(4, rs_), op=ALU.mult)
            self.dve("tensor_reduce", R=[("tA", 0), ("tA", 1)], W=["sa"], out=sa[:], in_=tA, axis=AX.X, op=ALU.add)
            for eng, rs_, hh in halves:
                n_ = rs_.stop - rs_.start
                eng("tensor_tensor", R=["rkv"], W=[("tB", hh)] + WIN, out=tB[:, rs_, :],
                    in0=rkv[:, 3, t, rs_].unsqueeze(2).to_broadcast([128, n_, 64]), in1=bi(2, rs_), op=ALU.mult)
                eng("tensor_tensor", R=["rkv", ("tA", hh)], W=[("Sst", hh)], out=S3[:, rs_, :], in0=S3[:, rs_, :], in1=bi(1, rs_), op=ALU.mult)
            for eng, rs_, hh in halves:
                n_ = rs_.stop - rs_.start
                eng("tensor_tensor", R=["sa", "rkv"], W=[("tA", hh)] + WIN, out=tA[:, rs_, :],
                    in0=sa[:, rs_].unsqueeze(2).to_broadcast([128, n_, 64]), in1=bi(5, rs_), op=ALU.mult)
                eng("tensor_tensor", R=[("tA", hh)], W=[("Sst", hh)], out=S3[:, rs_, :], in0=S3[:, rs_, :], in1=tA[:, rs_, :], op=ALU.add)
                eng("tensor_tensor", R=[("tB", hh)], W=[("Sst", hh)], out=S3[:, rs_, :], in0=S3[:, rs_, :], in1=tB[:, rs_, :], op=ALU.add)
                eng("tensor_tensor", R=[("Sst", hh), "rkv"], W=[("tA", hh)] + WIN, out=tA[:, rs_, :], in0=S3[:, rs_, :], in1=bi(0, rs_), op=ALU.mult)
            self.dve("tensor_reduce", R=[("tA", 0), ("tA", 1)], W=["ysr"], out=ysr[:, t, :], in_=tA, axis=AX.X, op=ALU.add)
        self.dma("pool", "nws", dout["nws"][:, :], S3[:].rearrange("p i j -> p (i j)"), R=[("Sst", 0), ("Sst", 1)], W=["nws"])
        ysc = self.y_scr
        self.dma("sp", "ys_w", ysc.rearrange("n f -> (n f)").rearrange("(p tc) -> p tc", p=128), ysr[:].rearrange("p t c -> p (t c)"),
                 R=["ysr"], W=["yscr"])
        ys5 = ysc.rearrange("n f -> (n f)").rearrange("(b h t c) -> b h t c", b=NB, h=8, t=4)
        ytok = SB["ytok"]
        for t in range(4):
            self.dma("sp", "ys_r", ytok[t:NS:4, 0, :, :], ys5[:, :, t, :], R=["yscr"], W=["ytok"])
        self.gn_transpose(ytok, SB["ysq_s"], 1, ["ytok"], ["ysq_s"])
        self.rwkv_combine(tl, SB["pm_s"], SB["mixT_s"], NS)
        tok0 = self.NSEQ * self.S
        self.dma("pool", "mixst", self.mix_scr[:, :, tok0:tok0 + NS].rearrange("k p t -> p k t"), SB["mixT_s"][:, :, 0:NS],
                 R=["mixT"], W=[("mixscr", tok0)])

    def norm_transpose(self, tl, sl):
        self.norm1(tl, sl)
        self.transpose1(tl, sl)

    def norm1(self, tl, sl):
        B, pb, cols = self.B, self.pb, self.cols
        x_ = B["xin"][sl]
        st = B["st"]
        rows, nsub = tl["rows"], tl["nsub"]
        nr = tl.get("nreal", rows)
        for s_ in range(nsub):
            xn = B["xn"][s_ % 2]
            sc = st[:, 4 * s_:4 * s_ + 4]
            self.act("activation", R=[("xin", sl)], W=[("xn", s_ % 2), "sta"], out=xn[0:nr, :], in_=x_[0:nr, s_, :], func=AF.Square,
                     accum_out=sc[0:nr, 0:1])
            self.act("activation", R=["sta"], W=["sta"], out=sc[0:nr, 1:2], in_=sc[0:nr, 0:1], func=AF.Ln, bias=RMS_EPS, scale=1.0 / D)
            self.act("activation", R=["sta"], W=["sta"], out=sc[0:nr, 2:3], in_=sc[0:nr, 1:2], func=AF.Exp, scale=-0.5)
            self.dve("scalar_tensor_tensor", R=[("xin", sl), "sta", "gbc1"], W=[("xn", s_ % 2)], out=xn[0:nr, :], in0=x_[0:nr, s_, :],
                     scalar=sc[0:nr, 2:3], in1=self.gbc1[0:nr, :], op0=ALU.mult, op1=ALU.mult)
            if nr < rows:
                self.dve("tensor_copy", R=[("xin", sl)], W=[("xn", s_ % 2)], out=xn[nr:rows, :], in_=x_[nr:rows, s_, :])
            want_shift = (tl["kind"] == "s") or (tl["last"] and s_ == nsub - 1)
            if want_shift:
                hl = B["hl"]
                self.dve("scalar_tensor_tensor", R=[("xin", sl), "sta", "gbc1"], W=["hl"], out=hl[0:nr, :], in0=x_[0:nr, s_, :],
                         scalar=sc[0:nr, 2:3], in1=self.gbc1[0:nr, :], op0=ALU.mult, op1=ALU.mult)
                if tl["kind"] == "p":
                    self.dma("pool", "nsp", self.dout["nsp"][tl["seq"]:tl["seq"] + 1, :], hl[127:128, :], R=["hl"], W=[("nsp", tl["seq"])])
                else:
                    self.dma("pool", "nss", self.dout["nss"][:, :], hl[3:nr:4, :], R=["hl"], W=["nss"])

    def transpose1(self, tl, sl):
        B, pb = self.B, self.pb
        rows, nsub = tl["rows"], tl["nsub"]
        for s_ in range(nsub):
            xn = B["xn"][s_ % 2]
            b = self.bank(2, 4, "tr")
            pbt = pb[b][:].bitcast(BF16).rearrange("p (k t) -> p k t", k=8)
            for kt in range(8):
                self.pe("transpose", R=[("xn", s_ % 2), "ident_bf"], W=[("pb", b)], out=pbt[:, kt, 0:rows],
                        in_=xn[0:rows, kt * 128:(kt + 1) * 128], identity=self.ident_bf[0:rows, 0:rows])
            c0 = s_ * 128
            if s_ % 2 == 0:
                self.act("activation", W=[("pb", b), "hT"], out=B["hT"][:, :, c0:c0 + rows], in_=pbt[:, :, 0:rows], func=AF.Copy)
            else:
                self.dve("tensor_copy", W=[("pb", b), "hT"], out=B["hT"][:, :, c0:c0 + rows], in_=pbt[:, :, 0:rows])

    def project(self, tl):
        B, pb, cols, col = self.B, self.pb, self.cols, self.col
        Tt = tl["T"]
        w_in_sb, hT = self.w_in_sb, B["hT"]
        kind = tl["kind"]
        order = [0, 1, 2, 3, 4] + list(range(6, 20))
        for s_ in range(tl["nsub"]):
            rows = tl["rows"]
            c0 = s_ * 128
            b = self.bank(2, 4, "tr")
            for kt in range(8):
                self.pe("matmul", R=["hT", ("w_in", 0)], W=[("pb", b)], out=pb[b][0:rows, 0:128], lhsT=hT[:, kt, c0:c0 + rows],
                        rhs=w_in_sb[:, kt, 640:768], start=(kt == 0), stop=(kt == 7))
            self.evac_v(tl, s_, b)
        for idx in range(0, len(order), 2):
            b = (0, 1, 4, 5)[self.bank(0, 4, "proj")]
            pair = order[idx:idx + 2]
            for hf, ot in enumerate(pair):
                for kt in range(8):
                    self.pe("matmul", R=["hT", ("w_in", ot // 8)], W=[("pb", b)], out=pb[b][:, hf * 256:hf * 256 + Tt],
                            lhsT=w_in_sb[:, kt, ot * 128:(ot + 1) * 128], rhs=hT[:, kt, 0:Tt], start=(kt == 0), stop=(kt == 7))
            for hf, ot in enumerate(pair):
                ps = pb[b][:, hf * 256:hf * 256 + Tt]
                if ot <= 4:
                    self.qknorm(tl, ot, ps, b)
                else:
                    self.shiftmix(tl, ot - 6, ps, b)

    def evac_v(self, tl, s_, b):
        B, pb = self.B, self.pb
        rows = tl["rows"]
        ps = pb[b][0:rows, 0:128].rearrange("p (k d) -> p k d", k=2)
        if tl["kind"] == "p":
            blk = (tl["t0"] // 128 + s_) % 4
            dst = B["vtok"][0:rows, blk, :, 0:64]
            is_last = tl["last"] and s_ == tl["nsub"] - 1
            if is_last:
                self.act("activation", W=[("pb", b), "vf"], out=B["vf"][0:rows, :], in_=pb[b][0:rows, 0:128], func=AF.Copy)
                self.dma("pool", "nvp", self.dout["nvp"][tl["seq"]], B["vf"][0:rows, :], R=["vf"], W=[("nvp", tl["seq"])])
            self.dve("tensor_copy", W=[("pb", b), "vtok"], out=dst, in_=ps)
        else:
            self.act("activation", W=[("pb", b), "vf"], out=B["vf"][0:rows, :], in_=pb[b][0:rows, 0:128], func=AF.Copy)
            self.dve("tensor_copy", W=[("pb", b), "vtok_s"], out=self.SB["vtok_s"][0:rows, :, 0:64], in_=ps)

    def qknorm(self, tl, ot, ps, b):
        B, pb, cols, col = self.B, self.pb, self.cols, self.col
        Tt = tl["T"]
        i = self._bankctr.get("qk", 0)
        self._bankctr["qk"] = i + 1
        raw, sq, lnv = B["raw"][i % 2], B["sq"][i % 2], B["lnv"][i % 2]
        kr, ks, kl = ("raw", 0), ("sq", 0), ("lnv", 0)
        self.act("activation", W=[("pb", b), kr], out=raw[:, 0:Tt], in_=ps, func=AF.Copy)
        self.act("activation", W=[("pb", b), ks], out=r32(sq[:, 0:Tt]), in_=ps, func=AF.Square)
        b2 = self.bank(2, 4, "tr")
        self.pe("matmul", R=[ks, "onesblk"], W=[("pb", b2)], out=pb[b2][:, 0:Tt], lhsT=r32(self.onesblk[:]), rhs=r32(sq[:, 0:Tt]),
                start=True, stop=True)
        self.act("activation", W=[("pb", b2), kl], out=lnv[:, 0:Tt], in_=pb[b2][:, 0:Tt], func=AF.Ln, bias=RMS_EPS, scale=1.0 / HD)
        self.act("activation", R=[kl], W=[kl], out=lnv[:, 0:Tt], in_=lnv[:, 0:Tt], func=AF.Exp, scale=-0.5)
        if ot < 4:
            g = cols[:, col["qgs"]:col["qgs"] + 1]
            dst = B["qT"][:, ot, 0:Tt] if tl["kind"] == "p" else self.SB["qT_s"][:, ot, 0:Tt]
            wk = "qT"
        else:
            g = cols[:, col["kg"]:col["kg"] + 1]
            dst = B["kT"][:, tl["t0"] % 512:tl["t0"] % 512 + Tt] if tl["kind"] == "p" else self.SB["kT_s"][:, 0:Tt]
            wk = "kT"
        self.dve("scalar_tensor_tensor", R=[kr, kl, "cols", "cols2"], W=[wk], out=dst, in0=raw[:, 0:Tt], scalar=g, in1=lnv[:, 0:Tt],
                 op0=ALU.mult, op1=ALU.mult)
        if ot == 4:
            if tl["kind"] == "p" and tl["last"]:
                c0, n = Tt - 128, 128
                out_ap, okey = self.dout["nkp"][tl["seq"]], ("nkp", tl["seq"])
            elif tl["kind"] == "s":
                c0, n = 0, self.NSAMP
                out_ap, okey = None, None
            else:
                return
            kf = B["kf"]
            self.dve("scalar_tensor_tensor", R=[kr, kl, "cols"], W=["kf"], out=kf[:, 0:n], in0=raw[:, c0:c0 + n], scalar=g,
                     in1=lnv[:, c0:c0 + n], op0=ALU.mult, op1=ALU.mult)
            b3 = self.bank(2, 4, "tr")
            self.pe("transpose", R=["kf", "ident_f"], W=[("pb", b3)], out=pb[b3][0:n, 0:128], in_=kf[:, 0:n], identity=self.ident_f[:])
            self.act("activation", W=[("pb", b3), "kft"], out=B["kft"][0:n, :], in_=pb[b3][0:n, 0:128], func=AF.Copy)
            if out_ap is not None:
                self.dma("pool", "nkp", out_ap, B["kft"][0:n, :], R=["kft"], W=[okey])

    def shiftmix(self, tl, i, ps, b):
        B, pb, cols, col = self.B, self.pb, self.cols, self.col
        Tt = tl["T"]
        n = self._bankctr.get("sm", 0)
        self._bankctr["sm"] = n + 1
        pc, df = B["pcur"][n % 2], B["diff"][n % 2]
        kp, kd = ("pcur", n % 2), ("diff", n % 2)
        mu = cols[:, col["mu"] + i:col["mu"] + i + 1]
        pm = B["pm"][:, i, 0:Tt] if tl["kind"] == "p" else self.SB["pm_s"][:, i, 0:Tt]
        kpm = ("pm", i)
        if tl["kind"] == "p":
            carry = B["carry"]
            if tl["first"]:
                self.pool("memset", W=[kp], ap=pc[:, 0:1], constant=0.0)
            else:
                self.pool("tensor_copy", R=[("carry", i)], W=[kp], out=pc[:, 0:1], in_=carry[:, i:i + 1])
            self.act("activation", W=[("pb", b), kp], out=pc[:, 1:Tt + 1], in_=ps, func=AF.Copy)
            self.pool("tensor_copy", R=[kp], W=[("carry", i)], out=carry[:, i:i + 1], in_=pc[:, Tt:Tt + 1])
            self.dve("tensor_tensor", R=[kp], W=[("pb", b), kd], out=df[:, 0:Tt], in0=pc[:, 0:Tt], in1=ps, op=ALU.subtract)
        else:
            NB, NS = self.NB, self.NSAMP
            self.act("activation", W=[("pb", b), kp], out=pc[:, 0:Tt], in_=ps, func=AF.Copy)
            pv = pc[:, 0:NS].rearrange("p (b t) -> p b t", t=4)
            dv = df[:, 0:NS].rearrange("p (b t) -> p b t", t=4)
            self.dve("tensor_tensor", R=[kp], W=[kd], out=dv[:, :, 1:4], in0=pv[:, :, 0:3], in1=pv[:, :, 1:4], op=ALU.subtract)
            self.dve("tensor_tensor", R=[kp], W=[kd], out=dv[:, :, 0:1], in0=pc[:, NS:NS + NB].unsqueeze(2), in1=pv[:, :, 0:1], op=ALU.subtract)
            Tt = NS
            ps = ps[:, 0:NS]
            pm = self.SB["pm_s"][:, i, 0:NS]
        self.dve("scalar_tensor_tensor", R=[kd, "cols"], W=[("pb", b), kpm], out=pm, in0=df[:, 0:Tt], scalar=mu, in1=ps,
                 op0=ALU.mult, op1=ALU.add)

    def tile_load(self, tl):
        tok0, Tt = tl["tok0"], tl["T"]
        self.dma("sp", "xa0", self.B["xin"][0][:, 0:tl["nsub"], :], self.din["xp"][tok0:tok0 + Tt, :].rearrange("(s p) d -> p s d", p=128),
                 W=[("xin", 0)])

    def tile_a(self, tl, nxt):
        B = self.B
        n = self._bankctr.get("tile", 0)
        self._bankctr["tile"] = n + 1
        tok0, Tt = tl["tok0"], tl["T"]
        self.project(tl)
        if nxt is not None:
            self.tile_load(nxt)
            self.norm1(nxt, 0)
        if self.stop_after == "proj":
            if n == 0:
                self.dbg("d_hT", B["hT"][:, :, :], ["hT"])
                self.dbg("d_qT", B["qT"][:, :, :], ["qT"])
                self.dbg("d_kT", B["kT"][:, 0:Tt], ["kT"])
                self.dbg("d_pm", B["pm"][:, :, :], [("pm", i) for i in range(14)])
                self.dbg("d_vtok", B["vtok"][:, 0:2, :, :], ["vtok"])
            return
        ga, ge = self.attention_prompt(tl), self.rwkv_elem(tl)
        alive = [ga, ge]
        while alive:
            for g_ in list(alive):
                try:
                    next(g_)
                except StopIteration:
                    alive.remove(g_)
                    if g_ is ga and nxt is not None:
                        self.transpose1(nxt, 0)
        if self.stop_after == "elem":
            if n == 0:
                self.dbg("d_AR", B["AR"][:], [("AR", j) for j in range(4)])
                self.dbg("d_BK", B["BK"][:], [("BK", j) for j in range(4)])
                self.dbg("d_bonus", B["bonus"][:], ["bonus"])
                self.dbg("d_mixT", B["mixT"][:], ["mixT"])
            return
        self.rwkv_chunks(tl)
        self.rwkv_combine(tl, B["pm"], B["mixT"], Tt)
        self.dma("pool", "mixst", self.mix_scr[:, :, tok0:tok0 + Tt].rearrange("k p t -> p k t"), B["mixT"][:, :, 0:Tt],
                 R=["mixT"], W=[("mixscr", tok0)])

    def attention_prompt(self, tl):
        B, pb = self.B, self.pb
        qT, kT, vtok = B["qT"], B["kT"], B["vtok"]
        for qb in range(tl["nsub"]):
            n = tl["t0"] // 128 + qb
            at = B["attn"][qb % 2]
            ka = ("attn", 0)
            q0 = qb * 128
            for kvh in range(2):
                kbs = [n - 1, n] if n > 0 else [n]
                ps_ = slice(kvh * 64, (kvh + 1) * 64)
                for ki, kb in enumerate(kbs):
                    b = self.bank(4, 6, "sc")
                    mt = 1 if kb == n else 0
                    self.pe("matmul", R=["ident_bf", "amask"], W=[("pb", b)], out=pb[b][:, :], lhsT=self.ident_bf[:],
                            rhs=self.amask[:, mt, :], start=True, stop=False)
                    self.pe("matmul", R=["kT", "qT"], W=[("pb", b)], out=pb[b][:, :].rearrange("p (j q) -> p j q", j=4),
                            lhsT=kT[ps_, (kb % 4) * 128:(kb % 4 + 1) * 128], rhs=qT[ps_, :, q0:q0 + 128], start=False, stop=True)
                    pt = B["pT"][kvh * 2 + ki]
                    self.act("activation", W=[("pb", b), ("pT", kvh * 2 + ki)], out=pt[:, :], in_=pb[b][:, :], func=AF.Exp)
                    yield
                bpv = self.bank(6, 8, "pv")
                pv = pb[bpv][:, 0:260].rearrange("p (j d) -> p j d", j=4)
                for j in range(4):
                    for ki, kb in enumerate(kbs):
                        pt = B["pT"][kvh * 2 + ki]
                        self.pe("matmul", R=[("pT", kvh * 2 + ki), "vtok"], W=[("pb", bpv)], out=pv[:, j, :],
                                lhsT=pt[:, j * 128:(j + 1) * 128], rhs=vtok[:, kb % 4, kvh, :], start=(ki == 0), stop=(ki == len(kbs) - 1))
                rden = B["rden"]
                self.dve("tensor_tensor", R=["esink"], W=[("pb", bpv), "rden"], out=rden[:, 0:4], in0=pv[:, :, 64],
                         in1=self.esink[:, kvh * 4:(kvh + 1) * 4], op=ALU.add)
                self.dve("reciprocal", R=["rden"], W=["rden"], out=rden[:, 4:8], in_=rden[:, 0:4])
                self.dve("tensor_tensor", R=["rden"], W=[("pb", bpv), ka],
                         out=at[:, kvh * 256:(kvh + 1) * 256].rearrange("p (j d) -> p j d", j=4), in0=pv[:, :, 0:64],
                         in1=rden[:, 4:8].unsqueeze(2).to_broadcast([128, 4, 64]), op=ALU.mult)
                yield
            self.attn_to_mix(at, ka, B["mixT"], q0, 128)
            yield

    def attn_to_mix(self, at, ka, mixT, q0, rows):
        pb = self.pb
        b = self.bank(2, 4, "tr")
        pbt = pb[b][:].bitcast(BF16).rearrange("p (k t) -> p k t", k=8)
        for k in range(4):
            self.pe("transpose", R=[ka, "ident_bf"], W=[("pb", b)], out=pbt[:, k, 0:rows], in_=at[0:rows, k * 128:(k + 1) * 128],
                    identity=self.ident_bf[0:rows, 0:rows])
        self.act("activation", W=[("pb", b), "mixT"], out=mixT[:, 0:4, q0:q0 + rows], in_=pbt[:, 0:4, 0:rows], func=AF.Copy)

    def rwkv_elem(self, tl, sample=False):
        B, pb, cols, col = self.B, self.pb, self.cols, self.col
        Tt = tl["T"] if not sample else self.NSAMP
        pm = B["pm"] if not sample else self.SB["pm_s"]
        C0 = float(np.exp(-0.5))
        tw, xar, sgg = B["tw"], B["xar"], B["sgg"]
        kpm = lambda i: ("pm", i)
        sgs = B["sg"]
        self.act("activation", R=[kpm(12)], W=["tw"], out=r32(tw[0:64, 0:Tt]), in_=pm[0:64, 12, 0:Tt], func=AF.Exp, scale=-2.0)
        self.act("activation", R=[kpm(13)], W=["sgg"], out=r32(sgg[:, 0:Tt]), in_=pm[:, 13, 0:Tt], func=AF.Exp, scale=-1.0)
        self.act("activation", R=["tw"], W=["tw"], out=r32(tw[0:64, 0:Tt]), in_=tw[0:64, 0:Tt], func=AF.Ln, bias=1.0)
        self.act("activation", R=["sgg"], W=["sgg"], out=r32(sgg[:, 0:Tt]), in_=sgg[:, 0:Tt], func=AF.Ln, bias=1.0)
        self.act("activation", R=["tw"], W=["tw"], out=r32(tw[0:64, 0:Tt]), in_=tw[0:64, 0:Tt], func=AF.Exp, scale=-1.0)
        self.act("activation", R=["sgg"], W=["sgg"], out=r32(sgg[:, 0:Tt]), in_=sgg[:, 0:Tt], func=AF.Exp, scale=-1.0)
        self.act("activation", R=[kpm(12)], W=["tw"], out=r32(xar[64:128, 0:Tt]), in_=pm[64:128, 12, 0:Tt], func=AF.Copy)
        self.dve("tensor_scalar", R=["tw"], W=["tw"], out=r32(tw[0:64, 0:Tt]), in0=tw[0:64, 0:Tt], scalar1=2.0, scalar2=-1.0,
                 op0=ALU.mult, op1=ALU.add)
        yield
        nch = Tt // 64
        v3 = lambda ap: ap.rearrange("p (c t) -> p c t", t=64)
        for j in range(4):
            xr, xk, xv = pm[:, j, 0:Tt], pm[:, 4 + j, 0:Tt], pm[:, 8 + j, 0:Tt]
            cj = lambda nm: cols[:, col[nm] + j:col[nm] + j + 1]
            js = slice(j * 128, (j + 1) * 128)
            sg, aa, Lc, eL, eN, rn, kkn, fac, kh, t2, rk = (B[k][:, 0:Tt] for k in ("sg", "aa", "Lc", "eL", "eN", "rn", "kkn", "fac", "kh", "t2", "rk"))
            b1 = self.bank(2, 4, "tr")
            self.pe("matmul", R=["tw", "wl"], W=[("pb", b1)], out=pb[b1][:, 0:Tt], lhsT=r32(self.wl[0:64, js]), rhs=r32(tw[0:64, 0:Tt]),
                    start=True, stop=True)
            b2 = self.bank(2, 4, "tr")
            self.pe("matmul", R=["tw", "wl"], W=[("pb", b2)], out=pb[b2][:, 0:Tt], lhsT=r32(self.wl[64:128, js]), rhs=r32(xar[64:128, 0:Tt]),
                    start=True, stop=True)
            self.act("activation", R=["cols2"], W=[("pb", b1), "sg"], out=sg, in_=pb[b1][:, 0:Tt], func=AF.Exp, scale=-1.0, bias=cj("nw0"))
            self.act("activation", R=["cols2"], W=[("pb", b2), "aa"], out=aa, in_=pb[b2][:, 0:Tt], func=AF.Exp, scale=-1.0, bias=cj("na0"))
            self.act("activation", R=["sg"], W=["sg"], out=sg, in_=sg, func=AF.Ln, bias=1.0)
            self.act("activation", R=["aa"], W=["aa"], out=aa, in_=aa, func=AF.Ln, bias=1.0)
            self.act("activation", R=["sg"], W=["sg"], out=sg, in_=sg, func=AF.Exp, scale=-1.0)
            self.act("activation", R=["aa"], W=["aa"], out=aa, in_=aa, func=AF.Exp, scale=-1.0)
            yield
            self.act("activation", R=[kpm(4 + j), "cols"], W=["rk"], out=r32(rk), in_=xk, func=AF.Square, scale=cj("kk"))
            b = self.bank(2, 4, "tr")
            self.pe("matmul", R=["rk", "onesblk"], W=[("pb", b)], out=pb[b][:, 0:Tt], lhsT=r32(self.onesblk[:]), rhs=r32(rk),
                    start=True, stop=True)
            self.act("activation", W=[("pb", b), "rn"], out=rn, in_=pb[b][:, 0:Tt], func=AF.Ln, bias=1e-24)
            self.act("activation", R=["rn"], W=["rn"], out=rn, in_=rn, func=AF.Exp, scale=-0.5)
            self.dve("scalar_tensor_tensor", R=[kpm(4 + j), "cols", "rn"], W=["kkn"], out=kkn, in0=xk, scalar=cj("kk"), in1=rn,
                     op0=ALU.mult, op1=ALU.mult)
            self.dve("tensor_scalar", R=["aa", "cols", "cols2"], W=["fac"], out=fac, in0=aa, scalar1=cj("ka"), scalar2=cj("omka"),
                     op0=ALU.mult, op1=ALU.add)
            self.dve("tensor_tensor", R=[kpm(4 + j), "fac"], W=["kh"], out=kh, in0=xk, in1=fac, op=ALU.mult)
            self.pool("tensor_tensor", R=["kkn", "aa"], W=["t2"], out=t2, in0=kkn, in1=aa, op=ALU.mult)
            yield
            self.dve("scalar_tensor_tensor", R=[kpm(j), "cols", "kh"], W=["rk"], out=r32(rk), in0=xr, scalar=cj("rk"), in1=kh,
                     op0=ALU.mult, op1=ALU.mult)
            b = self.bank(2, 4, "tr")
            self.pe("matmul", R=["rk", "onesblk"], W=[("pb", b)], out=pb[b][:, 0:Tt], lhsT=r32(self.onesblk[:]), rhs=r32(rk),
                    start=True, stop=True)
            bon = B["bonus"][:, j, 0:Tt] if not sample else self.SB["bonus_s"][:, j, 0:Tt]
            self.dve("tensor_tensor", R=[kpm(8 + j)], W=[("pb", b), "bonus"], out=bon, in0=pb[b][:, 0:Tt], in1=xv, op=ALU.mult)
            yield
            if sample:
                q6 = self.SB["q6"]
                self.act("activation", R=["sg"], W=[("q6", j)], out=q6[:, 1, j, 0:Tt], in_=sg, func=AF.Exp, scale=-C0)
                self.act("activation", R=[kpm(j)], W=[("q6", j)], out=q6[:, 0, j, 0:Tt], in_=xr, func=AF.Copy)
                self.act("activation", R=["kh"], W=[("q6", j)], out=q6[:, 2, j, 0:Tt], in_=kh, func=AF.Copy)
                self.act("activation", R=[kpm(8 + j)], W=[("q6", j)], out=q6[:, 3, j, 0:Tt], in_=xv, func=AF.Copy)
                self.act("activation", R=["kkn"], W=[("q6", j)], out=q6[:, 4, j, 0:Tt], in_=kkn, func=AF.Copy, scale=-1.0)
                self.act("activation", R=["t2"], W=[("q6", j)], out=q6[:, 5, j, 0:Tt], in_=t2, func=AF.Copy)
                continue
            self.dve("tensor_tensor_scan", R=["sg", "scanmask"], W=["Lc"], out=Lc, data0=self.scanmask[:, 0:Tt], data1=sg, initial=0.0,
                     op0=ALU.mult, op1=ALU.add)
            self.act("activation", R=["Lc"], W=["eL"], out=eL, in_=Lc, func=AF.Exp, scale=-C0)
            self.act("activation", R=["Lc"], W=["eN"], out=eN, in_=Lc, func=AF.Exp, scale=C0)
            AR, BK = B["AR"], B["BK"]
            kAR, kBK = ("AR", j), ("BK", j)
            self.dve("tensor_tensor", R=[kpm(j), "eL"], W=[kAR], out=r32(AR[:, j, :, 1, :]), in0=v3(xr), in1=v3(eL), op=ALU.mult)
            self.dve("scalar_tensor_tensor", R=["kkn", "eL"], W=[kAR], out=r32(AR[:, j, :, 0, 1:64]), in0=v3(kkn)[:, :, 1:64], scalar=-1.0,
                     in1=v3(eL)[:, :, 0:63], op0=ALU.mult, op1=ALU.mult)
            self.dve("tensor_scalar", R=["kkn"], W=[kAR], out=r32(AR[:, j, :, 0, 0:1]), in0=v3(kkn)[:, :, 0:1], scalar1=-1.0, scalar2=None,
                     op0=ALU.mult)
            self.dve("tensor_tensor", R=["t2", "eN"], W=[kBK], out=r32(BK[:, j, :, 0, :]), in0=v3(t2), in1=v3(eN), op=ALU.mult)
            self.dve("tensor_tensor", R=["kh", "eN"], W=[kBK], out=r32(BK[:, j, :, 1, :]), in0=v3(kh), in1=v3(eN), op=ALU.mult)
            self.pool("tensor_copy", R=["eL"], W=["eLC"], out=B["eLC"][:, j, 0:nch], in_=v3(eL)[:, :, 63])
            yield

    def rwkv_chunks(self, tl):
        B, pb = self.B, self.pb
        Tt = tl["T"]
        nch = Tt // 64
        AR, BK, pm = B["AR"], B["BK"], B["pm"]
        idf = self.ident_f
        if tl["first"]:
            self.hs = 0
            self.dve("tensor_scalar", R=["scanmask"], W=[("HS", 0, g_, q_) for g_ in range(2) for q_ in range(2)], out=r32(B["HS"][0][:].rearrange("p j v -> p (j v)")),
                     in0=self.scanmask[:, 0:256], scalar1=0.0, scalar2=None, op0=ALU.mult)
        self.hs0 = self.hs
        self.hs = (self.hs + nch) % 2
        hd = lambda h: (h // 2, slice((h % 2) * 64, (h % 2) * 64 + 64))
        fl = lambda ap: ap.rearrange("p a t -> p (a t)")
        self._Pm = {}

        def chunk_pre(c):
            cs = c % 2
            P_ = slice(cs * 64, cs * 64 + 64)
            Z, UV, BKt = B["Z"][cs], B["UV"][cs], B["BKtok"][cs]
            kZ, kUV, kBKt = ("Z", cs), ("UV", cs), ("BKtok", cs)
            keyc = lambda nm: (nm, cs)
            bs = [self.bank(0, 6, "ch"), self.bank(0, 6, "ch")]
            for hh in range(4):
                for g in range(2):
                    h = g + 2 * hh
                    j, ps_ = hd(h)
                    self.pe("matmul", R=[("AR", j), ("BK", j)], W=[("pb", bs[g])], out=pb[bs[g]][:, hh * 128:(hh + 1) * 128],
                            lhsT=r32(fl(BK[ps_, j, c, :, :])), rhs=r32(fl(AR[ps_, j, c, :, :])), start=True, stop=True)
            for g in range(2):
                pv = pb[bs[g]][:, :].rearrange("p (h t) -> p h t", h=4)
                self.dve("tensor_tensor", R=["maskZ"], W=[("pb", bs[g]), kZ], out=r32(Z[:, g::2, :]), in0=pv,
                         in1=self.maskZ[:].unsqueeze(1).to_broadcast([128, 4, 128]), op=ALU.mult)
            yield
            G0, Sa, Sb, Ga, Gb, GTa, GTb, Xsb = (B[k] for k in ("G0", "Sa", "Sb", "Ga", "Gb", "GTa", "GTb", "Xsb"))
            self.dve("tensor_copy", R=[kZ], W=[keyc("G0")], out=r32(G0[P_, :, :]), in_=Z[0:64, :, 0:64])
            self.dve("tensor_tensor", R=[kZ, "ident_f"], W=[keyc("Sa")], out=r32(Sa[P_, :, :]), in0=Z[0:64, :, 0:64],
                     in1=idf[0:64, 0:64].unsqueeze(1).to_broadcast([64, 8, 64]), op=ALU.add)
            yield
            bs = [self.bank(0, 6, "ch"), self.bank(0, 6, "ch")]
            for hh in range(4):
                for g in range(2):
                    h = g + 2 * hh
                    j, ps_ = hd(h)
                    self.pe("matmul", R=[("AR", j), ("BK", j)], W=[("pb", bs[g])], out=pb[bs[g]][0:64, hh * 64:(hh + 1) * 64],
                            lhsT=r32(AR[ps_, j, c, 0, :]), rhs=r32(BK[ps_, j, c, 0, :]), start=True, stop=True)
            for g in range(2):
                self.dve("tensor_tensor", R=["maskGT"], W=[("pb", bs[g]), keyc("GTa")], out=r32(GTa[P_, g::2, :]),
                         in0=pb[bs[g]][0:64, 0:256].rearrange("p (h t) -> p h t", h=4),
                         in1=self.maskGT[:].unsqueeze(1).to_broadcast([64, 4, 64]), op=ALU.mult)
            yield
            Gp, GTp, Sp = ("G0", G0), ("GTa", GTa), ("Sa", Sa)
            Gn, GTn, Sn = ("Ga", Ga), ("GTb", GTb), ("Sb", Sb)
            for k in range(1, 6):
                def mm(L, R_, box):
                    bb = self.bank(0, 6, "ch")
                    box[0] = bb
                    for h in range(8):
                        self.pe("matmul", R=[keyc(L[0]), keyc(R_[0])], W=[("pb", bb)], out=pb[bb][0:64, h * 64:(h + 1) * 64],
                                lhsT=r32(L[1][P_, h, :]), rhs=r32(R_[1][P_, h, :]), start=True, stop=True)
                        yield
                box = [0]
                if k <= 4:
                    yield from mm(GTp, Gp, box)
                    bb = box[0]
                    self.act("activation", W=[("pb", bb), keyc(Gn[0])], out=r32(Gn[1][P_, :, :]),
                             in_=pb[bb][0:64, :].rearrange("p (h t) -> p h t", h=8), func=AF.Copy)
                yield from mm(Gp, GTp, box)
                bb = box[0]
                self.act("activation", W=[("pb", bb), keyc(GTn[0])], out=r32(GTn[1][P_, :, :]),
                         in_=pb[bb][0:64, :].rearrange("p (h t) -> p h t", h=8), func=AF.Copy)
                yield from mm(GTn, Sp, box)
                bb = box[0]
                if cs == 0:
                    self.dve("tensor_tensor", R=[keyc(Sp[0])], W=[("pb", bb), keyc(Sn[0])], out=r32(Sn[1][P_, :, :]),
                             in0=pb[bb][0:64, :].rearrange("p (h t) -> p h t", h=8), in1=Sp[1][P_, :, :], op=ALU.add)
                else:
                    self.act("activation", W=[("pb", bb), keyc(Sn[0])], out=r32(Sn[1][P_, :, :]),
                             in_=pb[bb][0:64, :].rearrange("p (h t) -> p h t", h=8), func=AF.Copy)
                    self.dve("tensor_tensor", R=[keyc(Sp[0])], W=[keyc(Sn[0])], out=r32(Sn[1][P_, :, :]), in0=Sn[1][P_, :, :],
                             in1=Sp[1][P_, :, :], op=ALU.add)
                yield
                Gp, Gn = Gn, (("Gb", Gb) if Gn[0] == "Ga" else ("Ga", Ga))
                GTp, GTn = GTn, (("GTa", GTa) if GTn[0] == "GTb" else ("GTb", GTb))
                Sp, Sn = Sn, (("Sa", Sa) if Sn[0] == "Sb" else ("Sb", Sb))
            Pm = Sp
            self._Pm[c] = Pm
            yield
            bs = [self.bank(0, 6, "ch"), self.bank(0, 6, "ch")]
            for hh in range(4):
                for g in range(2):
                    h = g + 2 * hh
                    j, ps_ = hd(h)
                    self.pe("transpose", R=[("BK", j), "ident_f"], W=[("pb", bs[g])], out=pb[bs[g]][:, hh * 64:(hh + 1) * 64],
                            in_=fl(BK[ps_, j, c, :, :]), identity=idf[ps_, ps_])
            for g in range(2):
                self.act("activation", W=[("pb", bs[g]), kBKt], out=r32(BKt[:, g::2, :]),
                         in_=pb[bs[g]][:, 0:256].rearrange("p (h t) -> p h t", h=4), func=AF.Copy)
            yield
            bs = [self.bank(0, 6, "ch"), self.bank(0, 6, "ch")]
            for hh in range(4):
                for g in range(2):
                    h = g + 2 * hh
                    j, ps_ = hd(h)
                    self.pe("transpose", R=[("pm", 8 + j), "ident_f"], W=[("pb", bs[g])], out=pb[bs[g]][0:64, hh * 64:(hh + 1) * 64],
                            in_=pm[ps_, 8 + j, c * 64:(c + 1) * 64], identity=idf[ps_, ps_])
            for g in range(2):
                self.act("activation", W=[("pb", bs[g]), ("UV", cs, g, 0), ("UV", cs, g, 1)], out=r32(UV[64:128, g::2, :]),
                         in_=pb[bs[g]][0:64, 0:256].rearrange("p (h t) -> p h t", h=4), func=AF.Copy)
            self.dve("tensor_scalar", R=["scanmask"], W=[("UV", cs, g_, q_) for g_ in range(2) for q_ in range(2)], out=r32(UV[0:64, :, :].rearrange("p (a h) v -> p a (h v)", a=2)),
                     in0=self.scanmask[0:64, 0:256].unsqueeze(1).to_broadcast([64, 2, 256]), scalar1=0.0,
                     scalar2=None, op0=ALU.mult)

        NQ_ = 2

        def chain_group(c, g, q):
            cs = c % 2
            P_ = slice(cs * 64, cs * 64 + 64)
            gs = slice(g * 64, g * 64 + 64)
            Z, UV, BKt = B["Z"][cs], B["UV"][cs], B["BKtok"][cs]
            kZ, kUV, kBKt = ("Z", cs), ("UV", cs, g, q), ("BKtok", cs)
            kX = ("Xsb", cs, g, q)
            nh = 4 // NQ_
            hhs = range(q * nh, (q + 1) * nh)
            js = slice(q * nh, (q + 1) * nh)
            hsl = slice(g + 2 * q * nh, g + 2 * ((q + 1) * nh - 1) + 1, 2)
            cw = nh * 64
            Xsb = B["Xsb"]
            Pm = self._Pm[c]
            hi = (self.hs0 + c) % 2
            HSo, HSn = B["HS"][hi], B["HS"][1 - hi]
            kHo, kHn = ("HS", hi, g, q), ("HS", 1 - hi, g, q)
            v4 = lambda ap: ap.rearrange("p (h t) -> p h t", h=nh)
            bx = self.bank(0, 6, "ch")
            for hh in hhs:
                h = g + 2 * hh
                self.pe("matmul", R=[("AR", hh), kHo], W=[("pb", bx)], out=pb[bx][0:64, (hh - q * nh) * 64:(hh - q * nh + 1) * 64], lhsT=r32(AR[gs, hh, c, 0, :]),
                        rhs=r32(HSo[gs, hh, :]), start=True, stop=False)
                self.pe("matmul", R=[kZ, kUV], W=[("pb", bx)], out=pb[bx][0:64, (hh - q * nh) * 64:(hh - q * nh + 1) * 64], lhsT=r32(Z[:, h, 0:64]),
                        rhs=r32(UV[:, h, :]), start=False, stop=True)
            self.act("activation", W=[("pb", bx), kX], out=r32(Xsb[P_, hsl, :]), in_=v4(pb[bx][0:64, 0:cw]), func=AF.Copy)
            yield
            bu = self.bank(0, 6, "ch")
            for hh in hhs:
                h = g + 2 * hh
                self.pe("matmul", R=[(Pm[0], cs), kX], W=[("pb", bu)], out=pb[bu][0:64, (hh - q * nh) * 64:(hh - q * nh + 1) * 64],
                        lhsT=r32(Pm[1][P_, h, :]), rhs=r32(Xsb[P_, h, :]), start=True, stop=True)
            self.dve("tensor_copy", W=[("pb", bu), kUV], out=r32(UV[0:64, hsl, :]), in_=v4(pb[bu][0:64, 0:cw]))
            yield
            bh = self.bank(0, 6, "ch")
            for hh in hhs:
                h = g + 2 * hh
                self.pe("matmul", R=[kBKt, kUV], W=[("pb", bh)], out=pb[bh][0:64, (hh - q * nh) * 64:(hh - q * nh + 1) * 64], lhsT=r32(BKt[:, h, :]),
                        rhs=r32(UV[:, h, :]), start=True, stop=True)
            by = self.bank(0, 6, "ch")
            for hh in hhs:
                h = g + 2 * hh
                self.pe("matmul", R=[("AR", hh), kHo], W=[("pb", by)], out=pb[by][0:64, (hh - q * nh) * 64:(hh - q * nh + 1) * 64], lhsT=r32(AR[gs, hh, c, 1, :]),
                        rhs=r32(HSo[gs, hh, :]), start=True, stop=False)
                self.pe("matmul", R=[kZ, kUV], W=[("pb", by)], out=pb[by][0:64, (hh - q * nh) * 64:(hh - q * nh + 1) * 64], lhsT=r32(Z[:, h, 64:128]),
                        rhs=r32(UV[:, h, :]), start=False, stop=True)
            Ht, Hd = B["Htmp"], B["Hd"]
            self.act("activation", W=[("pb", bh), ("Htmp", g, q)], out=Ht[gs, js, :], in_=v4(pb[bh][0:64, 0:cw]), func=AF.Copy)
            self.act("activation", W=[("pb", by), ("ysb", c, g, q)], out=B["ysb"][:, c, hsl, :], in_=v4(pb[by][0:64, 0:cw]), func=AF.Copy)
            self.pool("tensor_tensor", R=[("Htmp", g, q), kHo], W=[("Hd", g, q)], out=Hd[gs, js, :], in0=Ht[gs, js, :], in1=HSo[gs, js, :], op=ALU.add)
            self.dve("tensor_tensor", R=[("Hd", g, q), "eLC"], W=[kHn], out=r32(HSn[gs, js, :]), in0=Hd[gs, js, :],
                     in1=B["eLC"][gs, js, c:c + 1].to_broadcast([64, nh, 64]), op=ALU.mult)
            yield

        for c0 in range(0, nch, 2):
            gens = [chunk_pre(c) for c in range(c0, min(c0 + 2, nch))]
            alive = list(gens)
            while alive:
                for g_ in list(alive):
                    try:
                        next(g_)
                    except StopIteration:
                        alive.remove(g_)
            def par_chain(g, q):
                for c in range(c0, min(c0 + 2, nch)):
                    yield from chain_group(c, g, q)
            alive = [par_chain(g, q) for q in range(NQ_) for g in range(2)]
            while alive:
                for g_ in list(alive):
                    try:
                        next(g_)
                    except StopIteration:
                        alive.remove(g_)
        ysq = B["AR"][0:64].rearrange("p j c a t -> p (j c a t)").rearrange("p (c h v) -> p c h v", c=nch, h=8)
        self.gn_transpose(B["ysb"], ysq, nch, [("ysb", c, g, q) for c in range(nch) for g in range(2) for q in range(2)], [("AR", j) for j in range(4)])
        if tl["last"]:
            HSf = B["HS"][self.hs]
            b = self.bank(0, 6, "ch")
            for j in range(4):
                self.pe("transpose", R=[("HS", self.hs, g_, q_) for g_ in range(2) for q_ in range(2)] + ["ident_f"], W=[("pb", b)], out=pb[b][0:64, j * 128:(j + 1) * 128], in_=HSf[:, j, :],
                        identity=idf[:])
            so = B["stout"]
            self.act("activation", W=[("pb", b), "stout"], out=so[:].rearrange("p h c -> p (h c)"), in_=pb[b][0:64, :], func=AF.Copy)
            q = tl["seq"]
            self.dma("pool", "nwp", self.dout["nwp"][q * 512:(q + 1) * 512, :].rearrange("(h v) c -> v h c", h=8), so[:], R=["stout"],
                     W=[("nwp", q)])

    def gn_transpose(self, ysb, ysq, nch, kys, kar):
        B, pb, idf = self.B, self.pb, self.ident_f
        gst = B["gst"]
        n8 = nch * 8
        yv = ysb[:, 0:nch, :, :].rearrange("p c h v -> p (c h) v")
        qv = ysq[:, 0:nch, :, :].rearrange("p c h v -> p (c h) v")
        self.dve("tensor_reduce", R=kys, W=["gst"], out=gst[:, 0:n8], in_=yv, axis=AX.X, op=ALU.add)
        self.act("activation", R=kys, W=kar, out=r32(qv), in_=yv, func=AF.Square)
        self.dve("tensor_reduce", R=kar, W=["gst"], out=gst[:, 32:32 + n8], in_=qv, axis=AX.X, op=ALU.add)
        self.dve("tensor_scalar", R=["gst"], W=["gst"], out=gst[:, 0:n8], in0=gst[:, 0:n8], scalar1=1.0 / 64, scalar2=None, op0=ALU.mult)
        self.dve("tensor_tensor", R=["gst"], W=["gst"], out=gst[:, 64:64 + n8], in0=gst[:, 0:n8], in1=gst[:, 0:n8], op=ALU.mult)
        self.dve("scalar_tensor_tensor", R=["gst"], W=["gst"], out=gst[:, 32:32 + n8], in0=gst[:, 32:32 + n8], scalar=1.0 / 64,
                 in1=gst[:, 64:64 + n8], op0=ALU.mult, op1=ALU.subtract)
        self.act("activation", R=["gst"], W=["gst"], out=gst[:, 32:32 + n8], in_=gst[:, 32:32 + n8], func=AF.Ln, bias=GN_EPS)
        self.act("activation", R=["gst"], W=["gst"], out=gst[:, 32:32 + n8], in_=gst[:, 32:32 + n8], func=AF.Exp, scale=-0.5)
        self.dve("tensor_tensor", R=["gst"] + kys, W=kys, out=yv, in0=yv, in1=gst[:, 0:n8].unsqueeze(2).to_broadcast([64, n8, 64]),
                 op=ALU.subtract)
        self.pool("tensor_tensor", R=["gst"] + kys, W=kys, out=yv, in0=yv, in1=gst[:, 32:32 + n8].unsqueeze(2).to_broadcast([64, n8, 64]),
                  op=ALU.mult)
        for c in range(nch):
            for j in range(4):
                bt = 6 + j // 2
                self.pe("transpose", R=kys + ["ident_f"], W=[("pb", bt)], out=pb[bt][:, (j % 2) * 256 + c * 64:(j % 2) * 256 + (c + 1) * 64],
                        in_=ysb[:, c, 2 * j:2 * j + 2, :].rearrange("p h v -> p (h v)"), identity=idf[0:64, 0:64])

    def rwkv_combine(self, tl, pm, mixT, Tt):
        B, pb, cols, col = self.B, self.pb, self.cols, self.col
        sgg = B["sgg"]
        for j in range(4):
            bt = 6 + j // 2
            cj = lambda nm: cols[:, col[nm] + j:col[nm] + j + 1]
            yg, yb = B["yg"][:, 0:Tt], B["yb"][:, 0:Tt]
            bon = B["bonus"][:, j, 0:Tt] if tl["kind"] == "p" else self.SB["bonus_s"][:, j, 0:Tt]
            self.dve("tensor_scalar", R=["cols"], W=[("pb", bt), "yg"], out=yg, in0=pb[bt][:, (j % 2) * 256:(j % 2) * 256 + Tt],
                     scalar1=cj("lng"), scalar2=cj("lnb"), op0=ALU.mult, op1=ALU.add)
            self.pool("tensor_tensor", R=["yg", "bonus"], W=["yb"], out=yb, in0=yg, in1=bon, op=ALU.add)
            b = self.bank(2, 4, "tr")
            self.pe("matmul", R=["sgg", "gup"], W=[("pb", b)], out=pb[b][:, 0:Tt], lhsT=r32(self.gup[:, j * 128:(j + 1) * 128]),
                    rhs=r32(sgg[:, 0:Tt]), start=True, stop=True)
            self.dve("tensor_tensor", R=["yb"], W=[("pb", b), "mixT"], out=mixT[:, 4 + j, 0:Tt], in0=pb[b][:, 0:Tt], in1=yb, op=ALU.mult)

    def tiles_b(self):
        tl = []
        for q in range(self.NSEQ):
            for t0 in range(0, self.S, self.T):
                tl.append(dict(kind="p", tok0=q * self.S + t0, T=self.T, nsub=self.T // 128, rows=128))
        tl.append(dict(kind="s", tok0=self.NSEQ * self.S, T=self.NSAMP, nsub=1, rows=self.NSAMP))
        return tl

    def phase_b(self):
        ar = self.ar
        ar.reset(self.shared_base)
        A = ar.alloc
        T = self.T
        w_up_sb = A("w_up_sb", [128, 8, DFF], BF16)
        w_dn_sb = A("w_dn_sb", [128, 32, D], BF16)
        if self.debug == "phaseB":
            self.load_weight_bf16(w_up_sb, self.din["w_up"], 8, DFF, "up", "w_up")
            self.load_weight_bf16(w_dn_sb, self.din["w_dn"], 32, D, "dn", "w_dn")
        else:
            vu = self.w_up_bf.rearrange("(kt p) n -> p kt n", p=128)
            for kt in range(8):
                self.dma("sp" if kt % 2 == 0 else "act", "w_up", w_up_sb[:, kt, :], vu[:, kt, :], R=["w_up_bf"], W=["w_up"])
            vd = self.w_dn_bf.rearrange("(kt p) n -> p kt n", p=128)
            for k4 in range(0, 32, 4):
                self.dma("sp" if (k4 // 4) % 2 == 0 else "act", "w_dn", w_dn_sb[:, k4:k4 + 4, :], vd[:, k4:k4 + 4, :], R=["w_dn_bf"], W=["w_dn"])
        mixt = [A(f"mixt{i}", [128, 8, T], BF16) for i in range(2)]
        xt = [A(f"xtb{i}", [128, T // 128, D], F32) for i in range(2)]
        h2 = [A(f"h2_{i}", [128, D], BF16) for i in range(2)]
        h2T = A("h2T", [128, 8, T], BF16)
        actf = [A(f"actf{i}", [128, T], F32) for i in range(4)]
        actT = A("actT", [128, 32, T], BF16)
        st = A("statb", [128, 8], F32)
        cols, cg2 = self.cols, self.col["g2"]
        pb = self.pb
        w_out_sb = self.w_out_sb
        nb = [0]

        def bank(lo, hi):
            b = lo + nb[0] % (hi - lo)
            nb[0] += 1
            return b

        tiles = self.tiles_b()

        def stage_a(it):
            tl = tiles[it]
            sl = it % 2
            Tt, nsub, rows, tok0 = tl["T"], tl["nsub"], tl["rows"], tl["tok0"]
            mt, x_ = mixt[sl], xt[sl]
            self.dma("sp", f"mixb{sl}", mt[:, :, 0:Tt], self.mix_scr[:, :, tok0:tok0 + Tt].rearrange("k p t -> p k t"),
                     R=[("mixscr", tok0)], W=[("mixt", sl)])
            if tl["kind"] == "p":
                xsrc = self.din["xp"][tok0:tok0 + Tt, :].rearrange("(s p) d -> p s d", p=128)
            else:
                xsrc = self.din["xs"][0:rows, :].rearrange("(s p) d -> p s d", s=1)
            self.dma("sp", f"xb{sl}", x_[0:rows, 0:nsub, :], xsrc, W=[("xtb", sl)])
            for s_ in range(nsub):
                c0 = s_ * 128
                for half in range(2):
                    b = bank(0, 2)
                    hs = slice(half * 512, (half + 1) * 512)
                    for kt in range(8):
                        self.pe("matmul", R=[("mixt", sl), "w_out"], W=[("pb", b)], out=pb[b][0:rows, :],
                                lhsT=mt[:, kt, c0:c0 + rows], rhs=w_out_sb[:, kt, hs], start=(kt == 0), stop=(kt == 7))
                    self.dve("tensor_tensor", R=[("xtb", sl)], W=[("pb", b), ("xtb", sl)], out=x_[0:rows, s_, hs],
                             in0=pb[b][0:rows, :], in1=x_[0:rows, s_, hs], op=ALU.add)
                hh = h2[s_ % 2]
                kh = ("h2", s_ % 2)
                sc = st[:, 4 * ((2 * it + s_) % 2):4 * ((2 * it + s_) % 2) + 4]
                self.act("activation", R=[("xtb", sl)], W=[kh, "statb"], out=hh[0:rows, :], in_=x_[0:rows, s_, :],
                         func=AF.Square, accum_out=sc[0:rows, 0:1])
                self.act("activation", R=["statb"], W=["statb"], out=sc[0:rows, 1:2], in_=sc[0:rows, 0:1], func=AF.Ln,
                         bias=RMS_EPS, scale=1.0 / D)
                self.act("activation", R=["statb"], W=["statb"], out=sc[0:rows, 2:3], in_=sc[0:rows, 1:2], func=AF.Exp, scale=-0.5)
                self.act("activation", R=[("xtb", sl), "statb"], W=[kh], out=hh[0:rows, :], in_=x_[0:rows, s_, :],
                         func=AF.Copy, scale=sc[0:rows, 2:3])

        def stage_t(it):
            tl = tiles[it]
            nsub, rows = tl["nsub"], tl["rows"]
            for s_ in range(nsub):
                c0 = s_ * 128
                hh = h2[s_ % 2]
                kh = ("h2", s_ % 2)
                b = bank(6, 8)
                pbt = pb[b][:].bitcast(BF16).rearrange("p (k t) -> p k t", k=8)
                for kt in range(8):
                    self.pe("transpose", R=[kh, "ident_bf"], W=[("pb", b)], out=pbt[:, kt, 0:rows],
                            in_=hh[0:rows, kt * 128:(kt + 1) * 128], identity=self.ident_bf[0:rows, 0:rows])
                self.dve("tensor_tensor", R=["cols"], W=[("pb", b), "h2T"], out=h2T[:, :, c0:c0 + rows], in0=pbt[:, :, 0:rows],
                         in1=cols[:, cg2:cg2 + 8].unsqueeze(2).to_broadcast([128, 8, rows]), op=ALU.mult)

        def stage_u(it):
            Tt = tiles[it]["T"]
            for hp in range(16):
                b = 2 + hp % 4
                for hf in range(2):
                    ht = 2 * hp + hf
                    for kt in range(8):
                        self.pe("matmul", R=["w_up", "h2T"], W=[("pb", b)], out=pb[b][:, hf * 256:hf * 256 + Tt],
                                lhsT=w_up_sb[:, kt, ht * 128:(ht + 1) * 128], rhs=h2T[:, kt, 0:Tt], start=(kt == 0), stop=(kt == 7))
                for hf in range(2):
                    ht = 2 * hp + hf
                    af = actf[ht % 4]
                    self.act("activation", W=[("pb", b), ("actf", ht % 4)], out=af[:, 0:Tt], in_=pb[b][:, hf * 256:hf * 256 + Tt],
                             func=AF.Relu)
                    self.pool("tensor_tensor", R=[("actf", ht % 4)], W=[("actT", ht)], out=actT[:, ht, 0:Tt], in0=af[:, 0:Tt],
                              in1=af[:, 0:Tt], op=ALU.mult)

        def stage_d(it):
            tl = tiles[it]
            sl = it % 2
            Tt, nsub, rows, tok0 = tl["T"], tl["nsub"], tl["rows"], tl["tok0"]
            x_ = xt[sl]
            if tl["kind"] == "p":
                ydst = self.dout["yp"][tok0:tok0 + Tt, :].rearrange("(s p) d -> p s d", p=128)
            else:
                ydst = self.dout["ys"][0:rows, :].rearrange("(s p) d -> p s d", s=1)
            for s_ in range(nsub):
                c0 = s_ * 128
                for half in range(2):
                    b = bank(0, 2)
                    hs = slice(half * 512, (half + 1) * 512)
                    for ht in range(32):
                        self.pe("matmul", R=[("actT", ht), "w_dn"], W=[("pb", b)], out=pb[b][0:rows, :],
                                lhsT=actT[:, ht, c0:c0 + rows], rhs=w_dn_sb[:, ht, hs], start=(ht == 0), stop=(ht == 31))
                    self.dve("tensor_tensor", R=[("xtb", sl)], W=[("pb", b), ("xtb", sl)], out=x_[0:rows, s_, hs],
                             in0=pb[b][0:rows, :], in1=x_[0:rows, s_, hs], op=ALU.add)
            self.dma("pool", f"yb{sl}", ydst, x_[0:rows, 0:nsub, :], R=[("xtb", sl)], W=[("yout", it)])

        n = len(tiles)
        stage_a(0)
        stage_t(0)
        for it in range(n):
            stage_u(it)
            if it + 1 < n:
                stage_a(it + 1)
            stage_d(it)
            if it + 1 < n:
                stage_t(it + 1)

    def build(self):
        from contextlib import ExitStack
        self.declare_dram()
        self.setup()
        self.load_small()
        self.load_weight_bf16(self.w_out_sb, self.din["w_out"], 8, D, "out", "w_out")
        if self.debug != "phaseB":
            self.phase_a()
            self.s.barrier()
        self.phase_b()
        with ExitStack() as es:
            self.s.emit(es)
        return self.nc


def _perm_w_in():
    q = []
    for j in range(4):
        q += list(range(j * 64, (j + 1) * 64)) + list(range((j + 4) * 64, (j + 5) * 64))
    k = list(range(512, 640))
    v = list(range(640, 768))
    o = 768
    r = list(range(o, o + 512))
    wd = list(range(o + 512, o + 576))
    kr = list(range(o + 576, o + 1088))
    vr = list(range(o + 1088, o + 1600))
    ad = list(range(o + 1600, o + 1664))
    gd = list(range(o + 1664, o + 1792))
    return np.array(q + k + v + r + kr + vr + wd + ad + gd)


def prep_core_inputs(inp, core, NSEQ, NB):
    f = lambda a: np.ascontiguousarray(np.asarray(a, dtype=np.float32))
    perm = _perm_w_in()
    S = inp["x_prompt"].shape[1]
    xp = f(inp["x_prompt"][core * NSEQ:(core + 1) * NSEQ]).reshape(NSEQ * S, D)
    bs = slice(core * NB, (core + 1) * NB)
    xs = np.concatenate([f(inp["x_sample"][bs]).reshape(NB * 4, D), f(inp["state_shift"][0][bs])], 0)
    d = dict(
        xp=xp, xs=f(xs),
        ck=f(inp["cache_k"][0][bs]).reshape(NB, 128, 128), cv=f(inp["cache_v"][0][bs]).reshape(NB, 128, 128),
        swkv=f(inp["state_wkv"][0][bs]).reshape(NB * 8, 4096),
        w_in=f(np.asarray(inp["w_in"][0])[:, perm]), mu=f(np.asarray(inp["rwkv_mu"][0])[perm[768:] - 768]),
        w_out=f(inp["w_out"][0]), w_up=f(inp["w_ff_up"][0]), w_dn=f(inp["w_ff_down"][0]),
        g1=f(inp["norm1_g"][0]), g2=f(inp["norm2_g"][0]), qg=f(inp["q_norm_g"][0]), kg=f(inp["k_norm_g"][0]),
        sinks=f(inp["attn_sinks"][0]), w0=f(inp["w_decay_0"][0]), wdu=f(inp["w_decay_up"][0]), a0=f(inp["a_0"][0]),
        aup=f(inp["a_up"][0]), gup=f(inp["g_up"][0]), kk=f(inp["k_k"][0]), ka=f(inp["k_a"][0]),
        rk=f(np.asarray(inp["r_k"][0]).reshape(-1)), lng=f(inp["ln_x_g"][0]), lnb=f(inp["ln_x_b"][0]),
    )
    return d


_PROG = {}


def kernel(**inputs):
    NC = 8
    Bp, S = inputs["x_prompt"].shape[0], inputs["x_prompt"].shape[1]
    Bs = inputs["x_sample"].shape[0]
    NSEQ, NB = Bp // NC, Bs // NC
    key = (NSEQ, S, NB)
    if key not in _PROG:
        _PROG[key] = K(NSEQ, S, NB).build()
    nc = _PROG[key]
    in_maps = [prep_core_inputs(inputs, c, NSEQ, NB) for c in range(NC)]
    res = run_bass_kernel_spmd(nc, in_maps, core_ids=list(range(NC)))
    R = res.results
    cat = lambda name: np.concatenate([np.asarray(r[name]) for r in R], 0)
    yp = cat("yp").reshape(Bp, S, D)
    ys = cat("ys").reshape(Bs, 4, D)
    nkp = cat("nkp").reshape(1, Bp, 128, 2, 64)
    nvp = cat("nvp").reshape(1, Bp, 128, 2, 64)
    nwp = cat("nwp").reshape(1, Bp, 8, 64, 64)
    nsp = cat("nsp").reshape(1, Bp, D)
    nks = cat("nks").reshape(1, Bs, 128, 2, 64)
    nvs = cat("nvs").reshape(1, Bs, 128, 2, 64)
    nws = cat("nws").reshape(1, Bs, 8, 64, 64)
    nss = cat("nss").reshape(1, Bs, D)
    return tuple(np.ascontiguousarray(a, dtype=np.float32) for a in (yp, ys, nkp, nvp, nwp, nsp, nks, nvs, nws, nss))
```

```python
import numpy as np
import concourse.bass as bass
import concourse.mybir as mybir
from concourse.bass_utils import run_bass_kernel_spmd

F32 = mybir.dt.float32
F32R = mybir.dt.float32r
BF16 = mybir.dt.bfloat16
AF = mybir.ActivationFunctionType
ALU = mybir.AluOpType
AX = mybir.AxisListType

D = 1024
HD = 64
NQ = 8
NKV = 2
WINDOW = 128
RW = 512
NH = 8
DFF = 4096
INW = 2560
RMS_EPS = 1e-6
GN_EPS = 64e-5
NEG = -30000.0
ENGS = ("pe", "act", "dve", "pool", "sp")
BLK = {"pe": "tensor", "act": "scalar", "dve": "vector", "pool": "gpsimd", "sp": "sync"}
SAME_ENG_SYNC = True


class Op:
    __slots__ = ("eng", "fn", "deps", "ddeps", "sig", "val", "sem", "is_dma", "semname", "idx")


class Sched:
    def __init__(self, nc):
        self.nc = nc
        self.streams = {e: [] for e in ENGS}
        self.lastw = {}
        self.readers = {}
        self.pending = {e: ([], {}) for e in ENGS}
        self.dma_count = {}
        self.ncomp = {}

    def add(self, eng, fn, r=(), w=(), dma=None, out=False):
        op = Op()
        op.eng, op.fn, op.sig, op.val, op.sem = eng, fn, False, 0, None
        op.is_dma = dma is not None
        op.semname = dma
        op.idx = self.ncomp.get(eng, 0)
        if not op.is_dma:
            self.ncomp[eng] = op.idx + 1
        deps = [(d, False) for d in self.pending[eng][0]]
        ddeps = dict(self.pending[eng][1])
        self.pending[eng] = ([], {})
        for k in r:
            lw = self.lastw.get(k)
            if lw is not None:
                deps.append((lw, True))
        for k in w:
            lw = self.lastw.get(k)
            if lw is not None:
                deps.append((lw, False))
            deps.extend((x, False) for x in self.readers.get(k, {}).values())
        keep = {}
        for d, raw in deps:
            if d is op:
                continue
            if d.is_dma:
                ddeps[d.semname] = max(ddeps.get(d.semname, 0), 16 * self.dma_count[d.semname])
                continue
            if d.eng == eng and not op.is_dma:
                if eng == "pe" or not SAME_ENG_SYNC or not raw or d.idx != op.idx - 1:
                    continue
            keep[id(d)] = d
        op.deps = list(keep.values())
        op.ddeps = ddeps
        for d in op.deps:
            d.sig = True
        if op.is_dma:
            op.sig = True
            self.dma_count[dma] = self.dma_count.get(dma, 0) + 1
            op.val = 16 * self.dma_count[dma]
        for k in r:
            self.readers.setdefault(k, {})[(eng, dma)] = op
        for k in w:
            self.lastw[k] = op
            self.readers[k] = {}
        self.streams[eng].append(op)
        return op

    def barrier(self):
        lasts = []
        for e in ENGS:
            comp = [o for o in self.streams[e] if not o.is_dma]
            if comp:
                comp[-1].sig = True
                lasts.append(comp[-1])
        dd = {k: 16 * v for k, v in self.dma_count.items()}
        for e in ENGS:
            self.pending[e] = ([o for o in lasts if o.eng != e], dict(dd))

    def emit(self, es):
        nc = self.nc
        eng_sem = {e: es.enter_context(nc.semaphore("s_" + e)) for e in ENGS}
        dma_sems = {k: es.enter_context(nc.semaphore("d_" + k)) for k in self.dma_count}
        for e in ENGS:
            cnt = 0
            for op in self.streams[e]:
                if op.is_dma:
                    op.sem = dma_sems[op.semname]
                elif op.sig:
                    cnt += 1
                    op.sem, op.val = eng_sem[e], cnt
        fin = Op()
        fin.eng, fin.fn, fin.sig, fin.is_dma, fin.deps = "sp", None, False, False, []
        fin.ddeps = {k: 16 * v for k, v in self.dma_count.items()}
        self.streams["sp"].append(fin)
        self.n_sems = len(dma_sems) + len(ENGS)
        block = es.enter_context(nc.Block())
        for e in ENGS:
            ops = self.streams[e]

            def body(engine, ops=ops):
                seen = {}
                for op in ops:
                    waits = {}
                    for d in op.deps:
                        key = "e_" + d.eng
                        if key not in waits or waits[key][1] < d.val:
                            waits[key] = (d.sem, d.val)
                    for name, val in op.ddeps.items():
                        waits["d_" + name] = (dma_sems[name], val)
                    for key, (sem, val) in waits.items():
                        if seen.get(key, 0) < val:
                            engine.wait_ge(sem, val)
                            seen[key] = val
                    if op.fn is None:
                        continue
                    ins = op.fn(engine)
                    if op.sig:
                        ins.then_inc(op.sem, 16 if op.is_dma else 1)

            getattr(block, BLK[e])(body)


class Arena:
    def __init__(self, nc, base, limit):
        self.nc, self.off, self.limit, self.n = nc, base, limit, 0
        self.peak = base

    def alloc(self, name, shape, dtype):
        esz = {F32: 4, F32R: 4, BF16: 2}[dtype]
        size = esz * int(np.prod(shape[1:]))
        size = (size + 31) // 32 * 32
        if self.off + size > self.limit:
            raise RuntimeError(f"SBUF arena overflow allocating {name} {shape}: off={self.off} size={size} limit={self.limit}")
        self.n += 1
        t = self.nc.alloc_sbuf_tensor_at(f"{name}", list(shape), dtype, offset=self.off)
        self.off += size
        self.peak = max(self.peak, self.off)
        return t

    def mark(self):
        return self.off

    def reset(self, m):
        self.off = m


def r32(ap):
    return ap.bitcast(F32R)


class K:
    def __init__(self, NSEQ, S, NB, stop_after=None, debug=False):
        self.NSEQ, self.S, self.NB = NSEQ, S, NB
        self.NSAMP = NB * 4
        self.TS = NB * 5
        self.NTOK = NSEQ * S + self.NSAMP
        self.T = 256
        self.stop_after = stop_after
        self.debug = debug
        nc = self.nc = bass.Bass("TRN2", target_bir_lowering=False)
        self.s = Sched(nc)
        self.din = {}
        self.dout = {}

    def declare_dram(self):
        nc, NSEQ, S, NB = self.nc, self.NSEQ, self.S, self.NB

        def i(name, shape):
            self.din[name] = nc.dram_tensor(name, list(shape), F32, kind="ExternalInput").ap()

        def o(name, shape):
            self.dout[name] = nc.dram_tensor(name, list(shape), F32, kind="ExternalOutput").ap()

        i("xp", [NSEQ * S, D])
        i("xs", [self.TS, D])
        i("ck", [NB, 128, 128])
        i("cv", [NB, 128, 128])
        i("swkv", [NB * 8, 4096])
        i("w_in", [D, INW])
        i("mu", [14 * 128])
        i("w_out", [D, D])
        i("w_up", [D, DFF])
        i("w_dn", [DFF, D])
        i("g1", [D])
        i("g2", [D])
        i("qg", [HD])
        i("kg", [HD])
        i("sinks", [NQ])
        i("w0", [RW])
        i("wdu", [64, RW])
        i("a0", [RW])
        i("aup", [64, RW])
        i("gup", [128, RW])
        i("kk", [RW])
        i("ka", [RW])
        i("rk", [RW])
        i("lng", [RW])
        i("lnb", [RW])
        o("yp", [NSEQ * S, D])
        o("ys", [self.NSAMP, D])
        o("nkp", [NSEQ, 128, 128])
        o("nvp", [NSEQ, 128, 128])
        o("nwp", [NSEQ * 8 * 64, 64])
        o("nsp", [NSEQ, D])
        o("nks", [NB, 128, 128])
        o("nvs", [NB, 128, 128])
        o("nws", [NB * 8, 4096])
        o("nss", [NB, D])
        if self.debug == "phaseB":
            self.mix_scr = nc.dram_tensor("mix_in", [8, 128, self.NTOK], BF16, kind="ExternalInput").ap()
        else:
            self.mix_scr = nc.dram_tensor("mix_scr", [8, 128, self.NTOK], BF16, kind="Internal").ap()
        self.rw_scr = nc.dram_tensor("rw_scr", [6, self.NSAMP, RW], F32, kind="Internal").ap()
        self.w_up_bf = nc.dram_tensor("w_up_bf", [D, DFF], BF16, kind="Internal").ap()
        self.w_dn_bf = nc.dram_tensor("w_dn_bf", [DFF, D], BF16, kind="Internal").ap()
        self.y_scr = nc.dram_tensor("y_scr", [self.NSAMP, RW], F32, kind="Internal").ap()

    def _op(self, eng, meth, R, W, kw):
        return self.s.add(eng, lambda e: getattr(e, meth)(**kw), R, W)

    def pe(self, meth, R=(), W=(), **kw):
        return self._op("pe", meth, R, W, kw)

    def act(self, meth, R=(), W=(), **kw):
        return self._op("act", meth, R, W, kw)

    def dve(self, meth, R=(), W=(), **kw):
        return self._op("dve", meth, R, W, kw)

    def pool(self, meth, R=(), W=(), **kw):
        return self._op("pool", meth, R, W, kw)

    def dbg(self, name, ap, R):
        if not self.debug:
            return
        self.ndbg = getattr(self, "ndbg", 0) + 1
        d = self.nc.dram_tensor(name, list(ap.shape), ap.dtype, kind="ExternalOutput").ap()
        self.dout[name] = d
        self.dma("sp", f"dbg{self.ndbg}", d, ap, R=R, W=[("dbg", name)])

    def dma(self, q, sem, out, in_, R=(), W=(), **kw):
        return self.s.add(q, lambda e: e.dma_start(out=out, in_=in_, **kw), R, W, dma=sem)

    def setup(self):
        nc = self.nc
        base = 16512
        self.ar = Arena(nc, base, 229376)
        A = self.ar.alloc
        self.pb = [nc.alloc_psum_tensor(f"pb{i}", [128, 512], F32) for i in range(8)]
        self.ident_f = A("ident_f", [128, 128], F32)
        self.ident_bf = A("ident_bf", [128, 128], BF16)
        self.onesblk = A("onesblk", [128, 128], F32)
        self.maskZ = A("maskZ", [128, 128], F32)
        self.maskGT = A("maskGT", [64, 64], F32)
        self.amask = A("amask", [128, 2, 512], BF16)
        self.scanmask = A("scanmask", [128, 256], F32)
        self.gbc1 = A("gbc1", [128, D], F32)
        self.cols = A("cols", [128, 96], F32)
        self.esink = A("esink", [128, 8], F32)
        self.w_out_sb = A("w_out_sb", [128, 8, D], BF16)
        obf = A("onesblk_f", [128, 128], F32)
        self.shared_base = self.ar.mark()
        P, DV = self.pool, self.dve
        idf, idb, ob, mz, mg = self.ident_f, self.ident_bf, self.onesblk, self.maskZ, self.maskGT

        P("memset", W=["ident_f"], ap=idf[:], constant=0.0)
        P("affine_select", W=["ident_f"], out=idf[:], in_=idf[:], pattern=[[-1, 128]], compare_op=ALU.not_equal,
          fill=1.0, base=0, channel_multiplier=1)
        DV("tensor_copy", R=["ident_f"], W=["ident_bf"], out=idb[:], in_=idf[:])
        P("memset", W=["onesblk_f"], ap=obf[:], constant=0.0)
        P("memset", W=["onesblk_f"], ap=obf[0:64, 0:64], constant=1.0)
        P("memset", W=["onesblk_f"], ap=obf[64:128, 64:128], constant=1.0)
        DV("tensor_copy", R=["onesblk_f"], W=["onesblk"], out=r32(ob[:]), in_=obf[:])
        P("memset", W=["maskZ"], ap=mz[:], constant=1.0)
        for (p0, c0, strict) in ((0, 0, True), (64, 0, True), (0, 64, False), (64, 64, False)):
            P("affine_select", W=["maskZ"], out=mz[p0:p0 + 64, c0:c0 + 64], in_=mz[p0:p0 + 64, c0:c0 + 64],
              pattern=[[1, 64]], compare_op=ALU.is_ge, fill=0.0, base=(-1 if strict else 0), channel_multiplier=-1)
        P("memset", W=["maskGT"], ap=mg[:], constant=1.0)
        P("affine_select", W=["maskGT"], out=mg[:], in_=mg[:], pattern=[[-1, 64]], compare_op=ALU.is_ge, fill=0.0,
          base=-1, channel_multiplier=1)
        am = self.amask
        P("memset", W=["amask"], ap=am[:], constant=0.0)
        for j in range(4):
            P("affine_select", W=["amask"], out=am[:, 0, j * 128:(j + 1) * 128], in_=am[:, 0, j * 128:(j + 1) * 128],
              pattern=[[-1, 128]], compare_op=ALU.is_ge, fill=NEG, base=-1, channel_multiplier=1)
            P("affine_select", W=["amask"], out=am[:, 1, j * 128:(j + 1) * 128], in_=am[:, 1, j * 128:(j + 1) * 128],
              pattern=[[1, 128]], compare_op=ALU.is_ge, fill=NEG, base=0, channel_multiplier=-1)
        sm = self.scanmask
        P("memset", W=["scanmask"], ap=sm[:], constant=1.0)
        P("memset", W=["scanmask"], ap=sm[:].rearrange("p (c t) -> p c t", t=64)[:, :, 0:1], constant=0.0)

    def load_small(self):
        nc, din = self.nc, self.din
        cols = self.cols
        self.col = {}
        n = [0]

        def colvec(name, src_ap, ncol):
            c0 = n[0]
            n[0] += ncol
            self.col[name] = c0
            self.dma("sp", "small", cols[:, c0:c0 + ncol], src_ap, W=["cols"], allow_slow_non_contiguous=True)

        colvec("mu", din["mu"].rearrange("(j p) -> p j", p=128), 14)
        for nm in ("w0", "a0", "kk", "ka", "rk", "lng", "lnb"):
            colvec(nm, din[nm].rearrange("(j p) -> p j", p=128), 4)
        colvec("g2", din["g2"].rearrange("(j p) -> p j", p=128), 8)
        for nm in ("qg", "kg"):
            c0 = n[0]
            n[0] += 1
            self.col[nm] = c0
            for h in range(2):
                self.dma("sp", "small", cols[h * 64:(h + 1) * 64, c0:c0 + 1], din[nm].rearrange("(p o) -> p o", o=1),
                         W=["cols"], allow_slow_non_contiguous=True)
        self.dma("sp", "small", self.gbc1[:], din["g1"].rearrange("(o d) -> o d", o=1).partition_broadcast(128), W=["gbc1"])
        self.dma("sp", "small", self.esink[:], din["sinks"].rearrange("(o d) -> o d", o=1).partition_broadcast(128), W=["esink"])
        c = self.col
        c["qgs"] = n[0]; n[0] += 1
        c["omka"] = n[0]; n[0] += 4
        c["nw0"] = n[0]; n[0] += 4
        c["na0"] = n[0]; n[0] += 4
        cq, cqs, cka, comka = c["qg"], c["qgs"], c["ka"], c["omka"]
        self.dve("tensor_scalar", R=["cols"], W=["cols2"], out=cols[:, cqs:cqs + 1], in0=cols[:, cq:cq + 1], scalar1=0.125,
                 scalar2=None, op0=ALU.mult)
        self.dve("tensor_scalar", R=["cols"], W=["cols2"], out=cols[:, comka:comka + 4], in0=cols[:, cka:cka + 4], scalar1=-1.0,
                 scalar2=1.0, op0=ALU.mult, op1=ALU.add)
        for src, dst in (("w0", "nw0"), ("a0", "na0")):
            self.dve("tensor_scalar", R=["cols"], W=["cols2"], out=cols[:, c[dst]:c[dst] + 4], in0=cols[:, c[src]:c[src] + 4], scalar1=-1.0,
                     scalar2=None, op0=ALU.mult)
        self.act("activation", W=["esink"], out=self.esink[:], in_=self.esink[:], func=AF.Exp)

    def load_weight_bf16(self, dst, src, nkt, ncols, name, key):
        v = src.rearrange("(kt p) n -> p kt n", p=128)
        step = 1024
        for kt in range(nkt):
            for c0 in range(0, ncols, step):
                c1 = min(ncols, c0 + step)
                self.dma("pool", f"w_{name}", dst[:, kt, c0:c1], v[:, kt, c0:c1], W=[key])

    def bank(self, lo, hi, tag):
        c = self._bankctr.get(tag, 0)
        self._bankctr[tag] = c + 1
        return lo + c % (hi - lo)

    def phase_a(self):
        ar = self.ar
        ar.reset(self.shared_base)
        A = ar.alloc
        T = self.T
        self._bankctr = {}
        din = self.din
        self.w_in_sb = A("w_in_sb", [128, 8, INW], BF16)
        vw = din["w_in"].rearrange("(kt p) n -> p kt n", p=128)
        for ci, (c0, c1) in enumerate(((0, 1024), (1024, 2048), (2048, INW))):
            for kt in range(8):
                self.dma("pool", f"w_in{ci}", self.w_in_sb[:, kt, c0:c1], vw[:, kt, c0:c1], W=[("w_in", ci)])
        self.wcast_jobs = []
        for r0 in range(0, D, 128):
            for c0 in range(0, DFF, 1024):
                self.wcast_jobs.append((self.w_up_bf[r0:r0 + 128, c0:c0 + 1024], din["w_up"][r0:r0 + 128, c0:c0 + 1024], "w_up_bf"))
        for r0 in range(0, DFF, 128):
            self.wcast_jobs.append((self.w_dn_bf[r0:r0 + 128, :], din["w_dn"][r0:r0 + 128, :], "w_dn_bf"))
        self.wl = A("wl", [128, RW], F32)
        self.gup = A("gup", [128, RW], F32)
        hl = A("hl", [128, D], F32)
        self.dma("sp", "wl", hl[0:64, 0:RW], din["wdu"], W=["hl"])
        self.dma("sp", "wl", hl[64:128, 0:RW], din["aup"], W=["hl"])
        self.dma("sp", "wl", hl[:, RW:2 * RW], din["gup"], W=["hl"])
        self.dve("tensor_copy", R=["hl"], W=["wl"], out=r32(self.wl[:]), in_=hl[:, 0:RW])
        self.dve("tensor_copy", R=["hl"], W=["gup"], out=r32(self.gup[:]), in_=hl[:, RW:2 * RW])
        B = self.B = {}
        B["xin"] = [A("xin0", [128, T // 128, D], F32)]
        B["xin"].append(B["xin"][0])
        B["xn"] = [A(f"xn{i}", [128, D], BF16) for i in range(2)]
        B["hl"] = hl
        B["st"] = A("sta", [128, 8], F32)
        B["hT"] = A("hT", [128, 8, T], BF16)
        B["raw"] = [A(f"raw{i}", [128, T], F32) for i in range(2)]
        B["sq"] = [A(f"sq{i}", [128, T], F32) for i in range(2)]
        B["lnv"] = [A(f"lnv{i}", [128, T], F32) for i in range(2)]
        B["vf"] = A("vf", [128, 128], F32)
        B["kf"] = A("kf", [128, 128], F32)
        B["kft"] = A("kft", [128, 128], F32)
        B["pT"] = [A(f"pT{i}", [128, 512], BF16) for i in range(2)] * 2
        B["attn"] = [A("attn0", [128, 512], BF16)] * 2
        B["rden"] = A("rden", [128, 8], F32)
        B["pcur"] = [A(f"pcur{i}", [128, T + 1], F32) for i in range(2)]
        B["diff"] = [A(f"diff{i}", [128, T], F32) for i in range(2)]
        for nm in ("tw", "sgg", "sg", "aa", "Lc", "eL", "eN", "rn", "kkn", "fac", "kh", "t2", "rk"):
            B[nm] = A(nm, [128, T], F32)
        B["xar"] = B["tw"]
        B["gst"] = A("gst", [64, 96], F32)
        B["yg"] = A("yg", [128, T], F32)
        B["yb"] = A("yb", [128, T], F32)
        self.prompt_only_base = ar.mark()
        B["qT"] = A("qT", [128, 4, T], BF16)
        B["kT"] = A("kT", [128, 512], BF16)
        B["vtok"] = A("vtok", [128, 4, 2, 65], BF16)
        B["carry"] = A("carry", [128, 16], F32)
        B["pm"] = A("pm", [128, 14, T], F32)
        B["mixT"] = A("mixT", [128, 8, T], BF16)
        B["bonus"] = A("bonus", [128, 4, T], F32)
        B["AR"] = A("AR", [128, 4, T // 64, 2, 64], F32)
        B["BK"] = A("BK", [128, 4, T // 64, 2, 64], F32)
        B["eLC"] = A("eLC", [128, 4, T // 64], F32)
        B["Z"] = [A(f"Z{i}", [128, 8, 128], F32) for i in range(2)]
        for nm in ("G0", "Ga", "Gb", "GTa", "GTb", "Sa", "Sb", "Xsb"):
            B[nm] = A(nm, [128, 8, 64], F32)
        B["BKtok"] = [A(f"BKtok{i}", [128, 8, 64], F32) for i in range(2)]
        B["UV"] = [A(f"UV{i}", [128, 8, 64], F32) for i in range(2)]
        B["HS"] = [A(f"HS{i}", [128, 4, 64], F32) for i in range(2)]
        B["Hd"] = A("Hd", [128, 4, 64], F32)
        B["Htmp"] = A("Htmp", [128, 4, 64], F32)
        B["ysb"] = A("ysb", [64, T // 64, 8, 64], F32)
        B["stout"] = B["Z"][0][0:64].rearrange("p h t -> p (h t)")[:, 0:512].rearrange("p (h c) -> p h c", h=8)
        self.a_mark = ar.mark()

        tiles = [dict(kind="p", seq=q, t0=t0, tok0=q * self.S + t0, T=T, nsub=T // 128, rows=128, first=(t0 == 0), last=(t0 + T == self.S))
                 for q in range(self.NSEQ) for t0 in range(0, self.S, T)]
        self.pool("memset", W=["vtok"], ap=B["vtok"][:, :, :, 64:65], constant=1.0)
        if self.stop_after in ("proj", "elem"):
            for tl in tiles:
                self.tile_load(tl)
                self.norm_transpose(tl, 0)
                self.tile_a(tl, None)
        else:
            self.tile_load(tiles[0])
            self.norm_transpose(tiles[0], 0)
            per = -(-len(self.wcast_jobs) // max(1, len(tiles) - 2))
            for i, tl in enumerate(tiles):
                self.tile_a(tl, tiles[i + 1] if i + 1 < len(tiles) else None)
                for _ in range(per):
                    if self.wcast_jobs:
                        o, i_, k_ = self.wcast_jobs.pop(0)
                        self.dma("pool", "wcast", o, i_, W=[k_])
            while self.wcast_jobs:
                o, i_, k_ = self.wcast_jobs.pop(0)
                self.dma("pool", "wcast", o, i_, W=[k_])
        if self.stop_after is not None:
            return
        self.sample_tile()

    def sample_tile(self):
        s_ = self.s
        s_.barrier()
        ar, B, pb, din, dout = self.ar, self.B, self.pb, self.din, self.dout
        ar.reset(self.prompt_only_base)
        A = ar.alloc
        NB, NS, TS = self.NB, self.NSAMP, self.TS
        SB = self.SB = {}
        SB["qT_s"] = A("qT_s", [128, 4, TS], BF16)
        SB["kT_s"] = A("kT_s", [128, TS], BF16)
        SB["vtok_s"] = A("vtok_s", [128, 2, 65], BF16)
        SB["pm_s"] = A("pm_s", [128, 14, TS], F32)
        SB["bonus_s"] = A("bonus_s", [128, 4, NS], F32)
        SB["q6"] = A("q6", [128, 6, 4, NS], F32)
        SB["mixT_s"] = A("mixT_s", [128, 8, NS], BF16)
        SB["maskc"] = A("maskc", [128, NB, 4, NS], BF16)
        SB["maskn"] = A("maskn", [128, 4, NS], BF16)
        SB["ckf"] = A("ckf", [128, NB, 128], F32)
        SB["ckb"] = A("ckb", [128, NB, 128], BF16)
        SB["ckT"] = A("ckT", [128, NB, 128], BF16)
        SB["cvb"] = A("cvb", [128, NB, 2, 65], BF16)
        SB["q6tok"] = A("q6tok", [64, 6, RW], F32)
        SB["rkv"] = A("rkv", [128, 6, 4, 64], F32)
        SB["Sst"] = A("Sst", [128, 64, 64], F32)
        wv = self.w_in_sb[:].rearrange("p k n -> p (k n)").bitcast(F32)
        SB["tA"] = wv[:, 0:4096].rearrange("p (i j) -> p i j", i=64)
        SB["tB"] = wv[:, 4096:8192].rearrange("p (i j) -> p i j", i=64)
        SB["sa"] = A("sa", [128, 64], F32)
        SB["ysr"] = A("ysr", [128, 4, 64], F32)
        SB["ytok"] = A("ytok", [64, 1, 8, 64], F32)
        SB["ysq_s"] = A("ysq_s", [64, 1, 8, 64], F32)
        tl = dict(kind="s", T=TS, nsub=1, rows=TS, nreal=NS, first=True, last=True, t0=0)
        x_ = B["xin"][0]
        self.dma("sp", "xs_in", x_[0:TS, 0, :], din["xs"][:, :], W=[("xin", 0)])
        self.dma("sp", "ck_in", SB["ckf"][:], din["ck"].rearrange("b k d -> k b d"), W=["ckf"])
        self.dma("sp", "swkv_in", SB["Sst"][:].rearrange("p i j -> p (i j)"), din["swkv"][:, :], W=[("Sst", 0), ("Sst", 1)])
        self.dma("sp", "nks_c", dout["nks"][:, 0:124, :], din["ck"][:, 4:128, :], W=["nks_c"])
        self.dma("sp", "nvs_c", dout["nvs"][:, 0:124, :], din["cv"][:, 4:128, :], W=["nvs_c"])
        mc, mn = SB["maskc"], SB["maskn"]
        mcf = mc[:].rearrange("p b j q -> p (b j q)")
        self.pool("memset", W=["maskc"], ap=mcf, constant=0.0)
        self.pool("affine_select", W=["maskc"], out=mcf, in_=mcf, pattern=[[1, NB], [0, 4], [-1, NB], [0, 4]],
                  compare_op=ALU.is_equal, fill=NEG, base=0, channel_multiplier=0)
        self.pool("affine_select", W=["maskc"], out=mcf, in_=mcf, pattern=[[0, NB], [0, 4], [0, NB], [-1, 4]],
                  compare_op=ALU.is_ge, fill=NEG, base=-1, channel_multiplier=1)
        mnf = mn[0:NS].rearrange("p j q -> p (j q)")
        self.pool("memset", W=["maskn"], ap=mn[:].rearrange("p j q -> p (j q)"), constant=0.0)
        self.pool("affine_select", W=["maskn"], out=mnf, in_=mnf, pattern=[[0, 4], [-4, NB], [0, 4]], compare_op=ALU.is_ge, fill=NEG,
                  base=0, channel_multiplier=1)
        self.pool("affine_select", W=["maskn"], out=mnf, in_=mnf, pattern=[[0, 4], [4, NB], [1, 4]], compare_op=ALU.is_ge, fill=NEG,
                  base=0, channel_multiplier=-1)
        self.pool("memset", W=["vtok_s"], ap=SB["vtok_s"][:, :, 64:65], constant=1.0)
        self.norm_transpose(tl, 0)
        self.project(tl)
        for t in range(4):
            self.dma("pool", "nks_n", dout["nks"][:, 124 + t, :], B["kft"][t:NS:4, :], R=["kft"], W=[("nks_n", t)])
            self.dma("pool", "nvs_n", dout["nvs"][:, 124 + t, :], B["vf"][t:NS:4, :], R=["vf"], W=[("nvs_n", t)])
        self.pool("tensor_copy", R=["ckf"], W=["ckb"], out=SB["ckb"][:], in_=SB["ckf"][:])
        for g in range(NB // 8):
            b = self.bank(2, 4, "tr")
            pbt = pb[b][:].bitcast(BF16).rearrange("p (k t) -> p k t", k=8)
            for i in range(8):
                self.pe("transpose", R=["ckb", "ident_bf"], W=[("pb", b)], out=pbt[:, i, :], in_=SB["ckb"][:, g * 8 + i, :],
                        identity=self.ident_bf[:])
            self.act("activation", W=[("pb", b), "ckT"], out=SB["ckT"][:, g * 8:(g + 1) * 8, :], in_=pbt[:, :, :], func=AF.Copy)
        self.dma("sp", "cv_in", SB["ckf"][:], din["cv"].rearrange("b k d -> k b d"), R=["ckb"], W=["ckf"])
        self.pool("memset", W=["cvb"], ap=SB["cvb"][:, :, :, 64:65], constant=1.0)
        self.pool("tensor_copy", R=["ckf"], W=["cvb"], out=SB["cvb"][:, :, :, 0:64], in_=SB["ckf"][:].rearrange("p b (k d) -> p b k d", k=2))
        if self.stop_after == "sproj":
            self.dbg("d_qTs", SB["qT_s"][:], ["qT"])
            self.dbg("d_pms", SB["pm_s"][:], [("pm", i) for i in range(14)])
            return
        qT_s, kT_s = SB["qT_s"], SB["kT_s"]
        at, ka = B["attn"][0], ("attn", 0)
        for kvh in range(2):
            ps_ = slice(kvh * 64, (kvh + 1) * 64)
            bpv = self.bank(6, 8, "pv")
            pv = pb[bpv][0:NS, 0:260].rearrange("p (j d) -> p j d", j=4)
            first = [True]

            def pv_mm(pt_ap, rhs_ap, rkeys, last):
                for j in range(4):
                    self.pe("matmul", R=rkeys, W=[("pb", bpv)], out=pv[:, j, :], lhsT=pt_ap(j), rhs=rhs_ap, start=first[0],
                            stop=last and j == 3, skip_group_check=True)
                    first[0] = False
            for g in range(NB // 2):
                b = self.bank(4, 6, "sc")
                ip = self._bankctr.get("spt", 0)
                self._bankctr["spt"] = ip + 1
                pt = B["pT"][ip % 2]
                kpt = ("pT", ip % 2)
                for hf in range(2):
                    bb = g * 2 + hf
                    o = pb[b][:, hf * 256:(hf + 1) * 256]
                    self.pe("matmul", R=["ident_bf", "maskc"], W=[("pb", b)], out=o, lhsT=self.ident_bf[:],
                            rhs=mc[:, bb, :, :].rearrange("p j q -> p (j q)"), start=True, stop=False)
                    self.pe("matmul", R=["ckT", "qT"], W=[("pb", b)], out=o.rearrange("p (j q) -> p j q", j=4), lhsT=SB["ckT"][ps_, bb, :],
                            rhs=qT_s[ps_, :, 0:NS], start=False, stop=True)
                self.act("activation", W=[("pb", b), kpt], out=pt[:, :], in_=pb[b][:, :], func=AF.Exp)
                for hf in range(2):
                    bb = g * 2 + hf
                    pv_mm(lambda j, hf=hf, pt=pt: pt[:, hf * 256 + j * NS:hf * 256 + (j + 1) * NS], SB["cvb"][:, bb, kvh, :], [kpt, "cvb"], False)
            b = self.bank(4, 6, "sc")
            ip = self._bankctr.get("spt", 0)
            self._bankctr["spt"] = ip + 1
            pt = B["pT"][ip % 2]
            kpt = ("pT", ip % 2)
            o = pb[b][0:NS, 0:256]
            self.pe("matmul", R=["ident_bf", "maskn"], W=[("pb", b)], out=o, lhsT=self.ident_bf[:, 0:NS],
                    rhs=mn[:].rearrange("p j q -> p (j q)"), start=True, stop=False)
            self.pe("matmul", R=["kT", "qT"], W=[("pb", b)], out=o.rearrange("p (j q) -> p j q", j=4), lhsT=kT_s[ps_, 0:NS],
                    rhs=qT_s[ps_, :, 0:NS], start=False, stop=True)
            self.act("activation", W=[("pb", b), kpt], out=pt[0:NS, 0:256], in_=o, func=AF.Exp)
            pv_mm(lambda j, pt=pt: pt[0:NS, j * NS:(j + 1) * NS], SB["vtok_s"][0:NS, kvh, :], [kpt, "vtok_s"], True)
            rden = B["rden"]
            self.dve("tensor_tensor", R=["esink"], W=[("pb", bpv), "rden"], out=rden[0:NS, 0:4], in0=pv[:, :, 64],
                     in1=self.esink[0:NS, kvh * 4:(kvh + 1) * 4], op=ALU.add)
            self.dve("reciprocal", R=["rden"], W=["rden"], out=rden[0:NS, 4:8], in_=rden[0:NS, 0:4])
            self.dve("tensor_tensor", R=["rden"], W=[("pb", bpv), ka],
                     out=at[0:NS, kvh * 256:(kvh + 1) * 256].rearrange("p (j d) -> p j d", j=4), in0=pv[:, :, 0:64],
                     in1=rden[0:NS, 4:8].unsqueeze(2).to_broadcast([NS, 4, 64]), op=ALU.mult)
        self.attn_to_mix(at, ka, SB["mixT_s"], 0, NS)
        for _ in self.rwkv_elem(tl, sample=True):
            pass
        q6, q6tok, rkv = SB["q6"], SB["q6tok"], SB["rkv"]
        idf = self.ident_f
        for sl in range(6):
            b = self.bank(0, 6, "ch")
            for j in range(4):
                self.pe("transpose", R=[("q6", j), "ident_f"], W=[("pb", b)], out=pb[b][0:NS, j * 128:(j + 1) * 128], in_=q6[:, sl, j, 0:NS],
                        identity=idf[:])
            self.act("activation", W=[("pb", b), "q6tok"], out=q6tok[0:NS, sl, :], in_=pb[b][0:NS, :], func=AF.Copy)
        scr = self.rw_scr
        scr5 = scr.rearrange("s n f -> s (n f)").rearrange("s (b h t c) -> s b h t c", b=NB, h=8, t=4)
        for t in range(4):
            for sl in range(6):
                self.dma("sp", "rw_w", scr5[sl, :, :, t, :], q6tok[t:NS:4, sl, :].rearrange("p (h c) -> p h c", h=8), R=["q6tok"], W=["rwscr"])
        self.dma("sp", "rw_r", rkv[:], scr.rearrange("s n f -> s (n f)").rearrange("s (p tc) -> p s tc", p=128).rearrange("p s (t c) -> p s t c", t=4),
                 R=["rwscr"], W=["rkv"])
        S3, tA, tB, sa, ysr = SB["Sst"], SB["tA"], SB["tB"], SB["sa"], SB["ysr"]
        RH = 30
        halves = ((self.dve, slice(0, RH), 0), (self.pool, slice(RH, 64), 1))
        WIN = [("w_in", 0), ("w_in", 1), ("w_in", 2)]
        for t in range(4):
            def bi(sl, rs_):
                n_ = rs_.stop - rs_.start
                return rkv[:, sl, t, :].unsqueeze(1).to_broadcast([128, n_, 64])
            for eng, rs_, hh in halves:
                eng("tensor_tensor", R=[("Sst", hh), "rkv"], W=[("tA", hh)] + WIN, out=tA[:, rs_, :], in0=S3[:, rs_, :], in1=bi(4, rs_), op=ALU.mult)
            self.dve("tensor_reduce", R=[("tA", 0), ("tA", 1)], W=["sa"], out=sa[:], in_=tA, axis=AX.X, op=ALU.add)
            for eng, rs_, hh in halves:
                n_ = rs_.stop - rs_.start
                eng("tensor_tensor", R=["rkv"], W=[("tB", hh)] + WIN, out=tB[:, rs_, :],
                    in0=rkv[:, 3, t, rs_].unsqueeze(2).to_broadcast([128, n_, 64]), in1=bi(2, rs_), op=ALU.mult)
                eng("tensor_tensor", R=["rkv", ("tA", hh)], W=[("Sst", hh)], out=S3[:, rs_, :], in0=S3[:, rs_, :], in1=bi(1, rs_), op=ALU.mult)
            for eng, rs_, hh in halves:
                n_ = rs_.stop - rs_.start
                eng("tensor_tensor", R=["sa", "rkv"], W=[("tA", hh)] + WIN, out=tA[:, rs_, :],
                    in0=sa[:, rs_].unsqueeze(2).to_broadcast([128, n_, 64]), in1=bi(5, rs_), op=ALU.mult)
                eng("tensor_tensor", R=[("tA", hh)], W=[("Sst", hh)], out=S3[:, rs_, :], in0=S3[:, rs_, :], in1=tA[:, rs_, :], op=ALU.add)
                eng("tensor_tensor", R=[("tB", hh)], W=[("Sst", hh)], out=S3[:, rs_, :], in0=S3[:, rs_, :], in1=tB[:, rs_, :], op=ALU.add)
                eng("tensor_tensor", R=[("Sst", hh), "rkv"], W=[("tA", hh)] + WIN, out=tA[:, rs_, :], in0=S3[:, rs_, :], in1=bi(0, rs_), op=ALU.mult)
            self.dve("tensor_reduce", R=[("tA", 0), ("tA", 1)], W=["ysr"], out=ysr[:, t, :], in_=tA, axis=AX.X, op=ALU.add)
        self.dma("pool", "nws", dout["nws"][:, :], S3[:].rearrange("p i j -> p (i j)"), R=[("Sst", 0), ("Sst", 1)], W=["nws"])
        ysc = self.y_scr
        self.dma("sp", "ys_w", ysc.rearrange("n f -> (n f)").rearrange("(p tc) -> p tc", p=128), ysr[:].rearrange("p t c -> p (t c)"),
                 R=["ysr"], W=["yscr"])
        ys5 = ysc.rearrange("n f -> (n f)").rearrange("(b h t c) -> b h t c", b=NB, h=8, t=4)
        ytok = SB["ytok"]
        for t in range(4):
            self.dma("sp", "ys_r", ytok[t:NS:4, 0, :, :], ys5[:, :, t, :], R=["yscr"], W=["ytok"])
        self.gn_transpose(ytok, SB["ysq_s"], 1, ["ytok"], ["ysq_s"])
        self.rwkv_combine(tl, SB["pm_s"], SB["mixT_s"], NS)
        tok0 = self.NSEQ * self.S
        self.dma("pool", "mixst", self.mix_scr[:, :, tok0:tok0 + NS].rearrange("k p t -> p k t"), SB["mixT_s"][:, :, 0:NS],
                 R=["mixT"], W=[("mixscr", tok0)])

    def norm_transpose(self, tl, sl):
        self.norm1(tl, sl)
        self.transpose1(tl, sl)

    def norm1(self, tl, sl):
        B, pb, cols = self.B, self.pb, self.cols
        x_ = B["xin"][sl]
        st = B["st"]
        rows, nsub = tl["rows"], tl["nsub"]
        nr = tl.get("nreal", rows)
        for s_ in range(nsub):
            xn = B["xn"][s_ % 2]
            sc = st[:, 4 * s_:4 * s_ + 4]
            self.act("activation", R=[("xin", sl)], W=[("xn", s_ % 2), "sta"], out=xn[0:nr, :], in_=x_[0:nr, s_, :], func=AF.Square,
                     accum_out=sc[0:nr, 0:1])
            self.act("activation", R=["sta"], W=["sta"], out=sc[0:nr, 1:2], in_=sc[0:nr, 0:1], func=AF.Ln, bias=RMS_EPS, scale=1.0 / D)
            self.act("activation", R=["sta"], W=["sta"], out=sc[0:nr, 2:3], in_=sc[0:nr, 1:2], func=AF.Exp, scale=-0.5)
            self.dve("scalar_tensor_tensor", R=[("xin", sl), "sta", "gbc1"], W=[("xn", s_ % 2)], out=xn[0:nr, :], in0=x_[0:nr, s_, :],
                     scalar=sc[0:nr, 2:3], in1=self.gbc1[0:nr, :], op0=ALU.mult, op1=ALU.mult)
            if nr < rows:
                self.dve("tensor_copy", R=[("xin", sl)], W=[("xn", s_ % 2)], out=xn[nr:rows, :], in_=x_[nr:rows, s_, :])
            want_shift = (tl["kind"] == "s") or (tl["last"] and s_ == nsub - 1)
            if want_shift:
                hl = B["hl"]
                self.dve("scalar_tensor_tensor", R=[("xin", sl), "sta", "gbc1"], W=["hl"], out=hl[0:nr, :], in0=x_[0:nr, s_, :],
                         scalar=sc[0:nr, 2:3], in1=self.gbc1[0:nr, :], op0=ALU.mult, op1=ALU.mult)
                if tl["kind"] == "p":
                    self.dma("pool", "nsp", self.dout["nsp"][tl["seq"]:tl["seq"] + 1, :], hl[127:128, :], R=["hl"], W=[("nsp", tl["seq"])])
                else:
                    self.dma("pool", "nss", self.dout["nss"][:, :], hl[3:nr:4, :], R=["hl"], W=["nss"])

    def transpose1(self, tl, sl):
        B, pb = self.B, self.pb
        rows, nsub = tl["rows"], tl["nsub"]
        for s_ in range(nsub):
            xn = B["xn"][s_ % 2]
            b = self.bank(2, 4, "tr")
            pbt = pb[b][:].bitcast(BF16).rearrange("p (k t) -> p k t", k=8)
            for kt in range(8):
                self.pe("transpose", R=[("xn", s_ % 2), "ident_bf"], W=[("pb", b)], out=pbt[:, kt, 0:rows],
                        in_=xn[0:rows, kt * 128:(kt + 1) * 128], identity=self.ident_bf[0:rows, 0:rows])
            c0 = s_ * 128
            if s_ % 2 == 0:
                self.act("activation", W=[("pb", b), "hT"], out=B["hT"][:, :, c0:c0 + rows], in_=pbt[:, :, 0:rows], func=AF.Copy)
            else:
                self.dve("tensor_copy", W=[("pb", b), "hT"], out=B["hT"][:, :, c0:c0 + rows], in_=pbt[:, :, 0:rows])

    def project(self, tl):
        B, pb, cols, col = self.B, self.pb, self.cols, self.col
        Tt = tl["T"]
        w_in_sb, hT = self.w_in_sb, B["hT"]
        kind = tl["kind"]
        order = [0, 1, 2, 3, 4] + list(range(6, 20))
        for s_ in range(tl["nsub"]):
            rows = tl["rows"]
            c0 = s_ * 128
            b = self.bank(2, 4, "tr")
            for kt in range(8):
                self.pe("matmul", R=["hT", ("w_in", 0)], W=[("pb", b)], out=pb[b][0:rows, 0:128], lhsT=hT[:, kt, c0:c0 + rows],
                        rhs=w_in_sb[:, kt, 640:768], start=(kt == 0), stop=(kt == 7))
            self.evac_v(tl, s_, b)
        for idx in range(0, len(order), 2):
            b = (0, 1, 4, 5)[self.bank(0, 4, "proj")]
            pair = order[idx:idx + 2]
            for hf, ot in enumerate(pair):
                for kt in range(8):
                    self.pe("matmul", R=["hT", ("w_in", ot // 8)], W=[("pb", b)], out=pb[b][:, hf * 256:hf * 256 + Tt],
                            lhsT=w_in_sb[:, kt, ot * 128:(ot + 1) * 128], rhs=hT[:, kt, 0:Tt], start=(kt == 0), stop=(kt == 7))
            gens = []
            for hf, ot in enumerate(pair):
                ps = pb[b][:, hf * 256:hf * 256 + Tt]
                gens.append(self.qknorm(tl, ot, ps, b) if ot <= 4 else self.shiftmix(tl, ot - 6, ps, b))
            while gens:
                for g_ in list(gens):
                    try:
                        next(g_)
                    except StopIteration:
                        gens.remove(g_)

    def evac_v(self, tl, s_, b):
        B, pb = self.B, self.pb
        rows = tl["rows"]
        ps = pb[b][0:rows, 0:128].rearrange("p (k d) -> p k d", k=2)
        if tl["kind"] == "p":
            blk = (tl["t0"] // 128 + s_) % 4
            dst = B["vtok"][0:rows, blk, :, 0:64]
            is_last = tl["last"] and s_ == tl["nsub"] - 1
            if is_last:
                self.act("activation", W=[("pb", b), "vf"], out=B["vf"][0:rows, :], in_=pb[b][0:rows, 0:128], func=AF.Copy)
                self.dma("pool", "nvp", self.dout["nvp"][tl["seq"]], B["vf"][0:rows, :], R=["vf"], W=[("nvp", tl["seq"])])
            self.dve("tensor_copy", W=[("pb", b), "vtok"], out=dst, in_=ps)
        else:
            self.act("activation", W=[("pb", b), "vf"], out=B["vf"][0:rows, :], in_=pb[b][0:rows, 0:128], func=AF.Copy)
            self.dve("tensor_copy", W=[("pb", b), "vtok_s"], out=self.SB["vtok_s"][0:rows, :, 0:64], in_=ps)

    def qknorm(self, tl, ot, ps, b):
        B, pb, cols, col = self.B, self.pb, self.cols, self.col
        Tt = tl["T"]
        i = self._bankctr.get("qk", 0)
        self._bankctr["qk"] = i + 1
        raw, sq, lnv = B["raw"][i % 2], B["sq"][i % 2], B["lnv"][i % 2]
        kr, ks, kl = ("raw", i % 2), ("sq", i % 2), ("lnv", i % 2)
        self.act("activation", W=[("pb", b), kr], out=raw[:, 0:Tt], in_=ps, func=AF.Copy)
        self.act("activation", W=[("pb", b), ks], out=r32(sq[:, 0:Tt]), in_=ps, func=AF.Square)
        yield
        b2 = self.bank(2, 4, "tr")
        self.pe("matmul", R=[ks, "onesblk"], W=[("pb", b2)], out=pb[b2][:, 0:Tt], lhsT=r32(self.onesblk[:]), rhs=r32(sq[:, 0:Tt]),
                start=True, stop=True)
        self.act("activation", W=[("pb", b2), kl], out=lnv[:, 0:Tt], in_=pb[b2][:, 0:Tt], func=AF.Ln, bias=RMS_EPS, scale=1.0 / HD)
        yield
        self.act("activation", R=[kl], W=[kl], out=lnv[:, 0:Tt], in_=lnv[:, 0:Tt], func=AF.Exp, scale=-0.5)
        yield
        if ot < 4:
            g = cols[:, col["qgs"]:col["qgs"] + 1]
            dst = B["qT"][:, ot, 0:Tt] if tl["kind"] == "p" else self.SB["qT_s"][:, ot, 0:Tt]
            wk = "qT"
        else:
            g = cols[:, col["kg"]:col["kg"] + 1]
            dst = B["kT"][:, tl["t0"] % 512:tl["t0"] % 512 + Tt] if tl["kind"] == "p" else self.SB["kT_s"][:, 0:Tt]
            wk = "kT"
        self.dve("scalar_tensor_tensor", R=[kr, kl, "cols", "cols2"], W=[wk], out=dst, in0=raw[:, 0:Tt], scalar=g, in1=lnv[:, 0:Tt],
                 op0=ALU.mult, op1=ALU.mult)
        if ot == 4:
            if tl["kind"] == "p" and tl["last"]:
                c0, n = Tt - 128, 128
                out_ap, okey = self.dout["nkp"][tl["seq"]], ("nkp", tl["seq"])
            elif tl["kind"] == "s":
                c0, n = 0, self.NSAMP
                out_ap, okey = None, None
            else:
                return
            yield
            kf = B["kf"]
            self.dve("scalar_tensor_tensor", R=[kr, kl, "cols"], W=["kf"], out=kf[:, 0:n], in0=raw[:, c0:c0 + n], scalar=g,
                     in1=lnv[:, c0:c0 + n], op0=ALU.mult, op1=ALU.mult)
            b3 = self.bank(2, 4, "tr")
            self.pe("transpose", R=["kf", "ident_f"], W=[("pb", b3)], out=pb[b3][0:n, 0:128], in_=kf[:, 0:n], identity=self.ident_f[:])
            self.act("activation", W=[("pb", b3), "kft"], out=B["kft"][0:n, :], in_=pb[b3][0:n, 0:128], func=AF.Copy)
            if out_ap is not None:
                self.dma("pool", "nkp", out_ap, B["kft"][0:n, :], R=["kft"], W=[okey])

    def shiftmix(self, tl, i, ps, b):
        B, pb, cols, col = self.B, self.pb, self.cols, self.col
        Tt = tl["T"]
        n = self._bankctr.get("sm", 0)
        self._bankctr["sm"] = n + 1
        pc, df = B["pcur"][n % 2], B["diff"][n % 2]
        kp, kd = ("pcur", n % 2), ("diff", n % 2)
        mu = cols[:, col["mu"] + i:col["mu"] + i + 1]
        pm = B["pm"][:, i, 0:Tt] if tl["kind"] == "p" else self.SB["pm_s"][:, i, 0:Tt]
        kpm = ("pm", i)
        if tl["kind"] == "p":
            carry = B["carry"]
            if tl["first"]:
                self.pool("memset", W=[kp], ap=pc[:, 0:1], constant=0.0)
            else:
                self.pool("tensor_copy", R=[("carry", i)], W=[kp], out=pc[:, 0:1], in_=carry[:, i:i + 1])
            self.act("activation", W=[("pb", b), kp], out=pc[:, 1:Tt + 1], in_=ps, func=AF.Copy)
            self.pool("tensor_copy", R=[kp], W=[("carry", i)], out=carry[:, i:i + 1], in_=pc[:, Tt:Tt + 1])
            yield
            self.dve("tensor_tensor", R=[kp], W=[("pb", b), kd], out=df[:, 0:Tt], in0=pc[:, 0:Tt], in1=ps, op=ALU.subtract)
            yield
        else:
            NB, NS = self.NB, self.NSAMP
            self.act("activation", W=[("pb", b), kp], out=pc[:, 0:Tt], in_=ps, func=AF.Copy)
            pv = pc[:, 0:NS].rearrange("p (b t) -> p b t", t=4)
            dv = df[:, 0:NS].rearrange("p (b t) -> p b t", t=4)
            self.dve("tensor_tensor", R=[kp], W=[kd], out=dv[:, :, 1:4], in0=pv[:, :, 0:3], in1=pv[:, :, 1:4], op=ALU.subtract)
            self.dve("tensor_tensor", R=[kp], W=[kd], out=dv[:, :, 0:1], in0=pc[:, NS:NS + NB].unsqueeze(2), in1=pv[:, :, 0:1], op=ALU.subtract)
            yield
            Tt = NS
            ps = ps[:, 0:NS]
            pm = self.SB["pm_s"][:, i, 0:NS]
        self.dve("scalar_tensor_tensor", R=[kd, "cols"], W=[("pb", b), kpm], out=pm, in0=df[:, 0:Tt], scalar=mu, in1=ps,
                 op0=ALU.mult, op1=ALU.add)

    def tile_load(self, tl):
        tok0, Tt = tl["tok0"], tl["T"]
        self.dma("sp", "xa0", self.B["xin"][0][:, 0:tl["nsub"], :], self.din["xp"][tok0:tok0 + Tt, :].rearrange("(s p) d -> p s d", p=128),
                 W=[("xin", 0)])

    def tile_a(self, tl, nxt):
        B = self.B
        n = self._bankctr.get("tile", 0)
        self._bankctr["tile"] = n + 1
        tok0, Tt = tl["tok0"], tl["T"]
        self.project(tl)
        if nxt is not None:
            self.tile_load(nxt)
            self.norm1(nxt, 0)
        if self.stop_after == "proj":
            if n == 0:
                self.dbg("d_hT", B["hT"][:, :, :], ["hT"])
                self.dbg("d_qT", B["qT"][:, :, :], ["qT"])
                self.dbg("d_kT", B["kT"][:, 0:Tt], ["kT"])
                self.dbg("d_pm", B["pm"][:, :, :], [("pm", i) for i in range(14)])
                self.dbg("d_vtok", B["vtok"][:, 0:2, :, :], ["vtok"])
            return
        ga, ge = self.attention_prompt(tl), self.rwkv_elem(tl)
        alive = [ga, ge]
        while alive:
            for g_ in list(alive):
                try:
                    next(g_)
                except StopIteration:
                    alive.remove(g_)
                    if g_ is ga and nxt is not None:
                        self.transpose1(nxt, 0)
        if self.stop_after == "elem":
            if n == 0:
                self.dbg("d_AR", B["AR"][:], [("AR", j) for j in range(4)])
                self.dbg("d_BK", B["BK"][:], [("BK", j) for j in range(4)])
                self.dbg("d_bonus", B["bonus"][:], ["bonus"])
                self.dbg("d_mixT", B["mixT"][:], ["mixT"])
            return
        self.rwkv_chunks(tl)
        self.rwkv_combine(tl, B["pm"], B["mixT"], Tt)
        self.dma("pool", "mixst", self.mix_scr[:, :, tok0:tok0 + Tt].rearrange("k p t -> p k t"), B["mixT"][:, :, 0:Tt],
                 R=["mixT"], W=[("mixscr", tok0)])

    def attention_prompt(self, tl):
        B, pb = self.B, self.pb
        qT, kT, vtok = B["qT"], B["kT"], B["vtok"]
        for qb in range(tl["nsub"]):
            n = tl["t0"] // 128 + qb
            at = B["attn"][qb % 2]
            ka = ("attn", 0)
            q0 = qb * 128
            for kvh in range(2):
                kbs = [n - 1, n] if n > 0 else [n]
                ps_ = slice(kvh * 64, (kvh + 1) * 64)
                for ki, kb in enumerate(kbs):
                    b = self.bank(4, 6, "sc")
                    mt = 1 if kb == n else 0
                    self.pe("matmul", R=["ident_bf", "amask"], W=[("pb", b)], out=pb[b][:, :], lhsT=self.ident_bf[:],
                            rhs=self.amask[:, mt, :], start=True, stop=False)
                    self.pe("matmul", R=["kT", "qT"], W=[("pb", b)], out=pb[b][:, :].rearrange("p (j q) -> p j q", j=4),
                            lhsT=kT[ps_, (kb % 4) * 128:(kb % 4 + 1) * 128], rhs=qT[ps_, :, q0:q0 + 128], start=False, stop=True)
                    pt = B["pT"][ki]
                    self.act("activation", W=[("pb", b), ("pT", ki)], out=pt[:, :], in_=pb[b][:, :], func=AF.Exp)
                    yield
                bpv = self.bank(6, 8, "pv")
                pv = pb[bpv][:, 0:260].rearrange("p (j d) -> p j d", j=4)
                for j in range(4):
                    for ki, kb in enumerate(kbs):
                        pt = B["pT"][ki]
                        self.pe("matmul", R=[("pT", ki), "vtok"], W=[("pb", bpv)], out=pv[:, j, :],
                                lhsT=pt[:, j * 128:(j + 1) * 128], rhs=vtok[:, kb % 4, kvh, :], start=(ki == 0), stop=(ki == len(kbs) - 1))
                rden = B["rden"]
                self.dve("tensor_tensor", R=["esink"], W=[("pb", bpv), "rden"], out=rden[:, 0:4], in0=pv[:, :, 64],
                         in1=self.esink[:, kvh * 4:(kvh + 1) * 4], op=ALU.add)
                self.dve("reciprocal", R=["rden"], W=["rden"], out=rden[:, 4:8], in_=rden[:, 0:4])
                self.dve("tensor_tensor", R=["rden"], W=[("pb", bpv), ka],
                         out=at[:, kvh * 256:(kvh + 1) * 256].rearrange("p (j d) -> p j d", j=4), in0=pv[:, :, 0:64],
                         in1=rden[:, 4:8].unsqueeze(2).to_broadcast([128, 4, 64]), op=ALU.mult)
                yield
            self.attn_to_mix(at, ka, B["mixT"], q0, 128)
            yield

    def attn_to_mix(self, at, ka, mixT, q0, rows):
        pb = self.pb
        b = self.bank(2, 4, "tr")
        pbt = pb[b][:].bitcast(BF16).rearrange("p (k t) -> p k t", k=8)
        for k in range(4):
            self.pe("transpose", R=[ka, "ident_bf"], W=[("pb", b)], out=pbt[:, k, 0:rows], in_=at[0:rows, k * 128:(k + 1) * 128],
                    identity=self.ident_bf[0:rows, 0:rows])
        self.act("activation", W=[("pb", b), "mixT"], out=mixT[:, 0:4, q0:q0 + rows], in_=pbt[:, 0:4, 0:rows], func=AF.Copy)

    def rwkv_elem(self, tl, sample=False):
        B, pb, cols, col = self.B, self.pb, self.cols, self.col
        Tt = tl["T"] if not sample else self.NSAMP
        pm = B["pm"] if not sample else self.SB["pm_s"]
        C0 = float(np.exp(-0.5))
        tw, xar, sgg = B["tw"], B["xar"], B["sgg"]
        kpm = lambda i: ("pm", i)
        sgs = B["sg"]
        self.act("activation", R=[kpm(12)], W=["tw"], out=r32(tw[0:64, 0:Tt]), in_=pm[0:64, 12, 0:Tt], func=AF.Exp, scale=-2.0)
        self.act("activation", R=[kpm(13)], W=["sgg"], out=r32(sgg[:, 0:Tt]), in_=pm[:, 13, 0:Tt], func=AF.Exp, scale=-1.0)
        self.act("activation", R=["tw"], W=["tw"], out=r32(tw[0:64, 0:Tt]), in_=tw[0:64, 0:Tt], func=AF.Ln, bias=1.0)
        self.act("activation", R=["sgg"], W=["sgg"], out=r32(sgg[:, 0:Tt]), in_=sgg[:, 0:Tt], func=AF.Ln, bias=1.0)
        self.act("activation", R=["tw"], W=["tw"], out=r32(tw[0:64, 0:Tt]), in_=tw[0:64, 0:Tt], func=AF.Exp, scale=-1.0)
        self.act("activation", R=["sgg"], W=["sgg"], out=r32(sgg[:, 0:Tt]), in_=sgg[:, 0:Tt], func=AF.Exp, scale=-1.0)
        self.act("activation", R=[kpm(12)], W=["tw"], out=r32(xar[64:128, 0:Tt]), in_=pm[64:128, 12, 0:Tt], func=AF.Copy)
        self.dve("tensor_scalar", R=["tw"], W=["tw"], out=r32(tw[0:64, 0:Tt]), in0=tw[0:64, 0:Tt], scalar1=2.0, scalar2=-1.0,
                 op0=ALU.mult, op1=ALU.add)
        yield
        nch = Tt // 64
        v3 = lambda ap: ap.rearrange("p (c t) -> p c t", t=64)
        for j in range(4):
            xr, xk, xv = pm[:, j, 0:Tt], pm[:, 4 + j, 0:Tt], pm[:, 8 + j, 0:Tt]
            cj = lambda nm: cols[:, col[nm] + j:col[nm] + j + 1]
            js = slice(j * 128, (j + 1) * 128)
            sg, aa, Lc, eL, eN, rn, kkn, fac, kh, t2, rk = (B[k][:, 0:Tt] for k in ("sg", "aa", "Lc", "eL", "eN", "rn", "kkn", "fac", "kh", "t2", "rk"))
            b1 = self.bank(2, 4, "tr")
            self.pe("matmul", R=["tw", "wl"], W=[("pb", b1)], out=pb[b1][:, 0:Tt], lhsT=r32(self.wl[0:64, js]), rhs=r32(tw[0:64, 0:Tt]),
                    start=True, stop=True)
            b2 = self.bank(2, 4, "tr")
            self.pe("matmul", R=["tw", "wl"], W=[("pb", b2)], out=pb[b2][:, 0:Tt], lhsT=r32(self.wl[64:128, js]), rhs=r32(xar[64:128, 0:Tt]),
                    start=True, stop=True)
            self.act("activation", R=["cols2"], W=[("pb", b1), "sg"], out=sg, in_=pb[b1][:, 0:Tt], func=AF.Exp, scale=-1.0, bias=cj("nw0"))
            self.act("activation", R=["cols2"], W=[("pb", b2), "aa"], out=aa, in_=pb[b2][:, 0:Tt], func=AF.Exp, scale=-1.0, bias=cj("na0"))
            self.act("activation", R=["sg"], W=["sg"], out=sg, in_=sg, func=AF.Ln, bias=1.0)
            self.act("activation", R=["aa"], W=["aa"], out=aa, in_=aa, func=AF.Ln, bias=1.0)
            self.act("activation", R=["sg"], W=["sg"], out=sg, in_=sg, func=AF.Exp, scale=-1.0)
            self.act("activation", R=["aa"], W=["aa"], out=aa, in_=aa, func=AF.Exp, scale=-1.0)
            yield
            self.act("activation", R=[kpm(4 + j), "cols"], W=["rk"], out=r32(rk), in_=xk, func=AF.Square, scale=cj("kk"))
            b = self.bank(2, 4, "tr")
            self.pe("matmul", R=["rk", "onesblk"], W=[("pb", b)], out=pb[b][:, 0:Tt], lhsT=r32(self.onesblk[:]), rhs=r32(rk),
                    start=True, stop=True)
            self.act("activation", W=[("pb", b), "rn"], out=rn, in_=pb[b][:, 0:Tt], func=AF.Ln, bias=1e-24)
            self.act("activation", R=["rn"], W=["rn"], out=rn, in_=rn, func=AF.Exp, scale=-0.5)
            self.dve("scalar_tensor_tensor", R=[kpm(4 + j), "cols", "rn"], W=["kkn"], out=kkn, in0=xk, scalar=cj("kk"), in1=rn,
                     op0=ALU.mult, op1=ALU.mult)
            self.dve("tensor_scalar", R=["aa", "cols", "cols2"], W=["fac"], out=fac, in0=aa, scalar1=cj("ka"), scalar2=cj("omka"),
                     op0=ALU.mult, op1=ALU.add)
            self.dve("tensor_tensor", R=[kpm(4 + j), "fac"], W=["kh"], out=kh, in0=xk, in1=fac, op=ALU.mult)
            self.pool("tensor_tensor", R=["kkn", "aa"], W=["t2"], out=t2, in0=kkn, in1=aa, op=ALU.mult)
            yield
            self.dve("scalar_tensor_tensor", R=[kpm(j), "cols", "kh"], W=["rk"], out=r32(rk), in0=xr, scalar=cj("rk"), in1=kh,
                     op0=ALU.mult, op1=ALU.mult)
            b = self.bank(2, 4, "tr")
            self.pe("matmul", R=["rk", "onesblk"], W=[("pb", b)], out=pb[b][:, 0:Tt], lhsT=r32(self.onesblk[:]), rhs=r32(rk),
                    start=True, stop=True)
            bon = B["bonus"][:, j, 0:Tt] if not sample else self.SB["bonus_s"][:, j, 0:Tt]
            self.dve("tensor_tensor", R=[kpm(8 + j)], W=[("pb", b), "bonus"], out=bon, in0=pb[b][:, 0:Tt], in1=xv, op=ALU.mult)
            yield
            if sample:
                q6 = self.SB["q6"]
                self.act("activation", R=["sg"], W=[("q6", j)], out=q6[:, 1, j, 0:Tt], in_=sg, func=AF.Exp, scale=-C0)
                self.act("activation", R=[kpm(j)], W=[("q6", j)], out=q6[:, 0, j, 0:Tt], in_=xr, func=AF.Copy)
                self.act("activation", R=["kh"], W=[("q6", j)], out=q6[:, 2, j, 0:Tt], in_=kh, func=AF.Copy)
                self.act("activation", R=[kpm(8 + j)], W=[("q6", j)], out=q6[:, 3, j, 0:Tt], in_=xv, func=AF.Copy)
                self.act("activation", R=["kkn"], W=[("q6", j)], out=q6[:, 4, j, 0:Tt], in_=kkn, func=AF.Copy, scale=-1.0)
                self.act("activation", R=["t2"], W=[("q6", j)], out=q6[:, 5, j, 0:Tt], in_=t2, func=AF.Copy)
                continue
            self.dve("tensor_tensor_scan", R=["sg", "scanmask"], W=["Lc"], out=Lc, data0=self.scanmask[:, 0:Tt], data1=sg, initial=0.0,
                     op0=ALU.mult, op1=ALU.add)
            self.act("activation", R=["Lc"], W=["eL"], out=eL, in_=Lc, func=AF.Exp, scale=-C0)
            self.act("activation", R=["Lc"], W=["eN"], out=eN, in_=Lc, func=AF.Exp, scale=C0)
            AR, BK = B["AR"], B["BK"]
            kAR, kBK = ("AR", j), ("BK", j)
            self.dve("tensor_tensor", R=[kpm(j), "eL"], W=[kAR], out=r32(AR[:, j, :, 1, :]), in0=v3(xr), in1=v3(eL), op=ALU.mult)
            self.dve("scalar_tensor_tensor", R=["kkn", "eL"], W=[kAR], out=r32(AR[:, j, :, 0, 1:64]), in0=v3(kkn)[:, :, 1:64], scalar=-1.0,
                     in1=v3(eL)[:, :, 0:63], op0=ALU.mult, op1=ALU.mult)
            self.dve("tensor_scalar", R=["kkn"], W=[kAR], out=r32(AR[:, j, :, 0, 0:1]), in0=v3(kkn)[:, :, 0:1], scalar1=-1.0, scalar2=None,
                     op0=ALU.mult)
            self.dve("tensor_tensor", R=["t2", "eN"], W=[kBK], out=r32(BK[:, j, :, 0, :]), in0=v3(t2), in1=v3(eN), op=ALU.mult)
            self.dve("tensor_tensor", R=["kh", "eN"], W=[kBK], out=r32(BK[:, j, :, 1, :]), in0=v3(kh), in1=v3(eN), op=ALU.mult)
            self.pool("tensor_copy", R=["eL"], W=["eLC"], out=B["eLC"][:, j, 0:nch], in_=v3(eL)[:, :, 63])
            yield

    def rwkv_chunks(self, tl):
        B, pb = self.B, self.pb
        Tt = tl["T"]
        nch = Tt // 64
        AR, BK, pm = B["AR"], B["BK"], B["pm"]
        idf = self.ident_f
        if tl["first"]:
            self.hs = 0
            self.dve("tensor_scalar", R=["scanmask"], W=[("HS", 0, 0), ("HS", 0, 1)], out=r32(B["HS"][0][:].rearrange("p j v -> p (j v)")),
                     in0=self.scanmask[:, 0:256], scalar1=0.0, scalar2=None, op0=ALU.mult)
        self.hs0 = self.hs
        self.hs = (self.hs + nch) % 2
        hd = lambda h: (h // 2, slice((h % 2) * 64, (h % 2) * 64 + 64))
        fl = lambda ap: ap.rearrange("p a t -> p (a t)")
        self._Pm = {}

        def chunk_pre(c):
            cs = c % 2
            P_ = slice(cs * 64, cs * 64 + 64)
            Z, UV, BKt = B["Z"][cs], B["UV"][cs], B["BKtok"][cs]
            kZ, kUV, kBKt = ("Z", cs), ("UV", cs), ("BKtok", cs)
            keyc = lambda nm: (nm, cs)
            bs = [self.bank(0, 6, "ch"), self.bank(0, 6, "ch")]
            for hh in range(4):
                for g in range(2):
                    h = g + 2 * hh
                    j, ps_ = hd(h)
                    self.pe("matmul", R=[("AR", j), ("BK", j)], W=[("pb", bs[g])], out=pb[bs[g]][:, hh * 128:(hh + 1) * 128],
                            lhsT=r32(fl(BK[ps_, j, c, :, :])), rhs=r32(fl(AR[ps_, j, c, :, :])), start=True, stop=True)
            for g in range(2):
                pv = pb[bs[g]][:, :].rearrange("p (h t) -> p h t", h=4)
                self.dve("tensor_tensor", R=["maskZ"], W=[("pb", bs[g]), kZ], out=r32(Z[:, g::2, :]), in0=pv,
                         in1=self.maskZ[:].unsqueeze(1).to_broadcast([128, 4, 128]), op=ALU.mult)
            yield
            G0, Sa, Sb, Ga, Gb, GTa, GTb, Xsb = (B[k] for k in ("G0", "Sa", "Sb", "Ga", "Gb", "GTa", "GTb", "Xsb"))
            self.dve("tensor_copy", R=[kZ], W=[keyc("G0")], out=r32(G0[P_, :, :]), in_=Z[0:64, :, 0:64])
            self.dve("tensor_tensor", R=[kZ, "ident_f"], W=[keyc("Sa")], out=r32(Sa[P_, :, :]), in0=Z[0:64, :, 0:64],
                     in1=idf[0:64, 0:64].unsqueeze(1).to_broadcast([64, 8, 64]), op=ALU.add)
            yield
            bs = [self.bank(0, 6, "ch"), self.bank(0, 6, "ch")]
            for hh in range(4):
                for g in range(2):
                    h = g + 2 * hh
                    j, ps_ = hd(h)
                    self.pe("matmul", R=[("AR", j), ("BK", j)], W=[("pb", bs[g])], out=pb[bs[g]][0:64, hh * 64:(hh + 1) * 64],
                            lhsT=r32(AR[ps_, j, c, 0, :]), rhs=r32(BK[ps_, j, c, 0, :]), start=True, stop=True)
            for g in range(2):
                self.dve("tensor_tensor", R=["maskGT"], W=[("pb", bs[g]), keyc("GTa")], out=r32(GTa[P_, g::2, :]),
                         in0=pb[bs[g]][0:64, 0:256].rearrange("p (h t) -> p h t", h=4),
                         in1=self.maskGT[:].unsqueeze(1).to_broadcast([64, 4, 64]), op=ALU.mult)
            yield
            Gp, GTp, Sp = ("G0", G0), ("GTa", GTa), ("Sa", Sa)
            Gn, GTn, Sn = ("Ga", Ga), ("GTb", GTb), ("Sb", Sb)
            for k in range(1, 6):
                def mm(L, R_, box):
                    bb = self.bank(0, 6, "ch")
                    box[0] = bb
                    for h in range(8):
                        self.pe("matmul", R=[keyc(L[0]), keyc(R_[0])], W=[("pb", bb)], out=pb[bb][0:64, h * 64:(h + 1) * 64],
                                lhsT=r32(L[1][P_, h, :]), rhs=r32(R_[1][P_, h, :]), start=True, stop=True)
                        yield
                box = [0]
                if k <= 4:
                    yield from mm(GTp, Gp, box)
                    bb = box[0]
                    self.act("activation", W=[("pb", bb), keyc(Gn[0])], out=r32(Gn[1][P_, :, :]),
                             in_=pb[bb][0:64, :].rearrange("p (h t) -> p h t", h=8), func=AF.Copy)
                yield from mm(Gp, GTp, box)
                bb = box[0]
                self.act("activation", W=[("pb", bb), keyc(GTn[0])], out=r32(GTn[1][P_, :, :]),
                         in_=pb[bb][0:64, :].rearrange("p (h t) -> p h t", h=8), func=AF.Copy)
                yield from mm(GTn, Sp, box)
                bb = box[0]
                if cs == 0:
                    self.dve("tensor_tensor", R=[keyc(Sp[0])], W=[("pb", bb), keyc(Sn[0])], out=r32(Sn[1][P_, :, :]),
                             in0=pb[bb][0:64, :].rearrange("p (h t) -> p h t", h=8), in1=Sp[1][P_, :, :], op=ALU.add)
                else:
                    self.act("activation", W=[("pb", bb), keyc(Sn[0])], out=r32(Sn[1][P_, :, :]),
                             in_=pb[bb][0:64, :].rearrange("p (h t) -> p h t", h=8), func=AF.Copy)
                    self.dve("tensor_tensor", R=[keyc(Sp[0])], W=[keyc(Sn[0])], out=r32(Sn[1][P_, :, :]), in0=Sn[1][P_, :, :],
                             in1=Sp[1][P_, :, :], op=ALU.add)
                yield
                Gp, Gn = Gn, (("Gb", Gb) if Gn[0] == "Ga" else ("Ga", Ga))
                GTp, GTn = GTn, (("GTa", GTa) if GTn[0] == "GTb" else ("GTb", GTb))
                Sp, Sn = Sn, (("Sa", Sa) if Sn[0] == "Sb" else ("Sb", Sb))
            Pm = Sp
            self._Pm[c] = Pm
            yield
            bs = [self.bank(0, 6, "ch"), self.bank(0, 6, "ch")]
            for hh in range(4):
                for g in range(2):
                    h = g + 2 * hh
                    j, ps_ = hd(h)
                    self.pe("transpose", R=[("BK", j), "ident_f"], W=[("pb", bs[g])], out=pb[bs[g]][:, hh * 64:(hh + 1) * 64],
                            in_=fl(BK[ps_, j, c, :, :]), identity=idf[ps_, ps_])
            for g in range(2):
                self.act("activation", W=[("pb", bs[g]), kBKt], out=r32(BKt[:, g::2, :]),
                         in_=pb[bs[g]][:, 0:256].rearrange("p (h t) -> p h t", h=4), func=AF.Copy)
            yield
            bs = [self.bank(0, 6, "ch"), self.bank(0, 6, "ch")]
            for hh in range(4):
                for g in range(2):
                    h = g + 2 * hh
                    j, ps_ = hd(h)
                    self.pe("transpose", R=[("pm", 8 + j), "ident_f"], W=[("pb", bs[g])], out=pb[bs[g]][0:64, hh * 64:(hh + 1) * 64],
                            in_=pm[ps_, 8 + j, c * 64:(c + 1) * 64], identity=idf[ps_, ps_])
            for g in range(2):
                self.act("activation", W=[("pb", bs[g]), ("UV", cs, g)], out=r32(UV[64:128, g::2, :]),
                         in_=pb[bs[g]][0:64, 0:256].rearrange("p (h t) -> p h t", h=4), func=AF.Copy)
            self.dve("tensor_scalar", R=["scanmask"], W=[("UV", cs, 0), ("UV", cs, 1)], out=r32(UV[0:64, :, :].rearrange("p (a h) v -> p a (h v)", a=2)),
                     in0=self.scanmask[0:64, 0:256].unsqueeze(1).to_broadcast([64, 2, 256]), scalar1=0.0,
                     scalar2=None, op0=ALU.mult)

        def chain_group(c, g):
            cs = c % 2
            P_ = slice(cs * 64, cs * 64 + 64)
            gs = slice(g * 64, g * 64 + 64)
            Z, UV, BKt = B["Z"][cs], B["UV"][cs], B["BKtok"][cs]
            kZ, kUV, kBKt = ("Z", cs), ("UV", cs, g), ("BKtok", cs)
            kX = ("Xsb", cs, g)
            Xsb = B["Xsb"]
            Pm = self._Pm[c]
            hi = (self.hs0 + c) % 2
            HSo, HSn = B["HS"][hi], B["HS"][1 - hi]
            kHo, kHn = ("HS", hi, g), ("HS", 1 - hi, g)
            v4 = lambda ap: ap.rearrange("p (h t) -> p h t", h=4)
            bx = self.bank(0, 6, "ch")
            for hh in range(4):
                h = g + 2 * hh
                self.pe("matmul", R=[("AR", hh), kHo], W=[("pb", bx)], out=pb[bx][0:64, hh * 64:(hh + 1) * 64], lhsT=r32(AR[gs, hh, c, 0, :]),
                        rhs=r32(HSo[gs, hh, :]), start=True, stop=False)
                self.pe("matmul", R=[kZ, kUV], W=[("pb", bx)], out=pb[bx][0:64, hh * 64:(hh + 1) * 64], lhsT=r32(Z[:, h, 0:64]),
                        rhs=r32(UV[:, h, :]), start=False, stop=True)
            self.act("activation", W=[("pb", bx), kX], out=r32(Xsb[P_, g::2, :]), in_=v4(pb[bx][0:64, 0:256]), func=AF.Copy)
            yield
            bu = self.bank(0, 6, "ch")
            for hh in range(4):
                h = g + 2 * hh
                self.pe("matmul", R=[(Pm[0], cs), kX], W=[("pb", bu)], out=pb[bu][0:64, hh * 64:(hh + 1) * 64],
                        lhsT=r32(Pm[1][P_, h, :]), rhs=r32(Xsb[P_, h, :]), start=True, stop=True)
            self.dve("tensor_copy", W=[("pb", bu), kUV], out=r32(UV[0:64, g::2, :]), in_=v4(pb[bu][0:64, 0:256]))
            yield
            bh = self.bank(0, 6, "ch")
            for hh in range(4):
                h = g + 2 * hh
                self.pe("matmul", R=[kBKt, kUV], W=[("pb", bh)], out=pb[bh][0:64, hh * 64:(hh + 1) * 64], lhsT=r32(BKt[:, h, :]),
                        rhs=r32(UV[:, h, :]), start=True, stop=True)
            by = self.bank(0, 6, "ch")
            for hh in range(4):
                h = g + 2 * hh
                self.pe("matmul", R=[("AR", hh), kHo], W=[("pb", by)], out=pb[by][0:64, hh * 64:(hh + 1) * 64], lhsT=r32(AR[gs, hh, c, 1, :]),
                        rhs=r32(HSo[gs, hh, :]), start=True, stop=False)
                self.pe("matmul", R=[kZ, kUV], W=[("pb", by)], out=pb[by][0:64, hh * 64:(hh + 1) * 64], lhsT=r32(Z[:, h, 64:128]),
                        rhs=r32(UV[:, h, :]), start=False, stop=True)
            Ht, Hd = B["Htmp"], B["Hd"]
            self.act("activation", W=[("pb", bh), ("Htmp", g)], out=Ht[gs, :, :], in_=v4(pb[bh][0:64, 0:256]), func=AF.Copy)
            self.act("activation", W=[("pb", by), ("ysb", c, g)], out=B["ysb"][:, c, g::2, :], in_=v4(pb[by][0:64, 0:256]), func=AF.Copy)
            self.pool("tensor_tensor", R=[("Htmp", g), kHo], W=[("Hd", g)], out=Hd[gs], in0=Ht[gs], in1=HSo[gs], op=ALU.add)
            self.dve("tensor_tensor", R=[("Hd", g), "eLC"], W=[kHn], out=r32(HSn[gs]), in0=Hd[gs],
                     in1=B["eLC"][gs, :, c:c + 1].to_broadcast([64, 4, 64]), op=ALU.mult)
            yield

        for c0 in range(0, nch, 2):
            gens = [chunk_pre(c) for c in range(c0, min(c0 + 2, nch))]
            alive = list(gens)
            while alive:
                for g_ in list(alive):
                    try:
                        next(g_)
                    except StopIteration:
                        alive.remove(g_)
            def par_chain(g):
                for c in range(c0, min(c0 + 2, nch)):
                    yield from chain_group(c, g)
            alive = [par_chain(0), par_chain(1)]
            while alive:
                for g_ in list(alive):
                    try:
                        next(g_)
                    except StopIteration:
                        alive.remove(g_)
        ysq = B["AR"][0:64].rearrange("p j c a t -> p (j c a t)").rearrange("p (c h v) -> p c h v", c=nch, h=8)
        self.gn_transpose(B["ysb"], ysq, nch, [("ysb", c, g) for c in range(nch) for g in range(2)], [("AR", j) for j in range(4)])
        if tl["last"]:
            HSf = B["HS"][self.hs]
            b = self.bank(0, 6, "ch")
            for j in range(4):
                self.pe("transpose", R=[("HS", self.hs, 0), ("HS", self.hs, 1), "ident_f"], W=[("pb", b)], out=pb[b][0:64, j * 128:(j + 1) * 128], in_=HSf[:, j, :],
                        identity=idf[:])
            so = B["stout"]
            self.act("activation", W=[("pb", b), ("Z", 0)], out=r32(so.rearrange("p h c -> p (h c)")), in_=pb[b][0:64, :], func=AF.Copy)
            q = tl["seq"]
            self.dma("pool", "nwp", self.dout["nwp"][q * 512:(q + 1) * 512, :].rearrange("(h v) c -> v h c", h=8), so, R=[("Z", 0)],
                     W=[("nwp", q)])

    def gn_transpose(self, ysb, ysq, nch, kys, kar):
        B, pb, idf = self.B, self.pb, self.ident_f
        gst = B["gst"]
        n8 = nch * 8
        yv = ysb[:, 0:nch, :, :].rearrange("p c h v -> p (c h) v")
        qv = ysq[:, 0:nch, :, :].rearrange("p c h v -> p (c h) v")
        self.dve("tensor_reduce", R=kys, W=["gst"], out=gst[:, 0:n8], in_=yv, axis=AX.X, op=ALU.add)
        self.act("activation", R=kys, W=kar, out=r32(qv), in_=yv, func=AF.Square)
        self.dve("tensor_reduce", R=kar, W=["gst"], out=gst[:, 32:32 + n8], in_=qv, axis=AX.X, op=ALU.add)
        self.dve("tensor_scalar", R=["gst"], W=["gst"], out=gst[:, 0:n8], in0=gst[:, 0:n8], scalar1=1.0 / 64, scalar2=None, op0=ALU.mult)
        self.dve("tensor_tensor", R=["gst"], W=["gst"], out=gst[:, 64:64 + n8], in0=gst[:, 0:n8], in1=gst[:, 0:n8], op=ALU.mult)
        self.dve("scalar_tensor_tensor", R=["gst"], W=["gst"], out=gst[:, 32:32 + n8], in0=gst[:, 32:32 + n8], scalar=1.0 / 64,
                 in1=gst[:, 64:64 + n8], op0=ALU.mult, op1=ALU.subtract)
        self.act("activation", R=["gst"], W=["gst"], out=gst[:, 32:32 + n8], in_=gst[:, 32:32 + n8], func=AF.Ln, bias=GN_EPS)
        self.act("activation", R=["gst"], W=["gst"], out=gst[:, 32:32 + n8], in_=gst[:, 32:32 + n8], func=AF.Exp, scale=-0.5)
        self.dve("tensor_tensor", R=["gst"] + kys, W=kys, out=yv, in0=yv, in1=gst[:, 0:n8].unsqueeze(2).to_broadcast([64, n8, 64]),
                 op=ALU.subtract)
        self.pool("tensor_tensor", R=["gst"] + kys, W=kys, out=yv, in0=yv, in1=gst[:, 32:32 + n8].unsqueeze(2).to_broadcast([64, n8, 64]),
                  op=ALU.mult)
        for c in range(nch):
            for j in range(4):
                bt = 6 + j // 2
                self.pe("transpose", R=kys + ["ident_f"], W=[("pb", bt)], out=pb[bt][:, (j % 2) * 256 + c * 64:(j % 2) * 256 + (c + 1) * 64],
                        in_=ysb[:, c, 2 * j:2 * j + 2, :].rearrange("p h v -> p (h v)"), identity=idf[0:64, 0:64])

    def rwkv_combine(self, tl, pm, mixT, Tt):
        B, pb, cols, col = self.B, self.pb, self.cols, self.col
        sgg = B["sgg"]
        for j in range(4):
            bt = 6 + j // 2
            cj = lambda nm: cols[:, col[nm] + j:col[nm] + j + 1]
            yg, yb = B["yg"][:, 0:Tt], B["yb"][:, 0:Tt]
            bon = B["bonus"][:, j, 0:Tt] if tl["kind"] == "p" else self.SB["bonus_s"][:, j, 0:Tt]
            self.dve("tensor_scalar", R=["cols"], W=[("pb", bt), "yg"], out=yg, in0=pb[bt][:, (j % 2) * 256:(j % 2) * 256 + Tt],
                     scalar1=cj("lng"), scalar2=cj("lnb"), op0=ALU.mult, op1=ALU.add)
            self.pool("tensor_tensor", R=["yg", "bonus"], W=["yb"], out=yb, in0=yg, in1=bon, op=ALU.add)
            b = self.bank(2, 4, "tr")
            self.pe("matmul", R=["sgg", "gup"], W=[("pb", b)], out=pb[b][:, 0:Tt], lhsT=r32(self.gup[:, j * 128:(j + 1) * 128]),
                    rhs=r32(sgg[:, 0:Tt]), start=True, stop=True)
            self.dve("tensor_tensor", R=["yb"], W=[("pb", b), "mixT"], out=mixT[:, 4 + j, 0:Tt], in0=pb[b][:, 0:Tt], in1=yb, op=ALU.mult)

    def tiles_b(self):
        tl = []
        for q in range(self.NSEQ):
            for t0 in range(0, self.S, self.T):
                tl.append(dict(kind="p", tok0=q * self.S + t0, T=self.T, nsub=self.T // 128, rows=128))
        tl.append(dict(kind="s", tok0=self.NSEQ * self.S, T=self.NSAMP, nsub=1, rows=self.NSAMP))
        return tl

    def phase_b(self):
        ar = self.ar
        ar.reset(self.shared_base)
        A = ar.alloc
        T = self.T
        w_up_sb = A("w_up_sb", [128, 8, DFF], BF16)
        w_dn_sb = A("w_dn_sb", [128, 32, D], BF16)
        if self.debug == "phaseB":
            self.load_weight_bf16(w_up_sb, self.din["w_up"], 8, DFF, "up", "w_up")
            self.load_weight_bf16(w_dn_sb, self.din["w_dn"], 32, D, "dn", "w_dn")
        else:
            vu = self.w_up_bf.rearrange("(kt p) n -> p kt n", p=128)
            for kt in range(8):
                self.dma("sp" if kt % 2 == 0 else "act", "w_up", w_up_sb[:, kt, :], vu[:, kt, :], R=["w_up_bf"], W=["w_up"])
            vd = self.w_dn_bf.rearrange("(kt p) n -> p kt n", p=128)
            for k4 in range(0, 32, 4):
                self.dma("sp" if (k4 // 4) % 2 == 0 else "act", "w_dn", w_dn_sb[:, k4:k4 + 4, :], vd[:, k4:k4 + 4, :], R=["w_dn_bf"], W=["w_dn"])
        mixt = [A(f"mixt{i}", [128, 8, T], BF16) for i in range(2)]
        xt = [A(f"xtb{i}", [128, T // 128, D], F32) for i in range(2)]
        h2 = [A(f"h2_{i}", [128, D], BF16) for i in range(2)]
        h2T = A("h2T", [128, 8, T], BF16)
        actf = [A(f"actf{i}", [128, T], F32) for i in range(4)]
        actT = A("actT", [128, 32, T], BF16)
        st = A("statb", [128, 8], F32)
        cols, cg2 = self.cols, self.col["g2"]
        pb = self.pb
        w_out_sb = self.w_out_sb
        nb = [0]

        def bank(lo, hi):
            b = lo + nb[0] % (hi - lo)
            nb[0] += 1
            return b

        tiles = self.tiles_b()

        def stage_a(it):
            tl = tiles[it]
            sl = it % 2
            Tt, nsub, rows, tok0 = tl["T"], tl["nsub"], tl["rows"], tl["tok0"]
            mt, x_ = mixt[sl], xt[sl]
            self.dma("sp", f"mixb{sl}", mt[:, :, 0:Tt], self.mix_scr[:, :, tok0:tok0 + Tt].rearrange("k p t -> p k t"),
                     R=[("mixscr", tok0)], W=[("mixt", sl)])
            if tl["kind"] == "p":
                xsrc = self.din["xp"][tok0:tok0 + Tt, :].rearrange("(s p) d -> p s d", p=128)
            else:
                xsrc = self.din["xs"][0:rows, :].rearrange("(s p) d -> p s d", s=1)
            self.dma("sp", f"xb{sl}", x_[0:rows, 0:nsub, :], xsrc, W=[("xtb", sl)])
            for s_ in range(nsub):
                c0 = s_ * 128
                for half in range(2):
                    b = bank(0, 2)
                    hs = slice(half * 512, (half + 1) * 512)
                    for kt in range(8):
                        self.pe("matmul", R=[("mixt", sl), "w_out"], W=[("pb", b)], out=pb[b][0:rows, :],
                                lhsT=mt[:, kt, c0:c0 + rows], rhs=w_out_sb[:, kt, hs], start=(kt == 0), stop=(kt == 7))
                    self.dve("tensor_tensor", R=[("xtb", sl)], W=[("pb", b), ("xtb", sl)], out=x_[0:rows, s_, hs],
                             in0=pb[b][0:rows, :], in1=x_[0:rows, s_, hs], op=ALU.add)
                hh = h2[s_ % 2]
                kh = ("h2", s_ % 2)
                sc = st[:, 4 * ((2 * it + s_) % 2):4 * ((2 * it + s_) % 2) + 4]
                self.act("activation", R=[("xtb", sl)], W=[kh, "statb"], out=hh[0:rows, :], in_=x_[0:rows, s_, :],
                         func=AF.Square, accum_out=sc[0:rows, 0:1])
                self.act("activation", R=["statb"], W=["statb"], out=sc[0:rows, 1:2], in_=sc[0:rows, 0:1], func=AF.Ln,
                         bias=RMS_EPS, scale=1.0 / D)
                self.act("activation", R=["statb"], W=["statb"], out=sc[0:rows, 2:3], in_=sc[0:rows, 1:2], func=AF.Exp, scale=-0.5)
                self.act("activation", R=[("xtb", sl), "statb"], W=[kh], out=hh[0:rows, :], in_=x_[0:rows, s_, :],
                         func=AF.Copy, scale=sc[0:rows, 2:3])

        def stage_t(it):
            tl = tiles[it]
            nsub, rows = tl["nsub"], tl["rows"]
            for s_ in range(nsub):
                c0 = s_ * 128
                hh = h2[s_ % 2]
                kh = ("h2", s_ % 2)
                b = bank(6, 8)
                pbt = pb[b][:].bitcast(BF16).rearrange("p (k t) -> p k t", k=8)
                for kt in range(8):
                    self.pe("transpose", R=[kh, "ident_bf"], W=[("pb", b)], out=pbt[:, kt, 0:rows],
                            in_=hh[0:rows, kt * 128:(kt + 1) * 128], identity=self.ident_bf[0:rows, 0:rows])
                self.dve("tensor_tensor", R=["cols"], W=[("pb", b), "h2T"], out=h2T[:, :, c0:c0 + rows], in0=pbt[:, :, 0:rows],
                         in1=cols[:, cg2:cg2 + 8].unsqueeze(2).to_broadcast([128, 8, rows]), op=ALU.mult)

        def stage_u(it):
            Tt = tiles[it]["T"]
            for hp in range(16):
                b = 2 + hp % 4
                for hf in range(2):
                    ht = 2 * hp + hf
                    for kt in range(8):
                        self.pe("matmul", R=["w_up", "h2T"], W=[("pb", b)], out=pb[b][:, hf * 256:hf * 256 + Tt],
                                lhsT=w_up_sb[:, kt, ht * 128:(ht + 1) * 128], rhs=h2T[:, kt, 0:Tt], start=(kt == 0), stop=(kt == 7))
                for hf in range(2):
                    ht = 2 * hp + hf
                    af = actf[ht % 4]
                    self.act("activation", W=[("pb", b), ("actf", ht % 4)], out=af[:, 0:Tt], in_=pb[b][:, hf * 256:hf * 256 + Tt],
                             func=AF.Relu)
                    self.pool("tensor_tensor", R=[("actf", ht % 4)], W=[("actT", ht)], out=actT[:, ht, 0:Tt], in0=af[:, 0:Tt],
                              in1=af[:, 0:Tt], op=ALU.mult)

        def stage_d(it):
            tl = tiles[it]
            sl = it % 2
            Tt, nsub, rows, tok0 = tl["T"], tl["nsub"], tl["rows"], tl["tok0"]
            x_ = xt[sl]
            if tl["kind"] == "p":
                ydst = self.dout["yp"][tok0:tok0 + Tt, :].rearrange("(s p) d -> p s d", p=128)
            else:
                ydst = self.dout["ys"][0:rows, :].rearrange("(s p) d -> p s d", s=1)
            for s_ in range(nsub):
                c0 = s_ * 128
                for half in range(2):
                    b = bank(0, 2)
                    hs = slice(half * 512, (half + 1) * 512)
                    for ht in range(32):
                        self.pe("matmul", R=[("actT", ht), "w_dn"], W=[("pb", b)], out=pb[b][0:rows, :],
                                lhsT=actT[:, ht, c0:c0 + rows], rhs=w_dn_sb[:, ht, hs], start=(ht == 0), stop=(ht == 31))
                    self.dve("tensor_tensor", R=[("xtb", sl)], W=[("pb", b), ("xtb", sl)], out=x_[0:rows, s_, hs],
                             in0=pb[b][0:rows, :], in1=x_[0:rows, s_, hs], op=ALU.add)
            self.dma("pool", f"yb{sl}", ydst, x_[0:rows, 0:nsub, :], R=[("xtb", sl)], W=[("yout", it)])

        n = len(tiles)
        stage_a(0)
        stage_t(0)
        for it in range(n):
            stage_u(it)
            if it + 1 < n:
                stage_a(it + 1)
            stage_d(it)
            if it + 1 < n:
                stage_t(it + 1)

    def build(self):
        from contextlib import ExitStack
        self.declare_dram()
        self.setup()
        self.load_small()
        self.load_weight_bf16(self.w_out_sb, self.din["w_out"], 8, D, "out", "w_out")
        if self.debug != "phaseB":
            self.phase_a()
            self.s.barrier()
        self.phase_b()
        with ExitStack() as es:
            self.s.emit(es)
        return self.nc


def _perm_w_in():
    q = []
    for j in range(4):
        q += list(range(j * 64, (j + 1) * 64)) + list(range((j + 4) * 64, (j + 5) * 64))
    k = list(range(512, 640))
    v = list(range(640, 768))
    o = 768
    r = list(range(o, o + 512))
    wd = list(range(o + 512, o + 576))
    kr = list(range(o + 576, o + 1088))
    vr = list(range(o + 1088, o + 1600))
    ad = list(range(o + 1600, o + 1664))
    gd = list(range(o + 1664, o + 1792))
    return np.array(q + k + v + r + kr + vr + wd + ad + gd)


def prep_core_inputs(inp, core, NSEQ, NB):
    f = lambda a: np.ascontiguousarray(np.asarray(a, dtype=np.float32))
    perm = _perm_w_in()
    S = inp["x_prompt"].shape[1]
    xp = f(inp["x_prompt"][core * NSEQ:(core + 1) * NSEQ]).reshape(NSEQ * S, D)
    bs = slice(core * NB, (core + 1) * NB)
    xs = np.concatenate([f(inp["x_sample"][bs]).reshape(NB * 4, D), f(inp["state_shift"][0][bs])], 0)
    d = dict(
        xp=xp, xs=f(xs),
        ck=f(inp["cache_k"][0][bs]).reshape(NB, 128, 128), cv=f(inp["cache_v"][0][bs]).reshape(NB, 128, 128),
        swkv=f(inp["state_wkv"][0][bs]).reshape(NB * 8, 4096),
        w_in=f(np.asarray(inp["w_in"][0])[:, perm]), mu=f(np.asarray(inp["rwkv_mu"][0])[perm[768:] - 768]),
        w_out=f(inp["w_out"][0]), w_up=f(inp["w_ff_up"][0]), w_dn=f(inp["w_ff_down"][0]),
        g1=f(inp["norm1_g"][0]), g2=f(inp["norm2_g"][0]), qg=f(inp["q_norm_g"][0]), kg=f(inp["k_norm_g"][0]),
        sinks=f(inp["attn_sinks"][0]), w0=f(inp["w_decay_0"][0]), wdu=f(inp["w_decay_up"][0]), a0=f(inp["a_0"][0]),
        aup=f(inp["a_up"][0]), gup=f(inp["g_up"][0]), kk=f(inp["k_k"][0]), ka=f(inp["k_a"][0]),
        rk=f(np.asarray(inp["r_k"][0]).reshape(-1)), lng=f(inp["ln_x_g"][0]), lnb=f(inp["ln_x_b"][0]),
    )
    return d


_PROG = {}


def kernel(**inputs):
    NC = 8
    Bp, S = inputs["x_prompt"].shape[0], inputs["x_prompt"].shape[1]
    Bs = inputs["x_sample"].shape[0]
    NSEQ, NB = Bp // NC, Bs // NC
    key = (NSEQ, S, NB)
    if key not in _PROG:
        _PROG[key] = K(NSEQ, S, NB).build()
    nc = _PROG[key]
    in_maps = [prep_core_inputs(inputs, c, NSEQ, NB) for c in range(NC)]
    res = run_bass_kernel_spmd(nc, in_maps, core_ids=list(range(NC)))
    R = res.results
    cat = lambda name: np.concatenate([np.asarray(r[name]) for r in R], 0)
    yp = cat("yp").reshape(Bp, S, D)
    ys = cat("ys").reshape(Bs, 4, D)
    nkp = cat("nkp").reshape(1, Bp, 128, 2, 64)
    nvp = cat("nvp").reshape(1, Bp, 128, 2, 64)
    nwp = cat("nwp").reshape(1, Bp, 8, 64, 64)
    nsp = cat("nsp").reshape(1, Bp, D)
    nks = cat("nks").reshape(1, Bs, 128, 2, 64)
    nvs = cat("nvs").reshape(1, Bs, 128, 2, 64)
    nws = cat("nws").reshape(1, Bs, 8, 64, 64)
    nss = cat("nss").reshape(1, Bs, D)
    return tuple(np.ascontiguousarray(a, dtype=np.float32) for a in (yp, ys, nkp, nvp, nwp, nsp, nks, nvs, nws, nss))
```
